# Optimizing a Trainium2 kernel written in Bass

```python
import math
import jax, jax.numpy as jnp
from jax import lax
import numpy as np

D_MODEL = 1024
BATCH = 1
SEQ = 16384
DEPTH = 4

N_META = 16
RMS_EPS = 1e-6
D_FF = 4 * D_MODEL
N_EVEN = (DEPTH + 1) // 2
N_ODD = DEPTH // 2
RWKV_HEAD = 64
RWKV_WIDTH = D_MODEL // 2
RWKV_HEADS = RWKV_WIDTH // RWKV_HEAD
DECAY_RANK = 64
ICLR_RANK = 64
GATE_RANK = 128
GN_EPS = RWKV_HEAD * 1e-5
POOL_WIDTH = D_MODEL - RWKV_WIDTH
POOL_WINDOWS = (2, 4, 8, 16)
POOL_GROUPS = len(POOL_WINDOWS)
POOL_GROUP_W = POOL_WIDTH // POOL_GROUPS
SHIFT_WIDTH = 3 * RWKV_WIDTH + DECAY_RANK + ICLR_RANK + GATE_RANK
EVEN_IN = SHIFT_WIDTH + POOL_WIDTH
RWKV_SPLITS = (RWKV_WIDTH, 2 * RWKV_WIDTH, 3 * RWKV_WIDTH,
               3 * RWKV_WIDTH + DECAY_RANK, 3 * RWKV_WIDTH + DECAY_RANK + ICLR_RANK)
DIFF_HEADS = 8
DIFF_HEAD = D_MODEL // (2 * DIFF_HEADS)
DIFF_IN = 3 * D_MODEL
SUBLN_EPS = 1e-5
ROPE_THETA = 10000.0
Q_BLOCK = 128

kernel_name = 'hybrid_rwkv7_pool_diffattn'


def rmsnorm(t, g, eps=RMS_EPS):
    tf = t.astype(jnp.float32)
    tf = tf * lax.rsqrt(jnp.mean(tf * tf, axis=-1, keepdims=True) + eps)
    return (tf * g.astype(jnp.float32)).astype(t.dtype)


def rope_tables(length):
    pos = jnp.arange(length, dtype=jnp.float32)
    inv = ROPE_THETA ** (-jnp.arange(0, DIFF_HEAD, 2, dtype=jnp.float32) / DIFF_HEAD)
    ang = pos[:, None] * inv[None, :]
    ang = jnp.concatenate([ang, ang], axis=-1)
    return jnp.cos(ang), jnp.sin(ang)


def apply_rope(t, cos, sin):
    tf = t.astype(jnp.float32)
    t1, t2 = jnp.split(tf, 2, axis=-1)
    rot = jnp.concatenate([-t2, t1], axis=-1)
    c = cos[None, :, None, None, :]
    s = sin[None, :, None, None, :]
    return (tf * c + rot * s).astype(t.dtype)


def rwkv7_scan(r, decay, k, v, kk, a):
    B, L, H, N = r.shape

    def step(S, inp):
        r_t, w_t, k_t, v_t, kk_t, a_t = inp
        sa = jnp.einsum('bhvk,bhk->bhv', S, -kk_t)
        S = (S * w_t[:, :, None, :]
             + sa[..., None] * (kk_t * a_t)[:, :, None, :]
             + v_t[..., None] * k_t[:, :, None, :])
        o = jnp.einsum('bhvk,bhk->bhv', S, r_t)
        return S, o

    xs = tuple(jnp.moveaxis(t, 1, 0) for t in (r, decay, k, v, kk, a))
    S0 = jnp.zeros((B, H, N, N), jnp.float32)
    _, o = lax.scan(step, S0, xs)
    return jnp.moveaxis(o, 0, 1)


def rwkv_pool_mixer(h, w_in, mu, w0, w_up, a0, a_up, g_up, k_k, k_a, r_k,
                    ln_w, ln_b, pool_w, pool_scale, w_out):
    B, L, _ = h.shape
    f32 = jnp.float32
    y = h @ w_in
    ys = y[..., :SHIFT_WIDTH]
    prev = jnp.pad(ys, ((0, 0), (1, 0), (0, 0)))[:, :L]
    ys = ys + (prev - ys) * mu
    r, k, v, wd, ad, gd = jnp.split(ys, RWKV_SPLITS, axis=-1)
    wlog = -jax.nn.softplus(-(w0 + jnp.tanh(wd) @ w_up).astype(f32)) - 0.5
    decay = jnp.exp(-jnp.exp(wlog))
    a = jax.nn.sigmoid((a0 + ad @ a_up).astype(f32))
    g = (jax.nn.sigmoid(gd) @ g_up).astype(f32)
    hs = lambda t: t.reshape(B, L, RWKV_HEADS, RWKV_HEAD)
    kf = k.astype(f32)
    kk = hs(kf * k_k)
    kk = kk / jnp.maximum(jnp.linalg.norm(kk, axis=-1, keepdims=True), 1e-12)
    kf = kf * (1.0 + (a - 1.0) * k_a)
    rf, kf, vf, a_h, dec = hs(r.astype(f32)), hs(kf), hs(v.astype(f32)), hs(a), hs(decay)
    o = rwkv7_scan(rf, dec, kf, vf, kk, a_h)
    mean = jnp.mean(o, axis=-1, keepdims=True)
    var = jnp.mean(jnp.square(o - mean), axis=-1, keepdims=True)
    o = ((o - mean) * lax.rsqrt(var + GN_EPS)).reshape(B, L, RWKV_WIDTH) * ln_w + ln_b
    bonus = jnp.sum(rf * kf * r_k, axis=-1, keepdims=True) * vf
    o = ((o + bonus.reshape(B, L, RWKV_WIDTH)) * g).astype(h.dtype)
    u = y[..., SHIFT_WIDTH:].reshape(B, L, POOL_GROUPS, POOL_GROUP_W).astype(f32)
    t_idx = jnp.arange(L)
    diffs = []
    for gi, win in enumerate(POOL_WINDOWS):
        ug = u[:, :, gi]
        c = jnp.cumsum(ug, axis=1)
        lag = jnp.pad(c, ((0, 0), (win, 0), (0, 0)))[:, :L]
        cnt = jnp.minimum(t_idx + 1, win).astype(f32)[None, :, None]
        diffs.append((c - lag) / cnt - ug)
    d = jnp.stack(diffs, axis=2)
    z = jnp.einsum('blgc,gcd->blgd', d, pool_w.astype(f32)).reshape(B, L, POOL_WIDTH)
    z = (z * pool_scale).astype(h.dtype)
    return jnp.concatenate([o, z], axis=-1) @ w_out


def causal_diff_attention(q, k, v, lam):
    B, L, H, _, dh = q.shape
    nb = -(-L // Q_BLOCK)
    Lp = nb * Q_BLOCK
    pad = Lp - L
    q = jnp.pad(q, ((0, 0), (0, pad), (0, 0), (0, 0), (0, 0)))
    k = jnp.pad(k, ((0, 0), (0, pad), (0, 0), (0, 0), (0, 0)))
    v = jnp.pad(v, ((0, 0), (0, pad), (0, 0), (0, 0)))
    qb = jnp.moveaxis(q.reshape(B, nb, Q_BLOCK, H, 2, dh), 1, 0)
    kpos = jnp.arange(Lp)
    scale = DIFF_HEAD ** -0.5

    def one_block(args):
        qblk, start = args
        s = jnp.einsum('bqhcd,bkhcd->bhcqk', qblk, k).astype(jnp.float32) * scale
        qpos = start + jnp.arange(Q_BLOCK)
        mask = kpos[None, :] <= qpos[:, None]
        s = jnp.where(mask, s, -jnp.inf)
        p = jax.nn.softmax(s, axis=-1)
        attn = p[:, :, 0] - lam * p[:, :, 1]
        return jnp.einsum('bhqk,bkhe->bqhe', attn.astype(v.dtype), v)

    out = lax.map(one_block, (qb, jnp.arange(nb) * Q_BLOCK))
    out = jnp.moveaxis(out, 0, 1).reshape(B, Lp, H, 2 * dh)
    return out[:, :L]


def diff_attn_mixer(h, w_in, lam_vecs, subln_w, w_out, cos, sin, layer):
    B, L, _ = h.shape
    y = h @ w_in
    q, k, v = jnp.split(y, 3, axis=-1)
    q = apply_rope(q.reshape(B, L, DIFF_HEADS, 2, DIFF_HEAD), cos, sin)
    k = apply_rope(k.reshape(B, L, DIFF_HEADS, 2, DIFF_HEAD), cos, sin)
    v = v.reshape(B, L, DIFF_HEADS, 2 * DIFF_HEAD)
    lam_init = 0.8 - 0.6 * math.exp(-0.3 * layer)
    lv = lam_vecs.astype(jnp.float32)
    lam = jnp.exp(jnp.sum(lv[0] * lv[1])) - jnp.exp(jnp.sum(lv[2] * lv[3])) + lam_init
    o = causal_diff_attention(q, k, v, lam)
    o = rmsnorm(o, subln_w, SUBLN_EPS) * (1.0 - lam_init)
    return o.reshape(B, L, D_MODEL).astype(h.dtype) @ w_out


def sq_relu_mlp(h, w1, w2):
    return jnp.square(jax.nn.relu(h @ w1)) @ w2


def setup_inputs(seed: int = 0) -> dict:
    key = jax.random.key(seed)
    ks = jax.random.split(key, 24)
    nrm = lambda i, shape, s: jax.random.normal(ks[i], shape, jnp.float32) * s
    RW = RWKV_WIDTH
    return {
        'x': nrm(0, (BATCH, SEQ, D_MODEL), 1.0),
        'meta': nrm(1, (N_META, D_MODEL), 1.0),
        'norm_g': 1.0 + nrm(2, (DEPTH, 4, D_MODEL), 0.1),
        'mlp_w1': nrm(3, (DEPTH, D_MODEL, D_FF), D_MODEL ** -0.5),
        'mlp_w2': nrm(4, (DEPTH, D_FF, D_MODEL), D_FF ** -0.5),
        'ev_w_in': nrm(5, (N_EVEN, D_MODEL, EVEN_IN), D_MODEL ** -0.5),
        'ev_mu': jax.random.uniform(ks[6], (N_EVEN, SHIFT_WIDTH), jnp.float32),
        'ev_w0': jax.random.uniform(ks[7], (N_EVEN, RW), jnp.float32, -4.0, 1.0),
        'ev_w_up': nrm(8, (N_EVEN, DECAY_RANK, RW), 0.5 * DECAY_RANK ** -0.5),
        'ev_a0': nrm(9, (N_EVEN, RW), 0.1),
        'ev_a_up': nrm(10, (N_EVEN, ICLR_RANK, RW), ICLR_RANK ** -0.5),
        'ev_g_up': nrm(11, (N_EVEN, GATE_RANK, RW), GATE_RANK ** -0.5),
        'ev_k_k': 0.85 + nrm(12, (N_EVEN, RW), 0.05),
        'ev_k_a': 1.0 + nrm(13, (N_EVEN, RW), 0.05),
        'ev_r_k': nrm(14, (N_EVEN, RWKV_HEADS, RWKV_HEAD), 0.1),
        'ev_ln_w': 1.0 + nrm(15, (N_EVEN, RW), 0.1),
        'ev_ln_b': nrm(16, (N_EVEN, RW), 0.02),
        'ev_pool_w': nrm(17, (N_EVEN, POOL_GROUPS, POOL_GROUP_W, POOL_GROUP_W), POOL_GROUP_W ** -0.5),
        'ev_pool_scale': 1.0 + nrm(18, (N_EVEN, POOL_WIDTH), 0.1),
        'ev_w_out': nrm(19, (N_EVEN, D_MODEL, D_MODEL), D_MODEL ** -0.5),
        'od_w_in': nrm(20, (N_ODD, D_MODEL, DIFF_IN), D_MODEL ** -0.5),
        'od_lambda': nrm(21, (N_ODD, 4, DIFF_HEAD), 0.1),
        'od_subln_w': 1.0 + nrm(22, (N_ODD, 2 * DIFF_HEAD), 0.1),
        'od_w_out': nrm(23, (N_ODD, D_MODEL, D_MODEL), D_MODEL ** -0.5),
    }


def reference(x, meta, norm_g, mlp_w1, mlp_w2, ev_w_in, ev_mu, ev_w0, ev_w_up, ev_a0,
              ev_a_up, ev_g_up, ev_k_k, ev_k_a, ev_r_k, ev_ln_w, ev_ln_b, ev_pool_w,
              ev_pool_scale, ev_w_out, od_w_in, od_lambda, od_subln_w, od_w_out):
    B = x.shape[0]
    h = jnp.concatenate([jnp.broadcast_to(meta[None].astype(x.dtype), (B, N_META, D_MODEL)), x], axis=1)
    L = h.shape[1]
    cos, sin = rope_tables(L)
    for i in range(DEPTH):
        g = norm_g[i]
        j = i // 2
        hn = rmsnorm(h, g[0])
        if i % 2 == 0:
            m = rwkv_pool_mixer(hn, ev_w_in[j], ev_mu[j], ev_w0[j], ev_w_up[j], ev_a0[j],
                                ev_a_up[j], ev_g_up[j], ev_k_k[j], ev_k_a[j], ev_r_k[j],
                                ev_ln_w[j], ev_ln_b[j], ev_pool_w[j], ev_pool_scale[j], ev_w_out[j])
        else:
            m = diff_attn_mixer(hn, od_w_in[j], od_lambda[j], od_subln_w[j], od_w_out[j], cos, sin, i)
        h = h + rmsnorm(m, g[1])
        f = sq_relu_mlp(rmsnorm(h, g[2]), mlp_w1[i], mlp_w2[i])
        h = h + rmsnorm(f, g[3])
    return h[:, N_META:]
```

```python
import math
from contextlib import ExitStack

import numpy as np
import concourse.bass as bass
import concourse.mybir as mybir
from concourse.bass_utils import run_bass_kernel_spmd

F32 = mybir.dt.float32
BF16 = mybir.dt.bfloat16
AF = mybir.ActivationFunctionType
ALU = mybir.AluOpType
AX = mybir.AxisListType

NCORES = 8
D = 1024
KC = 8
L = 16400
T = L // NCORES
TB = 410
NB = T // TB
DFF = 4096
RMS_EPS = 1e-6

N_DMA_SLOTS = 6


class Tile:
    __slots__ = ("t", "w", "r", "name")

    def __init__(self, t, name=""):
        self.t = t
        self.w = None
        self.r = {}
        self.name = name

    def __getitem__(self, idx):
        return self.t[idx]


class Builder:
    ENGS = ("pe", "dve", "act", "pool", "sp")

    def __init__(self, nc):
        self.nc = nc
        self.cnt = {e: 0 for e in self.ENGS}
        self.sem = {e: nc.alloc_semaphore(name=f"c_{e}") for e in self.ENGS}
        self.seen = {e: {f: 0 for f in self.ENGS} for e in self.ENGS}
        self.rec = {e: [] for e in self.ENGS}
        self.dslots = {}
        for e in ("sp", "act", "pool"):
            self.dslots[e] = [dict(sem=nc.alloc_semaphore(name=f"d_{e}{i}"), val=0)
                              for i in range(N_DMA_SLOTS)]
        self.dnext = {e: 0 for e in ("sp", "act", "pool")}
        self.dseen = {e: {} for e in self.ENGS}

    def _need(self, e, dep, waits):
        if dep is None:
            return
        if dep[0] == "c":
            _, f, n = dep
            if f == "pe" and e == "pe":
                return
            if self.seen[e][f] >= n:
                return
            self.seen[e][f] = n
            waits[("c", f)] = max(waits.get(("c", f), 0), n)
        else:
            _, f, s, v = dep
            key = (f, s)
            if self.dseen[e].get(key, 0) >= v:
                return
            self.dseen[e][key] = v
            waits[("d", f, s)] = max(waits.get(("d", f, s), 0), v)

    def _deps(self, e, reads, writes, waits):
        for t in reads:
            self._need(e, t.w, waits)
        for t in writes:
            self._need(e, t.w, waits)
            for d in t.r.values():
                self._need(e, d, waits)

    def op(self, e, fn, reads=(), writes=()):
        waits = {}
        self._deps(e, reads, writes, waits)
        self.cnt[e] += 1
        me = ("c", e, self.cnt[e])
        for t in reads:
            t.r[e] = me
        for t in writes:
            t.w = me
            t.r = {}
        self.rec[e].append((waits, fn, ("c", e)))
        return me

    def dma(self, e, out_ap, in_ap, reads=(), writes=(), **kw):
        waits = {}
        self._deps(e, reads, writes, waits)
        si = self.dnext[e]
        self.dnext[e] = (si + 1) % N_DMA_SLOTS
        slot = self.dslots[e][si]
        if slot["val"] > 0:
            self._need(e, ("d", e, si, slot["val"]), waits)
        slot["val"] += 16
        me = ("d", e, si, slot["val"])
        for t in reads:
            t.r[("dma", e, si)] = me
        for t in writes:
            t.w = me
            t.r = {}

        def fn(eng, out_ap=out_ap, in_ap=in_ap, kw=kw):
            return eng.dma_start(out=out_ap, in_=in_ap, **kw)
        self.rec[e].append((waits, fn, ("d", e, si)))
        return me

    def wait_all(self, e):
        waits = {}
        for f in self.ENGS:
            if f != e and self.cnt[f] > 0:
                self._need(e, ("c", f, self.cnt[f]), waits)
        for f in self.dslots:
            for si, s in enumerate(self.dslots[f]):
                if s["val"] > 0:
                    self._need(e, ("d", f, si, s["val"]), waits)
        self.rec[e].append((waits, None, None))

    def emit(self):
        nc = self.nc
        with nc.Block() as block:
            for e in self.ENGS:
                recs = self.rec[e]
                if not recs:
                    continue
                deco = {"pe": block.tensor, "dve": block.vector, "act": block.scalar,
                        "pool": block.gpsimd, "sp": block.sync}[e]

                def body(eng, recs=recs):
                    for waits, fn, inc in recs:
                        for k, v in waits.items():
                            if k[0] == "c":
                                eng.wait_ge(self.sem[k[1]], v)
                            else:
                                eng.wait_ge(self.dslots[k[1]][k[2]]["sem"], v)
                        if fn is None:
                            continue
                        ins = fn(eng)
                        if inc[0] == "c":
                            ins.then_inc(self.sem[inc[1]], 1)
                        else:
                            ins.then_inc(self.dslots[inc[1]][inc[2]]["sem"], 16)
                deco(body)


class Prog:
    def __init__(self):
        self.nc = bass.Bass("TRN2", target_bir_lowering=False)
        self.es = ExitStack()
        self.es.__enter__()
        self.B = Builder(self.nc)
        self.n = 0
        self._rr = 0

    def dram_in(self, name, shape, dt=F32):
        return self.nc.dram_tensor(name, list(shape), dt, kind="ExternalInput").ap()

    def dram_out(self, name, shape, dt=F32):
        return self.nc.dram_tensor(name, list(shape), dt, kind="ExternalOutput").ap()

    def sb_raw(self, shape, dt, name=None):
        self.n += 1
        name = "s_" + (name or f"sb{self.n}")
        return self.es.enter_context(self.nc.sbuf_tensor(name, list(shape), dt))

    def sb(self, shape, dt, name=None):
        t = self.sb_raw(shape, dt, name)
        return Tile(t, name or "")

    def ps(self, shape=(128, 512), dt=F32, name=None):
        self.n += 1
        name = "p_" + (name or f"ps{self.n}")
        return Tile(self.es.enter_context(self.nc.psum_tensor(name, list(shape), dt)), name)

    def finish(self):
        self.B.wait_all("sp")
        self.B.emit()
        self.es.__exit__(None, None, None)
        return self.nc


def emit_consts(P, norm=True, nbuf=2, sqw=TB):
    B = P.B
    ones = P.sb([128, 128], BF16, "ones_bf")
    B.op("pool", lambda e: e.memset(ones[:], 1.0), writes=[ones])
    P.ones = ones
    if not norm:
        return
    P.ps_ss = P.ps(name="ps_ss")
    P.sqt = [P.sb([128, KC, sqw], BF16, f"sq{i}") for i in range(nbuf)]
    P.rstd = [P.sb([128, sqw], F32, f"rstd{i}") for i in range(nbuf)]
    P._nbuf = nbuf
    P._sqi = 0


def emit_rstd(P, src_fn, src_tiles, nchunk=KC, width=TB, dim=D, eps=RMS_EPS):
    B = P.B
    i = P._sqi
    P._sqi = (P._sqi + 1) % P._nbuf
    sq, rstd, pss, ones = P.sqt[i], P.rstd[i], P.ps_ss, P.ones
    epsc = P.eps_col(eps)
    for c in range(nchunk):
        B.op("act", lambda e, c=c: e.activation(out=sq[:, c, 0:width], in_=src_fn(c), func=AF.Square),
             reads=src_tiles, writes=[sq])
    for c in range(nchunk):
        B.op("pe", lambda e, c=c: e.matmul(pss[:, 0:width], ones[:], sq[:, c, 0:width],
                                           start=(c == 0), stop=(c == nchunk - 1)),
             reads=[ones, sq], writes=[pss])
    B.op("act", lambda e: e.activation(out=rstd[:, 0:width], in_=pss[:, 0:width], func=AF.Sqrt,
                                       scale=1.0 / dim, bias=epsc[:]),
         reads=[pss, epsc], writes=[rstd])
    B.op("dve", lambda e: e.reciprocal(out=rstd[:, 0:width], in_=rstd[:, 0:width]),
         reads=[rstd], writes=[rstd])
    return rstd


def _eps_col(P, eps):
    if not hasattr(P, "_eps"):
        P._eps = {}
    if eps not in P._eps:
        t = P.sb([128, 1], F32, f"eps{len(P._eps)}")
        P.B.op("pool", lambda e: e.memset(t[:], float(eps)), writes=[t])
        P._eps[eps] = t
    return P._eps[eps]


Prog.eps_col = _eps_col


def load_gcols(P, ap, ncol, name):
    t = P.sb([128, ncol], F32, name)
    P.B.dma("sp", t[:], ap, writes=[t])
    return t


class WStream:
    def __init__(self, P, nstage=3, stage_elems=2048):
        self.P = P
        self.stage = [P.sb([128, stage_elems], F32, f"wst{i}") for i in range(nstage)]
        self.i = 0
        self.stage_elems = stage_elems
        self.cast_rr = 0

    def load(self, dst_tile, dst_ap_fn, src_ap, shape, scale_col=None, queue="sp"):
        P, B = self.P, self.P.B
        st = self.stage[self.i]
        self.i = (self.i + 1) % len(self.stage)
        n = int(np.prod(shape[1:]))
        assert n <= self.stage_elems
        if len(shape) == 3:
            sview = st.t[0:shape[0], 0:n].rearrange("p (a b) -> p a b", a=shape[1])
        else:
            sview = st.t[0:shape[0], 0:n]
        B.dma(queue, sview, src_ap, writes=[st])
        eng = ("pool", "dve")[self.cast_rr % 2] if False else "pool"
        self.cast_rr += 1
        B.op(eng, lambda e: e.tensor_copy(out=dst_ap_fn(), in_=sview), reads=[st], writes=[dst_tile])


def build_mlp():
    P = Prog()
    B = P.B
    h_d = P.dram_in("h", [KC, 128, T])
    w1_d = P.dram_in("w1", [D, DFF])
    w2_d = P.dram_in("w2", [DFF, D])
    g_d = P.dram_in("g", [128, 2 * KC])
    o_d = P.dram_out("ho", [KC, 128, T])
    emit_consts(P)
    gcol = load_gcols(P, g_d, 2 * KC, "gcol")

    hf_t = P.sb_raw([128, KC, T], F32, "hf")
    hf = [Tile(hf_t, f"hf{n}") for n in range(NB)]
    hn_t = P.sb_raw([128, KC, T], BF16, "hn")
    hn = [Tile(hn_t, f"hn{n}") for n in range(NB)]

    def blk(n):
        return slice(n * TB, (n + 1) * TB)

    for n in range(NB):
        for c in range(KC):
            B.dma("sp" if c % 2 == 0 else "act", hf_t[:, c, blk(n)], h_d[c, :, blk(n)], writes=[hf[n]])
    for n in range(NB):
        rstd = emit_rstd(P, lambda c, n=n: hf_t[:, c, blk(n)], [hf[n]])
        for c in range(KC):
            B.op("dve", lambda e, c=c, n=n, rstd=rstd: e.scalar_tensor_tensor(
                out=hn_t[:, c, blk(n)], in0=hf_t[:, c, blk(n)], scalar=gcol[:, c:c + 1],
                in1=rstd[:, 0:TB], op0=ALU.mult, op1=ALU.mult),
                reads=[hf[n], gcol, rstd], writes=[hn[n]])

    FG = 512
    NG = DFF // FG
    ws = WStream(P)
    w1b = [P.sb([128, KC, FG], BF16, f"w1b{i}") for i in range(2)]
    w2b = [P.sb([128, FG // 128, D], BF16, f"w2b{i}") for i in range(2)]
    h1 = [P.sb([128, FG // 128, TB], BF16, f"h1_{i}") for i in range(2)]
    rl = [P.sb([128, TB], F32, f"rl{i}") for i in range(2)]
    pha = [P.ps(name=f"ph{i}") for i in range(3)]
    pfa = [P.ps(name=f"pf{i}") for i in range(3)]
    w1v = w1_d.rearrange("(kc p) n -> p kc n", p=128)
    w2v = w2_d.rearrange("(fc p) n -> p fc n", p=128)
    ih = 0
    ipf = 0
    it = 0
    for j in range(NG):
        wb1, wb2 = w1b[j % 2], w2b[j % 2]
        for half in range(2):
            ks = slice(half * 4, half * 4 + 4)
            ws.load(wb1, lambda wb1=wb1, ks=ks: wb1[:, ks, :], w1v[:, ks, j * FG:(j + 1) * FG], [128, 4, FG])
        for half in range(2):
            fs = slice(half * 2, half * 2 + 2)
            ws.load(wb2, lambda wb2=wb2, fs=fs: wb2[:, fs, :],
                    w2v[:, j * 4 + half * 2: j * 4 + half * 2 + 2, :], [128, 2, D])
        for n in range(NB):
            h1t = h1[it % 2]
            it += 1
            for fc in range(FG // 128):
                ph = pha[ih % 3]
                r = rl[ih % 2]
                ih += 1
                for kc in range(KC):
                    B.op("pe", lambda e, ph=ph, wb1=wb1, kc=kc, fc=fc, n=n: e.matmul(
                        ph[:, 0:TB], wb1[:, kc, fc * 128:(fc + 1) * 128], hn_t[:, kc, blk(n)],
                        start=(kc == 0), stop=(kc == KC - 1)),
                        reads=[wb1, hn[n]], writes=[ph])
                B.op("act", lambda e, ph=ph, r=r: e.activation(out=r[:, :], in_=ph[:, 0:TB], func=AF.Relu),
                     reads=[ph], writes=[r])
                B.op("pool", lambda e, r=r, h1t=h1t, fc=fc: e.tensor_tensor(
                    out=h1t[:, fc, :], in0=r[:, :], in1=r[:, :], op=ALU.mult),
                    reads=[r], writes=[h1t])
            for m in range(KC):
                pf = pfa[ipf % 3]
                ipf += 1
                nfc = FG // 128
                for fc in range(nfc):
                    B.op("pe", lambda e, pf=pf, wb2=wb2, fc=fc, m=m, h1t=h1t: e.matmul(
                        pf[:, 0:TB], wb2[:, fc, m * 128:(m + 1) * 128], h1t[:, fc, :],
                        start=(fc == 0), stop=(fc == nfc - 1)),
                        reads=[wb2, h1t], writes=[pf])
                if j == 0:
                    B.op("act", lambda e, pf=pf, m=m, n=n: e.activation(
                        out=hf_t[:, m, blk(n)], in_=pf[:, 0:TB], func=AF.Copy),
                        reads=[pf], writes=[hf[n]])
                else:
                    B.op("dve", lambda e, pf=pf, m=m, n=n: e.tensor_tensor(
                        out=hf_t[:, m, blk(n)], in0=hf_t[:, m, blk(n)], in1=pf[:, 0:TB], op=ALU.add),
                        reads=[pf, hf[n]], writes=[hf[n]])
    hb = [P.sb([128, KC, TB], F32, f"hb{i}") for i in range(2)]
    for n in range(NB):
        hbt = hb[n % 2]
        for c in range(KC):
            B.dma("act", hbt[:, c, :], h_d[c, :, blk(n)], writes=[hbt])
        rstd = emit_rstd(P, lambda c, n=n: hf_t[:, c, blk(n)], [hf[n]])
        for c in range(KC):
            B.op("dve", lambda e, c=c, n=n, rstd=rstd: e.scalar_tensor_tensor(
                out=hf_t[:, c, blk(n)], in0=hf_t[:, c, blk(n)], scalar=gcol[:, KC + c:KC + c + 1],
                in1=rstd[:, 0:TB], op0=ALU.mult, op1=ALU.mult),
                reads=[hf[n], gcol, rstd], writes=[hf[n]])
            B.op("pool", lambda e, c=c, n=n, hbt=hbt: e.tensor_tensor(
                out=hbt[:, c, :], in0=hbt[:, c, :], in1=hf_t[:, c, blk(n)], op=ALU.add),
                reads=[hf[n], hbt], writes=[hbt])
        for c in range(KC):
            B.dma("sp", o_d[c, :, blk(n)], hbt[:, c, :], reads=[hbt])
    return P.finish()


def to_fm(h_tok):
    return np.ascontiguousarray(h_tok.T.reshape(KC, 128, h_tok.shape[0]))


def from_fm(h_fm):
    return np.ascontiguousarray(h_fm.reshape(D, h_fm.shape[-1]).T)


def gcols(*vecs):
    return np.ascontiguousarray(np.concatenate([v.reshape(KC, 128).T for v in vecs], axis=1))


_NC_CACHE = {}


def _get(name, builder):
    if name not in _NC_CACHE:
        _NC_CACHE[name] = builder()
    return _NC_CACHE[name]


def run_mlp(h_shards, w1, w2, g2, g3):
    nc = _get("mlp", build_mlp)
    g = gcols(g2, g3)
    in_maps = [{"h": h_shards[i], "w1": w1, "w2": w2, "g": g} for i in range(NCORES)]
    res = run_bass_kernel_spmd(nc, in_maps, core_ids=list(range(NCORES)))
    return [r["ho"] for r in res.results]


def blk(n):
    return slice(n * TB, (n + 1) * TB)


def load_h(P, h_d, name="hf"):
    hf_t = P.sb_raw([128, KC, T], F32, name)
    hf = [Tile(hf_t, f"{name}{n}") for n in range(NB)]
    for n in range(NB):
        for c in range(KC):
            P.B.dma("sp" if c % 2 == 0 else "act", hf_t[:, c, blk(n)], h_d[c, :, blk(n)], writes=[hf[n]])
    return hf_t, hf


def emit_norm_bf16(P, hf_t, hf, gcol, gofs, name="hn"):
    hn_t = P.sb_raw([128, KC, T], BF16, name)
    hn = [Tile(hn_t, f"{name}{n}") for n in range(NB)]
    for n in range(NB):
        rstd = emit_rstd(P, lambda c, n=n: hf_t[:, c, blk(n)], [hf[n]])
        for c in range(KC):
            P.B.op("dve", lambda e, c=c, n=n, rstd=rstd: e.scalar_tensor_tensor(
                out=hn_t[:, c, blk(n)], in0=hf_t[:, c, blk(n)], scalar=gcol[:, gofs + c:gofs + c + 1],
                in1=rstd[:, 0:TB], op0=ALU.mult, op1=ALU.mult),
                reads=[hf[n], gcol, rstd], writes=[hn[n]])
    return hn_t, hn


def emit_norm_from_dram(P, h_d, gcol, gofs, name="hn"):
    hn_t = P.sb_raw([128, KC, T], BF16, name)
    hn = [Tile(hn_t, f"{name}{n}") for n in range(NB)]
    hb = [P.sb([128, KC, TB], F32, f"{name}_hb{i}") for i in range(2)]
    for n in range(NB):
        hbt = hb[n % 2]
        for c in range(KC):
            P.B.dma("sp" if c % 2 == 0 else "act", hbt[:, c, :], h_d[c, :, blk(n)], writes=[hbt])
        rstd = emit_rstd(P, lambda c, hbt=hbt: hbt[:, c, :], [hbt])
        for c in range(KC):
            P.B.op("dve", lambda e, c=c, n=n, rstd=rstd, hbt=hbt: e.scalar_tensor_tensor(
                out=hn_t[:, c, blk(n)], in0=hbt[:, c, :], scalar=gcol[:, gofs + c:gofs + c + 1],
                in1=rstd[:, 0:TB], op0=ALU.mult, op1=ALU.mult),
                reads=[hbt, gcol, rstd], writes=[hn[n]])
    return hn_t, hn


def load_weight_resident(P, ws, w_d, ncols, name, group=512, kchunks=KC):
    wv = w_d.rearrange("(kc p) n -> p kc n", p=128)
    tiles = []
    per = max(1, ws.stage_elems // group)
    for g0 in range(0, ncols, group):
        gw = min(group, ncols - g0)
        wt = P.sb([128, kchunks, gw], BF16, f"{name}{g0}")
        for k0 in range(0, kchunks, per):
            k1 = min(kchunks, k0 + per)
            ws.load(wt, lambda wt=wt, k0=k0, k1=k1: wt[:, k0:k1, :], wv[:, k0:k1, g0:g0 + gw],
                    [128, k1 - k0, gw])
        tiles.append(wt)
    return tiles


def emit_linear(P, wtiles, group, m, x_t, x_tile, n, ps, kchunks=KC):
    wt = wtiles[(m * 128) // group]
    mo = (m * 128) % group
    for kc in range(kchunks):
        P.B.op("pe", lambda e, kc=kc: e.matmul(ps[:, 0:TB], wt[:, kc, mo:mo + 128], x_t[:, kc, blk(n)],
                                               start=(kc == 0), stop=(kc == kchunks - 1)),
               reads=[wt, x_tile], writes=[ps])


def build_oa():
    P = Prog()
    B = P.B
    h_d = P.dram_in("h", [KC, 128, T])
    w_d = P.dram_in("w", [D, 3 * D])
    g_d = P.dram_in("g", [128, KC])
    cs_d = P.dram_in("cs", [2, 128, T])
    pm_d = P.dram_in("perm", [128, 128])
    q_d = P.dram_out("q", [KC, 128, T], BF16)
    k_d = P.dram_out("k", [KC, 128, T], BF16)
    v_d = P.dram_out("v", [KC, 128, T], BF16)
    emit_consts(P)
    gcol = load_gcols(P, g_d, KC, "gcol")
    perm = P.sb([128, 128], F32, "perm")
    B.dma("sp", perm[:], pm_d, writes=[perm])
    cos = P.sb([128, T], F32, "cos")
    sin = P.sb([128, T], F32, "sin")
    B.dma("sp", cos[:], cs_d[0], writes=[cos])
    B.dma("act", sin[:], cs_d[1], writes=[sin])
    ws = WStream(P)
    wt = load_weight_resident(P, ws, w_d, 3 * D, "win")
    hn_t, hn = emit_norm_from_dram(P, h_d, gcol, 0)
    psa = [P.ps(name=f"pa{i}") for i in range(3)]
    psr = [P.ps(name=f"pr{i}") for i in range(2)]
    qf = [P.sb([128, TB], F32, f"qf{i}") for i in range(2)]
    t1 = [P.sb([128, TB], F32, f"t1{i}") for i in range(2)]
    t2 = [P.sb([128, TB], F32, f"t2{i}") for i in range(2)]
    ob = [P.sb([128, TB], BF16, f"ob{i}") for i in range(3)]
    ia = 0
    io = 0
    for n in range(NB):
        for m in range(3 * KC):
            ps = psa[ia % 3]
            emit_linear(P, wt, 512, m, hn_t, hn[n], n, ps)
            o = ob[io % 3]
            io += 1
            if m < 2 * KC:
                f, a, b, pr = qf[ia % 2], t1[ia % 2], t2[ia % 2], psr[ia % 2]
                B.op("act", lambda e, f=f, ps=ps: e.activation(out=f[:], in_=ps[:, 0:TB], func=AF.Copy),
                     reads=[ps], writes=[f])
                B.op("pe", lambda e, f=f, pr=pr: e.matmul(pr[:, 0:TB], perm[:], f[:], start=True, stop=True),
                     reads=[perm, f], writes=[pr])
                B.op("dve", lambda e, f=f, a=a, n=n: e.tensor_tensor(out=a[:], in0=f[:], in1=cos[:, blk(n)], op=ALU.mult),
                     reads=[f, cos], writes=[a])
                B.op("dve", lambda e, b=b, pr=pr, n=n: e.tensor_tensor(out=b[:], in0=pr[:, 0:TB], in1=sin[:, blk(n)], op=ALU.mult),
                     reads=[pr, sin], writes=[b])
                B.op("pool", lambda e, a=a, b=b, o=o: e.tensor_tensor(out=o[:], in0=a[:], in1=b[:], op=ALU.add),
                     reads=[a, b], writes=[o])
                dst = (q_d if m < KC else k_d)[m % KC, :, blk(n)]
            else:
                B.op("act", lambda e, o=o, ps=ps: e.activation(out=o[:], in_=ps[:, 0:TB], func=AF.Copy),
                     reads=[ps], writes=[o])
                dst = v_d[m % KC, :, blk(n)]
            ia += 1
            B.dma("sp", dst, o[:], reads=[o])
    return P.finish()


def rope_consts(pos0, npos):
    dh = 64
    inv = (np.float32(10000.0) ** (-np.arange(0, dh, 2, dtype=np.float32) / np.float32(dh))).astype(np.float32)
    pos = np.arange(pos0, pos0 + npos, dtype=np.float32)
    ang = (pos[:, None] * inv[None, :]).astype(np.float32)
    ang = np.concatenate([ang, ang], axis=-1)
    c = np.cos(ang).astype(np.float32).T
    s = np.sin(ang).astype(np.float32).T
    sgn = np.concatenate([-np.ones(32, np.float32), np.ones(32, np.float32)])[:, None]
    ss = s * sgn
    return np.ascontiguousarray(np.stack([np.concatenate([c, c], 0), np.concatenate([ss, ss], 0)], 0))


def rope_perm():
    pm = np.zeros((128, 128), np.float32)
    for b in (0, 64):
        for d in range(64):
            src = d + 32 if d < 32 else d - 32
            pm[b + src, b + d] = 1.0
    return pm


LP = 16512
NKB = LP // 128
SUBLN_EPS = 1e-5


def build_ob(lam_init):
    P = Prog()
    B = P.B
    q_d = P.dram_in("q", [128, LP], BF16)
    k_d = P.dram_in("k", [128, LP], BF16)
    v_d = P.dram_in("v", [128, LP], BF16)
    lv_d = P.dram_in("lv", [128, 256])
    sw_d = P.dram_in("sw", [128, 1])
    id_d = P.dram_in("ident", [128, 128], BF16)
    tri_d = P.dram_in("tri", [128, 128], BF16)
    o_d = P.dram_out("o", [128, LP], BF16)
    emit_consts(P, norm=False)
    ones = P.ones
    qT = P.sb([128, LP], BF16, "qT")
    kT = P.sb([128, LP], BF16, "kT")
    vT = P.sb([128, LP], BF16, "vT")
    for i, (t, d) in enumerate(((qT, q_d), (kT, k_d), (vT, v_d))):
        B.dma(("sp", "act", "pool")[i], t[:], d, writes=[t])
    ident = P.sb([128, 128], BF16, "ident")
    tri = P.sb([128, 128], BF16, "tri")
    B.dma("sp", ident[:], id_d, writes=[ident])
    B.dma("sp", tri[:], tri_d, writes=[tri])
    lv = P.sb([128, 256], F32, "lv")
    sw = P.sb([128, 1], F32, "sw")
    B.dma("sp", lv[:], lv_d, writes=[lv])
    B.dma("sp", sw[:], sw_d, writes=[sw])
    lp = P.sb([128, 128], F32, "lp")
    ls = P.sb([128, 2], F32, "ls")
    nlam = P.sb([128, 1], F32, "nlam")
    swc = P.sb([128, 1], F32, "swc")
    B.op("dve", lambda e: e.tensor_tensor(out=lp[:, 0:64], in0=lv[:, 0:64], in1=lv[:, 64:128], op=ALU.mult),
         reads=[lv], writes=[lp])
    B.op("dve", lambda e: e.tensor_tensor(out=lp[:, 64:128], in0=lv[:, 128:192], in1=lv[:, 192:256], op=ALU.mult),
         reads=[lv, lp], writes=[lp])
    B.op("dve", lambda e: e.reduce_sum(out=ls[:, 0:1], in_=lp[:, 0:64], axis=AX.X), reads=[lp], writes=[ls])
    B.op("dve", lambda e: e.reduce_sum(out=ls[:, 1:2], in_=lp[:, 64:128], axis=AX.X), reads=[lp, ls], writes=[ls])
    B.op("act", lambda e: e.activation(out=ls[:], in_=ls[:], func=AF.Exp), reads=[ls], writes=[ls])
    B.op("dve", lambda e: e.scalar_tensor_tensor(out=nlam[:], in0=ls[:, 1:2], scalar=-float(lam_init),
                                                 in1=ls[:, 0:1], op0=ALU.add, op1=ALU.subtract),
         reads=[ls], writes=[nlam])
    B.op("dve", lambda e: e.tensor_scalar(out=swc[:], in0=sw[:], scalar1=float(1.0 - lam_init), scalar2=None,
                                          op0=ALU.mult), reads=[sw], writes=[swc])
    V = P.sb([128, NKB, 128], BF16, "V")
    ptr = P.ps([128, 1024], BF16, name="ptr")
    for j0 in range(0, NKB, 8):
        j1 = min(NKB, j0 + 8)
        for j in range(j0, j1):
            B.op("pe", lambda e, j=j, j0=j0: e.transpose(ptr[:, (j - j0) * 128:(j - j0 + 1) * 128],
                                                          vT[:, j * 128:(j + 1) * 128], ident[:]),
                 reads=[vT, ident], writes=[ptr])
        B.op("dve", lambda e, j0=j0, j1=j1: e.tensor_copy(
            out=V[:, j0:j1, :], in_=ptr[:, 0:(j1 - j0) * 128].rearrange("p (a b) -> p a b", b=128)),
            reads=[ptr], writes=[V])
    acc = [[P.ps(name=f"O{c}"), P.ps(name=f"s{c}")] for c in range(2)]
    psS = [P.ps(name=f"S{i}") for i in range(3)]
    pts = [P.sb([128, 512], BF16, f"pt{i}") for i in range(4)]
    rr = [P.sb([128, 512], F32, f"rr{i}") for i in range(2)]
    oo = [P.sb([128, 512], F32, f"oo{i}") for i in range(2)]
    osq = P.sb([128, 512], BF16, "osq")
    rs = P.sb([128, 512], F32, "rs")
    outb = [P.sb([128, 512], BF16, f"outb{i}") for i in range(2)]
    epsc = P.eps_col(SUBLN_EPS)
    iS = 0
    ip = 0
    nqt = (LP + 511) // 512
    for i in range(nqt):
        q0 = i * 512
        qw = min(512, LP - q0)
        nqb = qw // 128
        nk = 4 * i + nqb
        for j in range(nk):
            a = j - 4 * i
            c0 = max(0, a) * 128
            for c in range(2):
                ps = psS[iS % 3]
                iS += 1
                pt = pts[ip % 4]
                ip += 1
                rows = slice(c * 64, (c + 1) * 64)
                B.op("pe", lambda e, ps=ps, rows=rows, j=j, c0=c0, q0=q0, qw=qw: e.matmul(
                    ps[:, c0:qw], kT[rows, j * 128:(j + 1) * 128], qT[rows, q0 + c0:q0 + qw],
                    start=True, stop=True), reads=[kT, qT], writes=[ps])
                B.op("act", lambda e, ps=ps, pt=pt, c0=c0, qw=qw: e.activation(
                    out=pt[:, c0:qw], in_=ps[:, c0:qw], func=AF.Exp, scale=0.125),
                    reads=[ps], writes=[pt])
                if a >= 0:
                    B.op("pool", lambda e, pt=pt, c0=c0: e.tensor_tensor(
                        out=pt[:, c0:c0 + 128], in0=pt[:, c0:c0 + 128], in1=tri[:], op=ALU.mult),
                        reads=[pt, tri], writes=[pt])
                O, s = acc[c]
                B.op("pe", lambda e, O=O, pt=pt, j=j, c0=c0, qw=qw, nk=nk: e.matmul(
                    O[:, c0:qw], V[:, j, :], pt[:, c0:qw], start=(j == 0), stop=(j == nk - 1)),
                    reads=[V, pt], writes=[O])
                B.op("pe", lambda e, s=s, pt=pt, j=j, c0=c0, qw=qw, nk=nk: e.matmul(
                    s[:, c0:qw], ones[:], pt[:, c0:qw], start=(j == 0), stop=(j == nk - 1)),
                    reads=[ones, pt], writes=[s])
        for c in range(2):
            O, s = acc[c]
            B.op("dve", lambda e, s=s, c=c, qw=qw: e.reciprocal(out=rr[c][:, 0:qw], in_=s[:, 0:qw]),
                 reads=[s], writes=[rr[c]])
            B.op("dve", lambda e, O=O, c=c, qw=qw: e.tensor_tensor(
                out=oo[c][:, 0:qw], in0=O[:, 0:qw], in1=rr[c][:, 0:qw], op=ALU.mult),
                reads=[O, rr[c]], writes=[oo[c]])
        B.op("dve", lambda e, qw=qw: e.scalar_tensor_tensor(
            out=oo[0][:, 0:qw], in0=oo[1][:, 0:qw], scalar=nlam[:, 0:1], in1=oo[0][:, 0:qw],
            op0=ALU.mult, op1=ALU.add), reads=[oo[0], oo[1], nlam], writes=[oo[0]])
        B.op("act", lambda e, qw=qw: e.activation(out=osq[:, 0:qw], in_=oo[0][:, 0:qw], func=AF.Square),
             reads=[oo[0]], writes=[osq])
        ps = psS[iS % 3]
        iS += 1
        B.op("pe", lambda e, ps=ps, qw=qw: e.matmul(ps[:, 0:qw], ones[:], osq[:, 0:qw], start=True, stop=True),
             reads=[ones, osq], writes=[ps])
        B.op("act", lambda e, ps=ps, qw=qw: e.activation(out=rs[:, 0:qw], in_=ps[:, 0:qw], func=AF.Sqrt,
                                                         scale=1.0 / 128, bias=epsc[:]),
             reads=[ps, epsc], writes=[rs])
        B.op("dve", lambda e, qw=qw: e.reciprocal(out=rs[:, 0:qw], in_=rs[:, 0:qw]), reads=[rs], writes=[rs])
        ot = outb[i % 2]
        B.op("dve", lambda e, ot=ot, qw=qw: e.scalar_tensor_tensor(
            out=ot[:, 0:qw], in0=oo[0][:, 0:qw], scalar=swc[:, 0:1], in1=rs[:, 0:qw],
            op0=ALU.mult, op1=ALU.mult), reads=[oo[0], swc, rs], writes=[ot])
        B.dma("sp", o_d[:, q0:q0 + qw], ot[:, 0:qw], reads=[ot])
    return P.finish()


def build_oc():
    P = Prog()
    B = P.B
    h_d = P.dram_in("h", [KC, 128, T])
    mo_d = P.dram_in("mo", [KC, 128, T], BF16)
    w_d = P.dram_in("w", [D, D])
    g_d = P.dram_in("g", [128, KC])
    o_d = P.dram_out("ho", [KC, 128, T])
    emit_consts(P)
    gcol = load_gcols(P, g_d, KC, "gcol")
    ws = WStream(P)
    wt = load_weight_resident(P, ws, w_d, D, "wout")
    hf_t, hf = load_h(P, h_d)
    mo_t = P.sb_raw([128, KC, T], BF16, "mo")
    mo = [Tile(mo_t, f"mo{n}") for n in range(NB)]
    for n in range(NB):
        for c in range(KC):
            B.dma("pool", mo_t[:, c, blk(n)], mo_d[c, :, blk(n)], writes=[mo[n]])
    emit_outproj_residual(P, wt, mo_t, mo, hf_t, hf, gcol, 0)
    for n in range(NB):
        for c in range(KC):
            B.dma("sp", o_d[c, :, blk(n)], hf_t[:, c, blk(n)], reads=[hf[n]])
    return P.finish()


def emit_outproj_residual(P, wt, mo_t, mo, hf_t, hf, gcol, gofs):
    B = P.B
    psa = [P.ps(name=f"po{i}") for i in range(3)]
    mb = [P.sb([128, KC, TB], F32, f"mb{i}") for i in range(2)]
    ia = 0
    for n in range(NB):
        m_t = mb[n % 2]
        for m in range(KC):
            ps = psa[ia % 3]
            ia += 1
            emit_linear(P, wt, 512, m, mo_t, mo[n], n, ps)
            B.op("act", lambda e, ps=ps, m=m, m_t=m_t: e.activation(out=m_t[:, m, :], in_=ps[:, 0:TB], func=AF.Copy),
                 reads=[ps], writes=[m_t])
        rstd = emit_rstd(P, lambda c, m_t=m_t: m_t[:, c, :], [m_t])
        for c in range(KC):
            B.op("dve", lambda e, c=c, m_t=m_t, rstd=rstd: e.scalar_tensor_tensor(
                out=m_t[:, c, :], in0=m_t[:, c, :], scalar=gcol[:, gofs + c:gofs + c + 1],
                in1=rstd[:, 0:TB], op0=ALU.mult, op1=ALU.mult),
                reads=[m_t, gcol, rstd], writes=[m_t])
            B.op("pool", lambda e, c=c, n=n, m_t=m_t: e.tensor_tensor(
                out=hf_t[:, c, blk(n)], in0=hf_t[:, c, blk(n)], in1=m_t[:, c, :], op=ALU.add),
                reads=[m_t, hf[n]], writes=[hf[n]])


def _run(name, builder, in_maps):
    nc = _get(name, builder)
    res = run_bass_kernel_spmd(nc, in_maps, core_ids=list(range(NCORES)))
    return res.results


def run_odd_mixer(h_shards, w_in, lam_vecs, subln_w, w_out, g0, g1, layer):
    perm = rope_perm()
    g = gcols(g0)
    in_maps = [{"h": h_shards[i], "w": w_in, "g": g, "cs": rope_consts(i * T, T), "perm": perm}
               for i in range(NCORES)]
    ra = _run("oa", build_oa, in_maps)
    bf = ra[0]["q"].dtype

    def gather(key, hd):
        full = np.zeros((128, LP), bf)
        for i in range(NCORES):
            full[:, i * T:(i + 1) * T] = ra[i][key][hd]
        return full
    lam_init = 0.8 - 0.6 * math.exp(-0.3 * layer)
    lv = np.ascontiguousarray(np.broadcast_to(lam_vecs.reshape(1, 256), (128, 256)))
    ident = np.eye(128, dtype=np.float32).astype(bf)
    tri = np.triu(np.ones((128, 128), np.float32)).astype(bf)
    sw = np.ascontiguousarray(subln_w.reshape(128, 1))
    in_maps = [{"q": gather("q", hd), "k": gather("k", hd), "v": gather("v", hd), "lv": lv, "sw": sw,
                "ident": ident, "tri": tri} for hd in range(NCORES)]
    rb = _run(f"ob{layer}", lambda: build_ob(lam_init), in_maps)
    g = gcols(g1)
    in_maps = []
    for i in range(NCORES):
        mo = np.stack([rb[hd]["o"][:, i * T:(i + 1) * T] for hd in range(NCORES)], 0)
        in_maps.append({"h": h_shards[i], "mo": np.ascontiguousarray(mo), "w": w_out, "g": g})
    rc = _run("oc", build_oc, in_maps)
    return [r["ho"] for r in rc]


CH = 82
NCH = T // CH
HALO = 16
CW = CH + HALO
NH = 8
RW = 512
EVEN_IN = 2304
GN_EPS = 64 * 1e-5
DEBUG = False
NLEV = 7
DECAY_SCALE = -math.exp(-0.5)


def build_even(aug):
    P = Prog()
    B = P.B
    h_d = P.dram_in("h", [KC, 128, T])
    hh_d = P.dram_in("hh", [KC, 128, HALO])
    w_d = P.dram_in("w", [D, EVEN_IN])
    g_d = P.dram_in("g", [128, 2 * KC])
    c64_d = P.dram_in("c64", [64, 8 * 8])
    c64b_d = P.dram_in("c64b", [64, 2])
    c128_d = P.dram_in("c128", [128, 1 + 4 + 4])
    wup_d = P.dram_in("wup", [64, RW])
    aup_d = P.dram_in("aup", [64, RW])
    gup_d = P.dram_in("gup", [128, RW])
    pw_d = P.dram_in("pw", [4, 128, 128])
    ln_d = P.dram_in("ln", [2, 128, RW])
    msk_d = P.dram_in("msk", [CH, 5 * CH])
    id_d = P.dram_in("ident", [128, 128])
    corr_d = P.dram_in("corr", [128, 4, HALO])
    if aug:
        sum_d = P.dram_out("summ", [64, NH, 128])
    else:
        wo_d = P.dram_in("wo", [D, D])
        sums_d = P.dram_in("sums", [NCORES, 64, NH, 128])
        oh_d = P.dram_in("onehot", [64, NCORES])
        o_d = P.dram_out("ho", [KC, 128, T])
        dbg_d = P.dram_out("dbg", [KC, 128, T], BF16) if DEBUG else None
    emit_consts(P, nbuf=1, sqw=CW)
    ones = P.ones
    gcol = load_gcols(P, g_d, 2 * KC, "gcol")

    def cload(ap, shape, name, q="sp"):
        t = P.sb(shape, F32, name)
        B.dma(q, t[:], ap, writes=[t])
        return t
    c64 = cload(c64_d, [64, 64], "c64")
    c64b = cload(c64b_d, [64, 2], "c64b")
    c128 = cload(c128_d, [128, 9], "c128")
    lnw = cload(ln_d[0], [128, RW], "lnw")
    lnb = cload(ln_d[1], [128, RW], "lnb", "act")
    msk = cload(msk_d, [CH, 5 * CH], "msk")
    ident = cload(id_d, [128, 128], "identf")
    corr = cload(corr_d, [128, 4, HALO], "corr")
    ones32 = P.sb([64, 64], F32, "ones32")
    B.op("pool", lambda e: e.memset(ones32[:], 1.0), writes=[ones32])
    tinyc = P.eps_col(1e-24)
    gnc = P.eps_col(GN_EPS)
    c1 = P.sb([64, 8], F32, "c1")
    B.op("dve", lambda e: e.tensor_scalar(out=c1[:], in0=c64[:, 32:40], scalar1=-1.0, scalar2=1.0,
                                          op0=ALU.mult, op1=ALU.add), reads=[c64], writes=[c1])

    def cc(i):
        return c64[:, i * 8:(i + 1) * 8]

    def bc(ap2, w):
        return ap2.unsqueeze(2).to_broadcast([64, NH, w])

    scanmask = P.sb([64, NH, CH], F32, "scanmask")
    B.op("pool", lambda e: e.memset(scanmask[:], 1.0), writes=[scanmask])
    B.op("pool", lambda e: e.memset(scanmask[:, :, 0:1], 0.0), reads=[scanmask], writes=[scanmask])

    ws = WStream(P, nstage=2, stage_elems=1024)
    wupb = P.sb([64, RW], BF16, "wupb")
    aupb = P.sb([64, RW], BF16, "aupb")
    gupb = P.sb([128, RW], BF16, "gupb")
    pwb = P.sb([128, 4, 128], BF16, "pwb")
    ws.load(wupb, lambda: wupb[:, :], wup_d, [64, RW])
    ws.load(aupb, lambda: aupb[:, :], aup_d, [64, RW])
    ws.load(gupb, lambda: gupb[:, :], gup_d, [128, RW])
    ws.load(pwb, lambda: pwb[:, :, :], pw_d.rearrange("g c d -> c g d"), [128, 4, 128])
    wt = load_weight_resident(P, ws, w_d, EVEN_IN, "win", group=256)
    if not aug:
        wot = load_weight_resident(P, ws, wo_d, D, "wout", group=512)

    hx = P.sb([128, KC, CW], F32, "hx")
    hnc = P.sb([128, KC, CW], BF16, "hnc")

    pj = [P.ps(name=f"pj{i}") for i in range(2)]
    sc = [P.ps(name=f"sc{i}") for i in range(4)]
    pst = P.ps(name="pstate")
    _isc = [0]
    _ipj = [0]

    def nsc():
        _isc[0] += 1
        return sc[_isc[0] % 4]

    def npj():
        _ipj[0] += 1
        return pj[_ipj[0] % 2]

    SW = 128 if aug else 64
    S = [P.sb([64, NH, SW], F32, f"S{i}") for i in range(2)]
    if aug:
        B.op("pool", lambda e: e.memset(S[0][:], 0.0), writes=[S[0]])
        for h in range(NH):
            B.op("pool", lambda e, h=h: e.tensor_copy(out=S[0][:, h, 0:64], in_=ident[0:64, 0:64]),
                 reads=[ident, S[0]], writes=[S[0]])
    else:
        sums = P.sb([64, NH, 128], F32, "sums")
        oh = cload(oh_d, [64, NCORES], "oh")
        X = [P.sb([64, NH, 64], F32, f"X{i}") for i in range(2)]
        PT = P.sb([64, NH, 64], F32, "PmT")
        B.op("pool", lambda e: e.memset(X[0][:], 0.0), writes=[X[0]])
        B.op("pool", lambda e: e.memset(S[0][:], 0.0), writes=[S[0]])
        for j in range(NCORES - 1):
            xo, xn = X[j % 2], X[(j + 1) % 2]
            B.dma("sp", sums[:], sums_d[j], writes=[sums])
            ps = nsc()
            for h in range(NH):
                B.op("pe", lambda e, ps=ps, j=j, h=h: e.transpose(ps[0:64, h * 64:(h + 1) * 64],
                                                                    sums[:, h, 0:64], ident[0:64, 0:64]),
                     reads=[sums, ident], writes=[ps])
            B.op("act", lambda e, ps=ps: e.activation(out=PT[:].rearrange("k h x -> k (h x)"), in_=ps[0:64, 0:512],
                                                      func=AF.Copy), reads=[ps], writes=[PT])
            ps2 = nsc()
            for h in range(NH):
                B.op("pe", lambda e, ps2=ps2, h=h, xo=xo: e.matmul(ps2[0:64, h * 64:(h + 1) * 64], PT[:, h, :],
                                                                    xo[:, h, :], start=True, stop=True),
                     reads=[PT, xo], writes=[ps2])
            B.op("dve", lambda e, ps2=ps2, xn=xn, j=j: e.tensor_tensor(
                out=xn[:], in0=ps2[0:64, 0:512].rearrange("k (h x) -> k h x", h=NH), in1=sums[:, :, 64:128],
                op=ALU.add), reads=[ps2, sums], writes=[xn])
            B.op("dve", lambda e, xn=xn, j=j: e.scalar_tensor_tensor(
                out=S[0][:], in0=xn[:], scalar=oh[:, j + 1:j + 2], in1=S[0][:], op0=ALU.mult, op1=ALU.add),
                reads=[xn, oh, S[0]], writes=[S[0]])

    def t64(name, w=CW, nb=1):
        return [P.sb([64, NH, w], F32, f"{name}{i}") for i in range(nb)]
    yr, yk, yv, dtm = t64("yr"), t64("yk"), t64("yv"), t64("dtm", CW, 1)
    ywa = [P.sb([64, 2, CW], F32, f"ywa{i}") for i in range(1)]
    ygd = [P.sb([128, CW], F32, f"ygd{i}") for i in range(1)]
    yu = [P.sb([128, 4, CW], F32, f"yu{i}") for i in range(1)]
    wab = [P.sb([64, 2, CH], BF16, f"wab{i}") for i in range(1)]
    gdb = [P.sb([128, CH], BF16, f"gdb{i}") for i in range(1)]
    bufA, bufB, bufC = t64("bA", CH), t64("bB", CH), t64("bC", CH)
    a_t, kk_t, kp_t, b_t, rk_t = t64("a", CH), t64("kk", CH), t64("kp", CH), t64("b", CH), t64("rk", CH)
    tmp_t = t64("tmp", CH, 1)
    AR = [P.sb([64, NH, 2, CH], F32, f"AR{i}") for i in range(1)]
    Bt, Kt, BW, KW = t64("Bt", CH), t64("Kt", CH), t64("BW", CH), t64("KW", CH)
    TM = [P.sb([CH, NH, 256], F32, f"TM{i}") for i in range(1)]
    SCm = [P.sb([CH, NH, 5 * CH], F32, f"SC{i}") for i in range(1)]
    Y = [P.sb([CH, NH, 128], F32, f"Y{i}") for i in range(1)]
    Pp = [P.sb([CH, NH, 2, CH], F32, f"Pp{i}") for i in range(2)]
    AF_t = t64("AhF", CH)
    Gp = [P.sb([64, NH, 64], F32, f"Gp{i}") for i in range(1)]
    if not aug:
        UT = [P.sb([CH, NH, 64], F32, f"UT{i}") for i in range(1)]
        OT = [P.sb([CH, NH, 64], F32, f"OT{i}") for i in range(1)]
        osq = P.sb([CH, NH, 64], F32, "osq")
        st = [P.sb([CH, 4 * NH], F32, f"st{i}") for i in range(1)]
        sT = [P.sb([CH, NH], F32, f"sT{i}") for i in range(1)]
        gg = [P.sb([128, 4, CH], F32, f"gg{i}") for i in range(1)]
        mo = [P.sb([128, KC, CH], BF16, f"mo{i}") for i in range(1)]
        ws1 = [P.sb([128, 4, CW], F32, f"ws1{i}") for i in range(1)]
        ws2 = [P.sb([128, 4, CW], F32, f"ws2{i}") for i in range(1)]
        dpb = [P.sb([128, 4, CH], BF16, f"dpb{i}") for i in range(1)]
        mblk = [P.sb([128, KC, CH], F32, f"mblk{i}") for i in range(1)]
        hck = [P.sb([128, KC, CH], F32, f"hck{i}") for i in range(1)]

    v3 = lambda ap, h: ap.rearrange("k (h x) -> k h x", h=h)

    for ci in range(NCH):
        p2 = 0
        e0 = ci * CH
        e1 = e0 + CW
        if ci == 0:
            B.dma("sp", hx[:, :, 0:HALO], hh_d.rearrange("c p t -> p c t"), writes=[hx])
            B.dma("act", hx[:, :, HALO:CW], h_d[:, :, 0:CH].rearrange("c p t -> p c t"), reads=[hx], writes=[hx])
        else:
            B.dma("sp", hx[:], h_d[:, :, ci * CH - HALO:(ci + 1) * CH].rearrange("c p t -> p c t"), writes=[hx])
        rstd = emit_rstd(P, lambda c: hx[:, c, :], [hx], width=CW)
        for c in range(KC):
            B.op("dve", lambda e, c=c, rstd=rstd: e.scalar_tensor_tensor(
                out=hnc[:, c, :], in0=hx[:, c, :], scalar=gcol[:, c:c + 1], in1=rstd[:, 0:CW],
                op0=ALU.mult, op1=ALU.mult), reads=[hx, gcol, rstd], writes=[hnc])
        hts = [hnc]

        def proj(col0, m, dst_fn, dst_tile, part=None):
            ps = npj()
            wtile = wt[col0 // 256]
            co = col0 % 256
            for kc in range(KC):
                B.op("pe", lambda e, kc=kc, ps=ps, wtile=wtile, co=co, m=m: e.matmul(
                    ps[0:m, 0:CW], wtile[:, kc, co:co + m], hnc[:, kc, :], start=(kc == 0), stop=(kc == KC - 1)),
                    reads=[wtile] + hts, writes=[ps])
            B.op("act", lambda e, ps=ps, m=m: e.activation(out=dst_fn(), in_=ps[0:m, 0:CW], func=AF.Copy),
                 reads=[ps], writes=[dst_tile])
        for arr, yt in enumerate((yr, yk, yv)):
            for h in range(NH):
                proj(arr * RW + h * 64, 64, lambda yt=yt, h=h: yt[p2][:, h, :], yt[p2])
        proj(3 * RW, 64, lambda: ywa[p2][:, 0, :], ywa[p2])
        proj(3 * RW + 64, 64, lambda: ywa[p2][:, 1, :], ywa[p2])
        proj(3 * RW + 128, 128, lambda: ygd[p2][:, :], ygd[p2])
        for gi in range(4):
            proj(1792 + gi * 128, 128, lambda gi=gi: yu[p2][:, gi, :], yu[p2])

        def shift(yt, nparts, lead, mu_b):
            d = dtm[0]
            dv = d[0:nparts, 0:lead, 0:CW - 1] if lead else None
            B.op("dve", lambda e: e.tensor_tensor(out=d[0:nparts, 0:lead, 0:CW - 1], in0=yt[0:nparts, :, 0:CW - 1],
                                                  in1=yt[0:nparts, :, 1:CW], op=ALU.subtract),
                 reads=[yt], writes=[d])
            B.op("dve", lambda e: e.tensor_tensor(out=d[0:nparts, 0:lead, 0:CW - 1], in0=d[0:nparts, 0:lead, 0:CW - 1],
                                                  in1=mu_b, op=ALU.mult), reads=[d, c64, c64b], writes=[d])
            B.op("dve", lambda e: e.tensor_tensor(out=yt[0:nparts, :, 1:CW], in0=yt[0:nparts, :, 1:CW],
                                                  in1=d[0:nparts, 0:lead, 0:CW - 1], op=ALU.add),
                 reads=[d, yt], writes=[yt])
        for arr, yt in enumerate((yr, yk, yv)):
            shift(yt[p2], 64, NH, bc(cc(arr), CW - 1))
        shift(ywa[p2], 64, 2, c64b[:, 0:2].unsqueeze(2).to_broadcast([64, 2, CW - 1]))
        B.op("pool", lambda e: e.tensor_tensor(out=ws_g[p2][:, 0:CW - 1], in0=ygd[p2][:, 0:CW - 1],
                                               in1=ygd[p2][:, 1:CW], op=ALU.subtract),
             reads=[ygd[p2]], writes=[ws_g[p2]]) if False else None
        R = yr[p2]
        Kx = yk[p2]
        Vx = yv[p2]
        own = slice(HALO, CW)
        B.op("act", lambda e: e.activation(out=wab[p2][:, 0, :], in_=ywa[p2][:, 0, own], func=AF.Tanh),
             reads=[ywa[p2]], writes=[wab[p2]])
        B.op("act", lambda e: e.activation(out=wab[p2][:, 1, :], in_=ywa[p2][:, 1, own], func=AF.Copy),
             reads=[ywa[p2], wab[p2]], writes=[wab[p2]])
        A_, B_, C_ = bufA[p2], bufB[p2], bufC[p2]
        for (srcrow, upw, bias_i, dst) in ((0, wupb, 6, A_), (1, aupb, 7, a_t[p2])):
            for h0 in (0, 4):
                ps = nsc()
                for h in range(h0, h0 + 4):
                    B.op("pe", lambda e, ps=ps, h=h, h0=h0, upw=upw, srcrow=srcrow: e.matmul(
                        ps[0:64, (h - h0) * CH:(h - h0 + 1) * CH], upw[:, h * 64:(h + 1) * 64], wab[p2][:, srcrow, :],
                        start=True, stop=True), reads=[upw, wab[p2]], writes=[ps])
                for h in range(h0, h0 + 4):
                    B.op("act", lambda e, ps=ps, h=h, h0=h0, dst=dst, bias_i=bias_i: e.activation(
                        out=dst[:, h, :], in_=ps[0:64, (h - h0) * CH:(h - h0 + 1) * CH], func=AF.Sigmoid,
                        bias=c64[:, bias_i * 8 + h:bias_i * 8 + h + 1]), reads=[ps, c64], writes=[dst])
        B.op("dve", lambda e: e.tensor_scalar(out=A_[:], in0=A_[:], scalar1=DECAY_SCALE, scalar2=None, op0=ALU.mult),
             reads=[A_], writes=[A_])
        B.op("dve", lambda e: e.tensor_tensor_scan(
            out=B_[:].rearrange("k h x -> k (h x)"), data0=scanmask[:].rearrange("k h x -> k (h x)"),
            data1=A_[:].rearrange("k h x -> k (h x)"), initial=0.0, op0=ALU.mult, op1=ALU.add),
            reads=[scanmask, A_], writes=[B_])
        B.op("dve", lambda e: e.tensor_tensor(out=A_[:], in0=B_[:], in1=A_[:], op=ALU.subtract),
             reads=[B_, A_], writes=[A_])
        B.op("act", lambda e: e.activation(out=A_[:], in_=A_[:], func=AF.Exp), reads=[A_], writes=[A_])
        B.op("act", lambda e: e.activation(out=C_[:], in_=B_[:], func=AF.Exp), reads=[B_], writes=[C_])
        B.op("act", lambda e: e.activation(out=B_[:], in_=B_[:], func=AF.Exp, scale=-1.0), reads=[B_], writes=[B_])
        kkx, tmp = kk_t[p2], tmp_t[0]
        B.op("dve", lambda e: e.tensor_tensor(out=kkx[:], in0=Kx[:, :, own], in1=bc(cc(3), CH), op=ALU.mult),
             reads=[Kx, c64], writes=[kkx])
        B.op("act", lambda e: e.activation(out=tmp[:], in_=kkx[:], func=AF.Square), reads=[kkx], writes=[tmp])
        for h0 in (0, 4):
            ps = nsc()
            B.op("pe", lambda e, ps=ps, h0=h0: e.matmul(
                ps[0:64, 0:4 * CH], ones32[:], tmp[:, h0:h0 + 4, :].rearrange("k h x -> k (h x)"),
                start=True, stop=True), reads=[ones32, tmp], writes=[ps])
            B.op("act", lambda e, ps=ps, h0=h0: e.activation(out=tmp[:, h0:h0 + 4, :].rearrange("k h x -> k (h x)"),
                                                             in_=ps[0:64, 0:4 * CH], func=AF.Sqrt, bias=tinyc[0:64, :]),
                 reads=[ps, tinyc, tmp], writes=[tmp])
        B.op("dve", lambda e: e.reciprocal(out=tmp[:], in_=tmp[:]), reads=[tmp], writes=[tmp])
        B.op("dve", lambda e: e.tensor_tensor(out=kkx[:], in0=kkx[:], in1=tmp[:], op=ALU.mult),
             reads=[kkx, tmp], writes=[kkx])
        kp, ax, bx, rk = kp_t[p2], a_t[p2], b_t[p2], rk_t[p2]
        B.op("dve", lambda e: e.tensor_tensor(out=kp[:], in0=ax[:], in1=bc(cc(4), CH), op=ALU.mult),
             reads=[ax, c64], writes=[kp])
        B.op("dve", lambda e: e.tensor_tensor(out=kp[:], in0=kp[:], in1=bc(c1[:, :], CH), op=ALU.add),
             reads=[kp, c1], writes=[kp])
        B.op("dve", lambda e: e.tensor_tensor(out=kp[:], in0=kp[:], in1=Kx[:, :, own], op=ALU.mult),
             reads=[kp, Kx], writes=[kp])
        B.op("dve", lambda e: e.tensor_tensor(out=bx[:], in0=kkx[:], in1=ax[:], op=ALU.mult),
             reads=[kkx, ax], writes=[bx])
        B.op("dve", lambda e: e.tensor_tensor(out=rk[:], in0=R[:, :, own], in1=bc(cc(5), CH), op=ALU.mult),
             reads=[R, c64], writes=[rk])
        B.op("dve", lambda e: e.tensor_tensor(out=rk[:], in0=rk[:], in1=kp[:], op=ALU.mult),
             reads=[rk, kp], writes=[rk])
        ar, bt, kt, bw, kw = AR[p2], Bt[p2], Kt[p2], BW[p2], KW[p2]
        B.op("dve", lambda e: e.scalar_tensor_tensor(out=ar[:, :, 0, :], in0=kkx[:], scalar=-1.0, in1=A_[:],
                                                     op0=ALU.mult, op1=ALU.mult), reads=[kkx, A_], writes=[ar])
        B.op("dve", lambda e: e.tensor_tensor(out=ar[:, :, 1, :], in0=R[:, :, own], in1=C_[:], op=ALU.mult),
             reads=[R, C_, ar], writes=[ar])
        B.op("dve", lambda e: e.tensor_tensor(out=bt[:], in0=bx[:], in1=B_[:], op=ALU.mult), reads=[bx, B_], writes=[bt])
        B.op("dve", lambda e: e.tensor_tensor(out=kt[:], in0=kp[:], in1=B_[:], op=ALU.mult), reads=[kp, B_], writes=[kt])
        wc_b = C_[:, :, CH - 1:CH].to_broadcast([64, NH, CH])
        B.op("dve", lambda e: e.tensor_tensor(out=bw[:], in0=bt[:], in1=wc_b, op=ALU.mult), reads=[bt, C_], writes=[bw])
        B.op("dve", lambda e: e.tensor_tensor(out=kw[:], in0=kt[:], in1=wc_b, op=ALU.mult), reads=[kt, C_], writes=[kw])

        tm = TM[p2]
        for hp in range(NH // 2):
            ps = nsc()
            for hh in range(2):
                h = hp * 2 + hh
                srcs = (ar[:, h, 0, :], Vx[:, h, own], bw[:, h, :], kw[:, h, :])
                for si, s_ap in enumerate(srcs):
                    c0 = hh * 256 + si * 64
                    B.op("pe", lambda e, ps=ps, s_ap=s_ap, c0=c0: e.transpose(ps[0:CH, c0:c0 + 64], s_ap, ident[0:64, 0:64]),
                         reads=[ar, Vx, bw, kw, ident], writes=[ps])
            B.op("act", lambda e, ps=ps, hp=hp: e.activation(
                out=tm[:, 2 * hp:2 * hp + 2, :], in_=ps[0:CH, 0:512].rearrange("c (h x) -> c h x", h=2), func=AF.Copy),
                reads=[ps], writes=[tm])
        scm = SCm[p2]
        for h in range(NH):
            ps = nsc()
            B.op("pe", lambda e, ps=ps, h=h: e.matmul(ps[0:CH, 0:2 * CH], bt[:, h, :], ar[:, h, :, :].rearrange("k a x -> k (a x)"),
                                                      start=True, stop=True), reads=[bt, ar], writes=[ps])
            B.op("pe", lambda e, ps=ps, h=h: e.matmul(ps[0:CH, 2 * CH:4 * CH], kt[:, h, :], ar[:, h, :, :].rearrange("k a x -> k (a x)"),
                                                      start=True, stop=True), reads=[kt, ar], writes=[ps])
            B.op("pe", lambda e, ps=ps, h=h: e.matmul(ps[0:CH, 4 * CH:5 * CH], ar[:, h, 0, :], bt[:, h, :],
                                                      start=True, stop=True), reads=[bt, ar], writes=[ps])
            B.op("dve", lambda e, ps=ps, h=h: e.tensor_tensor(out=scm[:, h, :], in0=ps[0:CH, 0:5 * CH], in1=msk[:],
                                                              op=ALU.mult), reads=[ps, msk], writes=[scm])
        y = Y[p2]
        B.op("pool", lambda e: e.tensor_copy(out=y[:, :, 0:64], in_=tm[:, :, 0:64]), reads=[tm], writes=[y])
        ps = nsc()
        for h in range(NH):
            B.op("pe", lambda e, ps=ps, h=h: e.matmul(ps[0:CH, h * 64:(h + 1) * 64], scm[:, h, 2 * CH:3 * CH], tm[:, h, 64:128],
                                                      start=True, stop=True), reads=[scm, tm], writes=[ps])
        B.op("act", lambda e, ps=ps: e.activation(out=y[:, :, 64:128], in_=ps[0:CH, 0:512].rearrange("c (h x) -> c h x", h=NH),
                                                  func=AF.Copy), reads=[ps, y], writes=[y])
        pcur = None
        for lev in range(NLEV):
            def Pm_(h, lev=lev, pcur=pcur):
                return scm[:, h, 0:CH] if lev == 0 else pcur[:, h, 0, :]

            def PmT_(h, lev=lev, pcur=pcur):
                return scm[:, h, 4 * CH:5 * CH] if lev == 0 else pcur[:, h, 1, :]
            ptile = scm if lev == 0 else pcur
            for h0 in (0, 4):
                ps = nsc()
                for h in range(h0, h0 + 4):
                    B.op("pe", lambda e, ps=ps, h=h, h0=h0, Pm_=Pm_: e.matmul(
                        ps[0:CH, (h - h0) * 128:(h - h0 + 1) * 128], Pm_(h), y[:, h, :], start=True, stop=True),
                        reads=[ptile, y], writes=[ps])
                B.op("dve", lambda e, ps=ps, h0=h0: e.tensor_tensor(
                    out=y[:, h0:h0 + 4, :], in0=y[:, h0:h0 + 4, :],
                    in1=ps[0:CH, 0:512].rearrange("c (h x) -> c h x", h=4), op=ALU.add),
                    reads=[ps, y], writes=[y])
            if lev < NLEV - 1:
                pn = Pp[lev % 2]
                for h0 in (0, 2, 4, 6):
                    ps = nsc()
                    for h in range(h0, h0 + 2):
                        o0 = (h - h0) * 2 * CH
                        B.op("pe", lambda e, ps=ps, h=h, o0=o0, Pm_=Pm_, PmT_=PmT_: e.matmul(
                            ps[0:CH, o0:o0 + CH], PmT_(h), Pm_(h), start=True, stop=True),
                            reads=[ptile], writes=[ps])
                        B.op("pe", lambda e, ps=ps, h=h, o0=o0, Pm_=Pm_, PmT_=PmT_: e.matmul(
                            ps[0:CH, o0 + CH:o0 + 2 * CH], Pm_(h), PmT_(h), start=True, stop=True),
                            reads=[ptile], writes=[ps])
                    B.op("act", lambda e, ps=ps, h0=h0, pn=pn: e.activation(
                        out=pn[:, h0:h0 + 2, :, :].rearrange("c h a x -> c h (a x)"),
                        in_=ps[0:CH, 0:4 * CH].rearrange("c (h x) -> c h x", h=2), func=AF.Copy),
                        reads=[ps], writes=[pn])
                pcur = pn
        ahf = AF_t[p2]
        for h0 in (0, 4):
            ps = nsc()
            for h in range(h0, h0 + 4):
                B.op("pe", lambda e, ps=ps, h=h, h0=h0: e.transpose(ps[0:64, (h - h0) * CH:(h - h0 + 1) * CH],
                                                                     y[:, h, 0:64], ident[0:CH, 0:CH]),
                     reads=[y, ident], writes=[ps])
            B.op("act", lambda e, ps=ps, h0=h0: e.activation(
                out=ahf[:, h0:h0 + 4, :], in_=ps[0:64, 0:4 * CH].rearrange("k (h x) -> k h x", h=4), func=AF.Copy),
                reads=[ps], writes=[ahf])
        gp = Gp[p2]
        ps = nsc()
        for h in range(NH):
            B.op("pe", lambda e, ps=ps, h=h: e.matmul(ps[0:64, h * 64:(h + 1) * 64], y[:, h, 0:64], tm[:, h, 128:192],
                                                      start=True, stop=True), reads=[y, tm], writes=[ps])
        B.op("act", lambda e, ps=ps: e.activation(out=gp[:].rearrange("k h x -> k (h x)"), in_=ps[0:64, 0:512], func=AF.Copy),
             reads=[ps], writes=[gp])
        So, Sn = S[ci % 2], S[(ci + 1) % 2]
        if not aug:
            ut, ot = UT[p2], OT[p2]
            ps = nsc()
            for h in range(NH):
                B.op("pe", lambda e, ps=ps, h=h, So=So: e.matmul(ps[0:CH, h * 64:(h + 1) * 64], ahf[:, h, :], So[:, h, :],
                                                          start=True, stop=True), reads=[ahf, So], writes=[ps])
            B.op("dve", lambda e, ps=ps: e.tensor_tensor(out=ut[:], in0=ps[0:CH, 0:512].rearrange("c (h x) -> c h x", h=NH),
                                                         in1=y[:, :, 64:128], op=ALU.add), reads=[ps, y], writes=[ut])
            ps = nsc()
            for h in range(NH):
                o = ps[0:CH, h * 64:(h + 1) * 64]
                B.op("pe", lambda e, o=o, h=h, So=So: e.matmul(o, ar[:, h, 1, :], So[:, h, :], start=True, stop=False),
                     reads=[ar, So], writes=[ps])
                B.op("pe", lambda e, o=o, h=h: e.matmul(o, scm[:, h, CH:2 * CH], ut[:, h, :], start=False, stop=False),
                     reads=[scm, ut], writes=[ps])
                B.op("pe", lambda e, o=o, h=h: e.matmul(o, scm[:, h, 3 * CH:4 * CH], tm[:, h, 64:128], start=False, stop=True),
                     reads=[scm, tm], writes=[ps])
            psO = ps
            ps = nsc()
            for h in range(NH):
                B.op("pe", lambda e, ps=ps, h=h: e.matmul(ps[0:CH, h:h + 1], rk[:, h, :], ones32[:, 0:1], start=True, stop=True),
                     reads=[rk, ones32], writes=[ps])
            B.op("act", lambda e, ps=ps: e.activation(out=sT[p2][:], in_=ps[0:CH, 0:NH], func=AF.Copy), reads=[ps], writes=[sT[p2]])
            s_ = st[p2]
            B.op("act", lambda e, psO=psO: e.activation(out=ot[:].rearrange("c h x -> c (h x)"), in_=psO[0:CH, 0:512], func=AF.Copy),
                 reads=[psO], writes=[ot])
            B.op("act", lambda e: e.activation(out=osq[:], in_=ot[:], func=AF.Square), reads=[ot], writes=[osq])
            B.op("dve", lambda e: e.reduce_sum(out=s_[:, 0:8], in_=ot[:], axis=AX.X), reads=[ot], writes=[s_])
            B.op("dve", lambda e: e.reduce_sum(out=s_[:, 8:16], in_=osq[:], axis=AX.X), reads=[osq, s_], writes=[s_])
            B.op("dve", lambda e: e.tensor_scalar(out=s_[:, 16:24], in0=s_[:, 0:8], scalar1=1.0 / 64, scalar2=None, op0=ALU.mult),
                 reads=[s_], writes=[s_])
            B.op("dve", lambda e: e.tensor_tensor(out=s_[:, 0:8], in0=s_[:, 16:24], in1=s_[:, 16:24], op=ALU.mult),
                 reads=[s_], writes=[s_])
            B.op("dve", lambda e: e.scalar_tensor_tensor(out=s_[:, 8:16], in0=s_[:, 8:16], scalar=1.0 / 64, in1=s_[:, 0:8],
                                                         op0=ALU.mult, op1=ALU.subtract), reads=[s_], writes=[s_])
            B.op("act", lambda e: e.activation(out=s_[:, 24:32], in_=s_[:, 8:16], func=AF.Sqrt, bias=gnc[0:CH, :]),
                 reads=[s_, gnc], writes=[s_])
            B.op("dve", lambda e: e.reciprocal(out=s_[:, 24:32], in_=s_[:, 24:32]), reads=[s_], writes=[s_])
            B.op("dve", lambda e: e.tensor_tensor(out=ot[:], in0=ot[:], in1=s_[:, 16:24].unsqueeze(2).to_broadcast([CH, NH, 64]),
                                                  op=ALU.subtract), reads=[ot, s_], writes=[ot])
            B.op("dve", lambda e: e.tensor_tensor(out=ot[:], in0=ot[:], in1=s_[:, 24:32].unsqueeze(2).to_broadcast([CH, NH, 64]),
                                                  op=ALU.mult), reads=[ot, s_], writes=[ot])
            otf = lambda: ot[:].rearrange("c h x -> c (h x)")
            B.op("dve", lambda e: e.tensor_tensor(out=otf(), in0=otf(), in1=lnw[0:CH, :], op=ALU.mult), reads=[ot, lnw], writes=[ot])
            B.op("dve", lambda e: e.tensor_tensor(out=otf(), in0=otf(), in1=lnb[0:CH, :], op=ALU.add), reads=[ot, lnb], writes=[ot])
            B.op("dve", lambda e: e.tensor_tensor(out=osq[:], in0=tm[:, :, 64:128],
                                                  in1=sT[p2][:, :].unsqueeze(2).to_broadcast([CH, NH, 64]), op=ALU.mult),
                 reads=[tm, sT[p2], osq], writes=[osq])
            B.op("dve", lambda e: e.tensor_tensor(out=ot[:], in0=ot[:], in1=osq[:], op=ALU.add), reads=[ot, osq], writes=[ot])
            gdt = ygd[p2]
            B.op("pool", lambda e: e.tensor_tensor(out=ws1[p2][:, 0, 0:CW - 1], in0=gdt[:, 0:CW - 1], in1=gdt[:, 1:CW], op=ALU.subtract),
                 reads=[gdt], writes=[ws1[p2]])
            B.op("dve", lambda e: e.scalar_tensor_tensor(out=gdt[:, 1:CW], in0=ws1[p2][:, 0, 0:CW - 1], scalar=c128[:, 0:1],
                                                         in1=gdt[:, 1:CW], op0=ALU.mult, op1=ALU.add),
                 reads=[ws1[p2], c128, gdt], writes=[gdt])
            B.op("act", lambda e: e.activation(out=gdb[p2][:], in_=gdt[:, own], func=AF.Sigmoid), reads=[gdt], writes=[gdb[p2]])
            ps = nsc()
            for pr in range(4):
                B.op("pe", lambda e, ps=ps, pr=pr: e.matmul(ps[:, pr * CH:(pr + 1) * CH], gupb[:, pr * 128:(pr + 1) * 128], gdb[p2][:],
                                                            start=True, stop=True), reads=[gupb, gdb[p2]], writes=[ps])
            B.op("act", lambda e, ps=ps: e.activation(out=gg[p2][:].rearrange("p a x -> p (a x)"), in_=ps[:, 0:4 * CH], func=AF.Copy),
                 reads=[ps], writes=[gg[p2]])
            ps = nsc()
            for pr in range(4):
                B.op("pe", lambda e, ps=ps, pr=pr: e.transpose(ps[:, pr * CH:(pr + 1) * CH],
                                                                ot[:, 2 * pr:2 * pr + 2, :].rearrange("c h x -> c (h x)"), ident[0:CH, 0:CH]),
                     reads=[ot, ident], writes=[ps])
            mot = mo[p2]
            B.op("dve", lambda e, ps=ps: e.tensor_tensor(out=mot[:, 0:4, :], in0=ps[:, 0:4 * CH].rearrange("p (a x) -> p a x", a=4),
                                                         in1=gg[p2][:], op=ALU.mult), reads=[ps, gg[p2]], writes=[mot])
            u = yu[p2]
            w1_, w2_ = ws1[p2], ws2[p2]
            B.op("pool", lambda e: e.tensor_tensor(out=w1_[:, :, 1:CW], in0=u[:, :, 1:CW], in1=u[:, :, 0:CW - 1], op=ALU.add),
                 reads=[u, w1_], writes=[w1_])
            B.op("pool", lambda e: e.tensor_tensor(out=w2_[:, 1:4, 3:CW], in0=w1_[:, 1:4, 3:CW], in1=w1_[:, 1:4, 1:CW - 2], op=ALU.add),
                 reads=[w1_], writes=[w2_])
            B.op("pool", lambda e: e.tensor_copy(out=w2_[:, 0, 1:CW], in_=w1_[:, 0, 1:CW]), reads=[w1_, w2_], writes=[w2_])
            B.op("pool", lambda e: e.tensor_tensor(out=w1_[:, 2:4, 7:CW], in0=w2_[:, 2:4, 7:CW], in1=w2_[:, 2:4, 3:CW - 4], op=ALU.add),
                 reads=[w2_, w1_], writes=[w1_])
            B.op("pool", lambda e: e.tensor_copy(out=w2_[:, 2, 7:CW], in_=w1_[:, 2, 7:CW]), reads=[w1_, w2_], writes=[w2_])
            B.op("pool", lambda e: e.tensor_tensor(out=w2_[:, 3, 15:CW], in0=w1_[:, 3, 15:CW], in1=w1_[:, 3, 7:CW - 8], op=ALU.add),
                 reads=[w1_, w2_], writes=[w2_])
            for gi in range(4):
                B.op("dve", lambda e, gi=gi: e.scalar_tensor_tensor(
                    out=w2_[:, gi, own], in0=w2_[:, gi, own], scalar=1.0 / (2 << gi), in1=u[:, gi, own],
                    op0=ALU.mult, op1=ALU.subtract), reads=[w2_, u], writes=[w2_])
            if ci == 0:
                B.op("pool", lambda e: e.tensor_tensor(out=w1_[:, :, 0:HALO], in0=w2_[:, :, HALO:2 * HALO], in1=u[:, :, HALO:2 * HALO],
                                                       op=ALU.add), reads=[w2_, u, w1_], writes=[w1_])
                B.op("pool", lambda e: e.tensor_tensor(out=w1_[:, :, 0:HALO], in0=w1_[:, :, 0:HALO], in1=corr[:], op=ALU.mult),
                     reads=[w1_, corr], writes=[w1_])
                B.op("pool", lambda e: e.tensor_tensor(out=w2_[:, :, HALO:2 * HALO], in0=w1_[:, :, 0:HALO], in1=u[:, :, HALO:2 * HALO],
                                                       op=ALU.subtract), reads=[w1_, u, w2_], writes=[w2_])
            B.op("pool", lambda e: e.tensor_copy(out=dpb[p2][:], in_=w2_[:, :, own]), reads=[w2_], writes=[dpb[p2]])
            ps = nsc()
            for gi in range(4):
                B.op("pe", lambda e, ps=ps, gi=gi: e.matmul(ps[:, gi * CH:(gi + 1) * CH], pwb[:, gi, :], dpb[p2][:, gi, :],
                                                            start=True, stop=True), reads=[pwb, dpb[p2]], writes=[ps])
            for gi in range(4):
                B.op("act", lambda e, ps=ps, gi=gi: e.activation(out=mot[:, 4 + gi, :], in_=ps[:, gi * CH:(gi + 1) * CH], func=AF.Identity,
                                                                 scale=c128[:, 5 + gi:6 + gi]), reads=[ps, c128, mot], writes=[mot])
            if DEBUG:
                B.dma("sp", dbg_d[:, :, ci * CH:(ci + 1) * CH].rearrange("c p t -> p c t"), mot[:], reads=[mot])
            mb = mblk[p2]
            hc = hck[0]
            for m in range(KC):
                ps = npj()
                wtile = wot[(m * 128) // 512]
                mo_ = (m * 128) % 512
                for kc in range(KC):
                    B.op("pe", lambda e, ps=ps, kc=kc, wtile=wtile, mo_=mo_: e.matmul(
                        ps[:, 0:CH], wtile[:, kc, mo_:mo_ + 128], mot[:, kc, :], start=(kc == 0), stop=(kc == KC - 1)),
                        reads=[wtile, mot], writes=[ps])
                B.op("act", lambda e, ps=ps, m=m: e.activation(out=mb[:, m, :], in_=ps[:, 0:CH], func=AF.Copy),
                     reads=[ps], writes=[mb])
            rstd = emit_rstd(P, lambda c: mb[:, c, :], [mb], width=CH)
            for c in range(KC):
                B.op("dve", lambda e, c=c, rstd=rstd: e.scalar_tensor_tensor(
                    out=mb[:, c, :], in0=mb[:, c, :], scalar=gcol[:, KC + c:KC + c + 1], in1=rstd[:, 0:CH],
                    op0=ALU.mult, op1=ALU.mult), reads=[mb, gcol, rstd], writes=[mb])
            B.op("pool", lambda e: e.tensor_tensor(out=hc[:], in0=hx[:, :, HALO:CW], in1=mb[:], op=ALU.add), reads=[hx, mb, hc], writes=[hc])
            B.dma("sp", o_d[:, :, ci * CH:(ci + 1) * CH].rearrange("c p t -> p c t"), hc[:], reads=[hc])
        halves = (0, 64) if aug else (0,)
        for h in range(NH):
            o = pst[0:64, h * 64:(h + 1) * 64]
            B.op("pe", lambda e, o=o, h=h: e.matmul(o, tm[:, h, 128:192], y[:, h, 64:128], start=True, stop=False),
                 reads=[tm, y], writes=[pst])
            B.op("pe", lambda e, o=o, h=h: e.matmul(o, tm[:, h, 192:256], tm[:, h, 64:128], start=False, stop=False),
                 reads=[tm], writes=[pst])
            B.op("pe", lambda e, o=o, h=h, So=So: e.matmul(o, gp[:, h, :], So[:, h, SW - 64:SW], start=False, stop=True),
                 reads=[gp, So], writes=[pst])
        wcb = C_[:, :, CH - 1:CH].to_broadcast([64, NH, 64])
        B.op("dve", lambda e, So=So, Sn=Sn: e.tensor_tensor(out=Sn[:, :, SW - 64:SW], in0=So[:, :, SW - 64:SW], in1=wcb, op=ALU.mult),
             reads=[So, C_, Sn], writes=[Sn])
        B.op("dve", lambda e, Sn=Sn: e.tensor_tensor(out=Sn[:, :, SW - 64:SW], in0=Sn[:, :, SW - 64:SW],
                                              in1=pst[0:64, 0:512].rearrange("k (h x) -> k h x", h=NH), op=ALU.add),
             reads=[Sn, pst], writes=[Sn])
        if aug:
            ps = nsc()
            for h in range(NH):
                B.op("pe", lambda e, ps=ps, h=h, So=So: e.matmul(ps[0:64, h * 64:(h + 1) * 64], gp[:, h, :], So[:, h, 0:64],
                                                          start=True, stop=True), reads=[gp, So], writes=[ps])
            B.op("dve", lambda e, So=So, Sn=Sn: e.tensor_tensor(out=Sn[:, :, 0:64], in0=So[:, :, 0:64], in1=wcb, op=ALU.mult),
                 reads=[So, C_, Sn], writes=[Sn])
            B.op("dve", lambda e, ps=ps, Sn=Sn: e.tensor_tensor(out=Sn[:, :, 0:64], in0=Sn[:, :, 0:64],
                                                         in1=ps[0:64, 0:512].rearrange("k (h x) -> k h x", h=NH), op=ALU.add),
                 reads=[Sn, ps], writes=[Sn])
    if aug:
        B.dma("sp", sum_d, S[NCH % 2][:], reads=[S[NCH % 2]])
    return P.finish()


def even_consts(p):
    def h64(v):
        return v.reshape(NH, 64).T
    mu = p["mu"]
    c64 = np.concatenate([h64(mu[0:512]), h64(mu[512:1024]), h64(mu[1024:1536]), h64(p["k_k"]), h64(p["k_a"]),
                          h64(p["r_k"].reshape(-1)), h64(p["w0"]), h64(p["a0"])], axis=1)
    c64b = np.stack([mu[1536:1600], mu[1600:1664]], axis=1)
    c128 = np.zeros((128, 9), np.float32)
    c128[:, 0] = mu[1664:1792]
    c128[:, 5:9] = p["pool_scale"].reshape(4, 128).T
    ln = np.stack([np.broadcast_to(p["ln_w"][None], (128, RW)), np.broadcast_to(p["ln_b"][None], (128, RW))], 0)
    r = np.arange(CH)
    su = (r[None, :] > r[:, None]).astype(np.float32)
    ui = (r[None, :] >= r[:, None]).astype(np.float32)
    sl = (r[None, :] < r[:, None]).astype(np.float32)
    msk = np.concatenate([su, ui, su, ui, sl], axis=1)
    out = dict(c64=c64, c64b=c64b, c128=c128, wup=p["w_up"], aup=p["a_up"], gup=p["g_up"], pw=p["pool_w"],
               ln=ln, msk=msk, ident=np.eye(128, dtype=np.float32))
    return {k: np.ascontiguousarray(v, dtype=np.float32) for k, v in out.items()}


def pool_corr(core):
    corr = np.ones((128, 4, HALO), np.float32)
    if core == 0:
        for gi in range(4):
            win = 2 << gi
            t = np.arange(HALO)
            corr[:, gi, :] = (win / np.minimum(t + 1, win)).astype(np.float32)[None]
    return corr


def run_even_mixer(h_shards, p, w_in, w_out, g0, g1):
    cst = even_consts(p)
    g = gcols(g0, g1)
    base = []
    for i in range(NCORES):
        hh = np.zeros((KC, 128, HALO), np.float32) if i == 0 else np.ascontiguousarray(h_shards[i - 1][:, :, T - HALO:])
        m = {"h": h_shards[i], "hh": hh, "w": w_in, "g": g, "corr": pool_corr(i)}
        m.update(cst)
        base.append(m)
    r1 = _run("e1", lambda: build_even(True), base)
    sums = np.ascontiguousarray(np.stack([r1[i]["summ"] for i in range(NCORES)], 0))
    in2 = []
    for i in range(NCORES):
        m = dict(base[i])
        oh = np.zeros((64, NCORES), np.float32)
        oh[:, i] = 1.0
        m.update({"wo": w_out, "sums": sums, "onehot": oh})
        in2.append(m)
    r2 = _run("e2", lambda: build_even(False), in2)
    if DEBUG:
        global _DBG
        _DBG = [r["dbg"] for r in r2]
    return [r["ho"] for r in r2]


def kernel(x, meta, norm_g, mlp_w1, mlp_w2, ev_w_in, ev_mu, ev_w0, ev_w_up, ev_a0, ev_a_up, ev_g_up,
           ev_k_k, ev_k_a, ev_r_k, ev_ln_w, ev_ln_b, ev_pool_w, ev_pool_scale, ev_w_out,
           od_w_in, od_lambda, od_subln_w, od_w_out):
    f = lambda a: np.ascontiguousarray(np.asarray(a), dtype=np.float32)
    x = f(x)
    h = np.concatenate([f(meta), x[0]], axis=0)
    shards = [to_fm(h[c * T:(c + 1) * T]) for c in range(NCORES)]
    norm_g = f(norm_g)
    for i in range(4):
        j = i // 2
        g = norm_g[i]
        if i % 2 == 0:
            p = dict(mu=f(ev_mu[j]), w0=f(ev_w0[j]), w_up=f(ev_w_up[j]), a0=f(ev_a0[j]), a_up=f(ev_a_up[j]),
                     g_up=f(ev_g_up[j]), k_k=f(ev_k_k[j]), k_a=f(ev_k_a[j]), r_k=f(ev_r_k[j]), ln_w=f(ev_ln_w[j]),
                     ln_b=f(ev_ln_b[j]), pool_w=f(ev_pool_w[j]), pool_scale=f(ev_pool_scale[j]))
            shards = run_even_mixer(shards, p, f(ev_w_in[j]), f(ev_w_out[j]), g[0], g[1])
        else:
            shards = run_odd_mixer(shards, f(od_w_in[j]), f(od_lambda[j]), f(od_subln_w[j]), f(od_w_out[j]),
                                   g[0], g[1], i)
        shards = run_mlp(shards, f(mlp_w1[i]), f(mlp_w2[i]), g[2], g[3])
    hfull = np.concatenate([from_fm(s) for s in shards], axis=0)
    return np.ascontiguousarray(hfull[16:][None]).astype(np.float32)
```

```python
import math
from contextlib import ExitStack

import numpy as np
import concourse.bass as bass
import concourse.mybir as mybir
from concourse.bass_utils import run_bass_kernel_spmd

F32 = mybir.dt.float32
BF16 = mybir.dt.bfloat16
AF = mybir.ActivationFunctionType
ALU = mybir.AluOpType
AX = mybir.AxisListType

NCORES = 8
D = 1024
KC = 8
L = 16400
T = L // NCORES
TB = 410
NB = T // TB
DFF = 4096
RMS_EPS = 1e-6

N_DMA_SLOTS = 6


class Tile:
    __slots__ = ("t", "w", "r", "name")

    def __init__(self, t, name=""):
        self.t = t
        self.w = None
        self.r = {}
        self.name = name

    def __getitem__(self, idx):
        return self.t[idx]


class Builder:
    ENGS = ("pe", "dve", "act", "pool", "sp")

    def __init__(self, nc):
        self.nc = nc
        self.cnt = {e: 0 for e in self.ENGS}
        self.sem = {e: nc.alloc_semaphore(name=f"c_{e}") for e in self.ENGS}
        self.seen = {e: {f: 0 for f in self.ENGS} for e in self.ENGS}
        self.rec = {e: [] for e in self.ENGS}
        self.dslots = {}
        for e in ("sp", "act", "pool"):
            self.dslots[e] = [dict(sem=nc.alloc_semaphore(name=f"d_{e}{i}"), val=0)
                              for i in range(N_DMA_SLOTS)]
        self.dnext = {e: 0 for e in ("sp", "act", "pool")}
        self.dseen = {e: {} for e in self.ENGS}

    def _need(self, e, dep, waits):
        if dep is None:
            return
        if dep[0] == "c":
            _, f, n = dep
            if f == "pe" and e == "pe":
                return
            if self.seen[e][f] >= n:
                return
            self.seen[e][f] = n
            waits[("c", f)] = max(waits.get(("c", f), 0), n)
        else:
            _, f, s, v = dep
            key = (f, s)
            if self.dseen[e].get(key, 0) >= v:
                return
            self.dseen[e][key] = v
            waits[("d", f, s)] = max(waits.get(("d", f, s), 0), v)

    def _deps(self, e, reads, writes, waits):
        for t in reads:
            self._need(e, t.w, waits)
        for t in writes:
            self._need(e, t.w, waits)
            for d in t.r.values():
                self._need(e, d, waits)

    def op(self, e, fn, reads=(), writes=()):
        waits = {}
        self._deps(e, reads, writes, waits)
        self.cnt[e] += 1
        me = ("c", e, self.cnt[e])
        for t in reads:
            t.r[e] = me
        for t in writes:
            t.w = me
            t.r = {}
        self.rec[e].append((waits, fn, ("c", e)))
        return me

    def dma(self, e, out_ap, in_ap, reads=(), writes=(), **kw):
        waits = {}
        self._deps(e, reads, writes, waits)
        si = self.dnext[e]
        self.dnext[e] = (si + 1) % N_DMA_SLOTS
        slot = self.dslots[e][si]
        if slot["val"] > 0:
            self._need(e, ("d", e, si, slot["val"]), waits)
        slot["val"] += 16
        me = ("d", e, si, slot["val"])
        for t in reads:
            t.r[("dma", e, si)] = me
        for t in writes:
            t.w = me
            t.r = {}

        def fn(eng, out_ap=out_ap, in_ap=in_ap, kw=kw):
            return eng.dma_start(out=out_ap, in_=in_ap, **kw)
        self.rec[e].append((waits, fn, ("d", e, si)))
        return me

    def wait_all(self, e):
        waits = {}
        for f in self.ENGS:
            if f != e and self.cnt[f] > 0:
                self._need(e, ("c", f, self.cnt[f]), waits)
        for f in self.dslots:
            for si, s in enumerate(self.dslots[f]):
                if s["val"] > 0:
                    self._need(e, ("d", f, si, s["val"]), waits)
        self.rec[e].append((waits, None, None))

    def emit(self):
        nc = self.nc
        with nc.Block() as block:
            for e in self.ENGS:
                recs = self.rec[e]
                if not recs:
                    continue
                deco = {"pe": block.tensor, "dve": block.vector, "act": block.scalar,
                        "pool": block.gpsimd, "sp": block.sync}[e]

                def body(eng, recs=recs):
                    for waits, fn, inc in recs:
                        for k, v in waits.items():
                            if k[0] == "c":
                                eng.wait_ge(self.sem[k[1]], v)
                            else:
                                eng.wait_ge(self.dslots[k[1]][k[2]]["sem"], v)
                        if fn is None:
                            continue
                        ins = fn(eng)
                        if inc[0] == "c":
                            ins.then_inc(self.sem[inc[1]], 1)
                        else:
                            ins.then_inc(self.dslots[inc[1]][inc[2]]["sem"], 16)
                deco(body)


class Prog:
    def __init__(self):
        self.nc = bass.Bass("TRN2", target_bir_lowering=False)
        self.es = ExitStack()
        self.es.__enter__()
        self.B = Builder(self.nc)
        self.n = 0
        self._rr = 0

    def dram_in(self, name, shape, dt=F32):
        return self.nc.dram_tensor(name, list(shape), dt, kind="ExternalInput").ap()

    def dram_out(self, name, shape, dt=F32):
        return self.nc.dram_tensor(name, list(shape), dt, kind="ExternalOutput").ap()

    def sb_raw(self, shape, dt, name=None):
        self.n += 1
        name = "s_" + (name or f"sb{self.n}")
        return self.es.enter_context(self.nc.sbuf_tensor(name, list(shape), dt))

    def sb(self, shape, dt, name=None):
        t = self.sb_raw(shape, dt, name)
        return Tile(t, name or "")

    def ps(self, shape=(128, 512), dt=F32, name=None):
        self.n += 1
        name = "p_" + (name or f"ps{self.n}")
        return Tile(self.es.enter_context(self.nc.psum_tensor(name, list(shape), dt)), name)

    def finish(self):
        self.B.wait_all("sp")
        self.B.emit()
        self.es.__exit__(None, None, None)
        return self.nc


def emit_consts(P, norm=True, nbuf=2, sqw=TB):
    B = P.B
    ones = P.sb([128, 128], BF16, "ones_bf")
    B.op("pool", lambda e: e.memset(ones[:], 1.0), writes=[ones])
    P.ones = ones
    if not norm:
        return
    P.ps_ss = P.ps(name="ps_ss")
    P.sqt = [P.sb([128, KC, sqw], BF16, f"sq{i}") for i in range(nbuf)]
    P.rstd = [P.sb([128, sqw], F32, f"rstd{i}") for i in range(nbuf)]
    P._nbuf = nbuf
    P._sqi = 0


def emit_rstd(P, src_fn, src_tiles, nchunk=KC, width=TB, dim=D, eps=RMS_EPS):
    B = P.B
    i = P._sqi
    P._sqi = (P._sqi + 1) % P._nbuf
    sq, rstd, pss, ones = P.sqt[i], P.rstd[i], P.ps_ss, P.ones
    epsc = P.eps_col(eps)
    for c in range(nchunk):
        B.op("act", lambda e, c=c: e.activation(out=sq[:, c, 0:width], in_=src_fn(c), func=AF.Square),
             reads=src_tiles, writes=[sq])
    for c in range(nchunk):
        B.op("pe", lambda e, c=c: e.matmul(pss[:, 0:width], ones[:], sq[:, c, 0:width],
                                           start=(c == 0), stop=(c == nchunk - 1)),
             reads=[ones, sq], writes=[pss])
    B.op("act", lambda e: e.activation(out=rstd[:, 0:width], in_=pss[:, 0:width], func=AF.Sqrt,
                                       scale=1.0 / dim, bias=epsc[:]),
         reads=[pss, epsc], writes=[rstd])
    B.op("dve", lambda e: e.reciprocal(out=rstd[:, 0:width], in_=rstd[:, 0:width]),
         reads=[rstd], writes=[rstd])
    return rstd


def _eps_col(P, eps):
    if not hasattr(P, "_eps"):
        P._eps = {}
    if eps not in P._eps:
        t = P.sb([128, 1], F32, f"eps{len(P._eps)}")
        P.B.op("pool", lambda e: e.memset(t[:], float(eps)), writes=[t])
        P._eps[eps] = t
    return P._eps[eps]


Prog.eps_col = _eps_col


def load_gcols(P, ap, ncol, name):
    t = P.sb([128, ncol], F32, name)
    P.B.dma("sp", t[:], ap, writes=[t])
    return t


class WStream:
    def __init__(self, P, nstage=3, stage_elems=2048):
        self.P = P
        self.stage = [P.sb([128, stage_elems], F32, f"wst{i}") for i in range(nstage)]
        self.i = 0
        self.stage_elems = stage_elems
        self.cast_rr = 0

    def load(self, dst_tile, dst_ap_fn, src_ap, shape, scale_col=None, queue="sp"):
        P, B = self.P, self.P.B
        st = self.stage[self.i]
        self.i = (self.i + 1) % len(self.stage)
        n = int(np.prod(shape[1:]))
        assert n <= self.stage_elems
        if len(shape) == 3:
            sview = st.t[0:shape[0], 0:n].rearrange("p (a b) -> p a b", a=shape[1])
        else:
            sview = st.t[0:shape[0], 0:n]
        B.dma(queue, sview, src_ap, writes=[st])
        eng = ("pool", "dve")[self.cast_rr % 2] if False else "pool"
        self.cast_rr += 1
        B.op(eng, lambda e: e.tensor_copy(out=dst_ap_fn(), in_=sview), reads=[st], writes=[dst_tile])


def build_mlp():
    P = Prog()
    B = P.B
    h_d = P.dram_in("h", [KC, 128, T])
    w1_d = P.dram_in("w1", [D, DFF])
    w2_d = P.dram_in("w2", [DFF, D])
    g_d = P.dram_in("g", [128, 2 * KC])
    o_d = P.dram_out("ho", [KC, 128, T])
    emit_consts(P)
    gcol = load_gcols(P, g_d, 2 * KC, "gcol")

    hf_t = P.sb_raw([128, KC, T], F32, "hf")
    hf = [Tile(hf_t, f"hf{n}") for n in range(NB)]
    hn_t = P.sb_raw([128, KC, T], BF16, "hn")
    hn = [Tile(hn_t, f"hn{n}") for n in range(NB)]

    def blk(n):
        return slice(n * TB, (n + 1) * TB)

    for n in range(NB):
        for c in range(KC):
            B.dma("sp" if c % 2 == 0 else "act", hf_t[:, c, blk(n)], h_d[c, :, blk(n)], writes=[hf[n]])
    for n in range(NB):
        rstd = emit_rstd(P, lambda c, n=n: hf_t[:, c, blk(n)], [hf[n]])
        for c in range(KC):
            B.op("dve", lambda e, c=c, n=n, rstd=rstd: e.scalar_tensor_tensor(
                out=hn_t[:, c, blk(n)], in0=hf_t[:, c, blk(n)], scalar=gcol[:, c:c + 1],
                in1=rstd[:, 0:TB], op0=ALU.mult, op1=ALU.mult),
                reads=[hf[n], gcol, rstd], writes=[hn[n]])

    FG = 512
    NG = DFF // FG
    ws = WStream(P)
    w1b = [P.sb([128, KC, FG], BF16, f"w1b{i}") for i in range(2)]
    w2b = [P.sb([128, FG // 128, D], BF16, f"w2b{i}") for i in range(2)]
    h1 = [P.sb([128, FG // 128, TB], BF16, f"h1_{i}") for i in range(2)]
    rl = [P.sb([128, TB], F32, f"rl{i}") for i in range(2)]
    pha = [P.ps(name=f"ph{i}") for i in range(3)]
    pfa = [P.ps(name=f"pf{i}") for i in range(3)]
    w1v = w1_d.rearrange("(kc p) n -> p kc n", p=128)
    w2v = w2_d.rearrange("(fc p) n -> p fc n", p=128)
    ih = 0
    ipf = 0
    it = 0
    for j in range(NG):
        wb1, wb2 = w1b[j % 2], w2b[j % 2]
        for half in range(2):
            ks = slice(half * 4, half * 4 + 4)
            ws.load(wb1, lambda wb1=wb1, ks=ks: wb1[:, ks, :], w1v[:, ks, j * FG:(j + 1) * FG], [128, 4, FG])
        for half in range(2):
            fs = slice(half * 2, half * 2 + 2)
            ws.load(wb2, lambda wb2=wb2, fs=fs: wb2[:, fs, :],
                    w2v[:, j * 4 + half * 2: j * 4 + half * 2 + 2, :], [128, 2, D])
        for n in range(NB):
            h1t = h1[it % 2]
            it += 1
            for fc in range(FG // 128):
                ph = pha[ih % 3]
                r = rl[ih % 2]
                ih += 1
                for kc in range(KC):
                    B.op("pe", lambda e, ph=ph, wb1=wb1, kc=kc, fc=fc, n=n: e.matmul(
                        ph[:, 0:TB], wb1[:, kc, fc * 128:(fc + 1) * 128], hn_t[:, kc, blk(n)],
                        start=(kc == 0), stop=(kc == KC - 1)),
                        reads=[wb1, hn[n]], writes=[ph])
                B.op("act", lambda e, ph=ph, r=r: e.activation(out=r[:, :], in_=ph[:, 0:TB], func=AF.Relu),
                     reads=[ph], writes=[r])
                B.op("pool", lambda e, r=r, h1t=h1t, fc=fc: e.tensor_tensor(
                    out=h1t[:, fc, :], in0=r[:, :], in1=r[:, :], op=ALU.mult),
                    reads=[r], writes=[h1t])
            for m in range(KC):
                pf = pfa[ipf % 3]
                ipf += 1
                nfc = FG // 128
                for fc in range(nfc):
                    B.op("pe", lambda e, pf=pf, wb2=wb2, fc=fc, m=m, h1t=h1t: e.matmul(
                        pf[:, 0:TB], wb2[:, fc, m * 128:(m + 1) * 128], h1t[:, fc, :],
                        start=(fc == 0), stop=(fc == nfc - 1)),
                        reads=[wb2, h1t], writes=[pf])
                if j == 0:
                    B.op("act", lambda e, pf=pf, m=m, n=n: e.activation(
                        out=hf_t[:, m, blk(n)], in_=pf[:, 0:TB], func=AF.Copy),
                        reads=[pf], writes=[hf[n]])
                else:
                    B.op("dve", lambda e, pf=pf, m=m, n=n: e.tensor_tensor(
                        out=hf_t[:, m, blk(n)], in0=hf_t[:, m, blk(n)], in1=pf[:, 0:TB], op=ALU.add),
                        reads=[pf, hf[n]], writes=[hf[n]])
    hb = [P.sb([128, KC, TB], F32, f"hb{i}") for i in range(2)]
    for n in range(NB):
        hbt = hb[n % 2]
        for c in range(KC):
            B.dma("act", hbt[:, c, :], h_d[c, :, blk(n)], writes=[hbt])
        rstd = emit_rstd(P, lambda c, n=n: hf_t[:, c, blk(n)], [hf[n]])
        for c in range(KC):
            B.op("dve", lambda e, c=c, n=n, rstd=rstd: e.scalar_tensor_tensor(
                out=hf_t[:, c, blk(n)], in0=hf_t[:, c, blk(n)], scalar=gcol[:, KC + c:KC + c + 1],
                in1=rstd[:, 0:TB], op0=ALU.mult, op1=ALU.mult),
                reads=[hf[n], gcol, rstd], writes=[hf[n]])
            B.op("pool", lambda e, c=c, n=n, hbt=hbt: e.tensor_tensor(
                out=hbt[:, c, :], in0=hbt[:, c, :], in1=hf_t[:, c, blk(n)], op=ALU.add),
                reads=[hf[n], hbt], writes=[hbt])
        for c in range(KC):
            B.dma("sp", o_d[c, :, blk(n)], hbt[:, c, :], reads=[hbt])
    return P.finish()


def to_fm(h_tok):
    return np.ascontiguousarray(h_tok.T.reshape(KC, 128, h_tok.shape[0]))


def from_fm(h_fm):
    return np.ascontiguousarray(h_fm.reshape(D, h_fm.shape[-1]).T)


def gcols(*vecs):
    return np.ascontiguousarray(np.concatenate([v.reshape(KC, 128).T for v in vecs], axis=1))


_NC_CACHE = {}


def _get(name, builder):
    if name not in _NC_CACHE:
        _NC_CACHE[name] = builder()
    return _NC_CACHE[name]


def run_mlp(h_shards, w1, w2, g2, g3):
    nc = _get("mlp", build_mlp)
    g = gcols(g2, g3)
    in_maps = [{"h": h_shards[i], "w1": w1, "w2": w2, "g": g} for i in range(NCORES)]
    res = run_bass_kernel_spmd(nc, in_maps, core_ids=list(range(NCORES)))
    return [r["ho"] for r in res.results]


def blk(n):
    return slice(n * TB, (n + 1) * TB)


def load_h(P, h_d, name="hf"):
    hf_t = P.sb_raw([128, KC, T], F32, name)
    hf = [Tile(hf_t, f"{name}{n}") for n in range(NB)]
    for n in range(NB):
        for c in range(KC):
            P.B.dma("sp" if c % 2 == 0 else "act", hf_t[:, c, blk(n)], h_d[c, :, blk(n)], writes=[hf[n]])
    return hf_t, hf


def emit_norm_bf16(P, hf_t, hf, gcol, gofs, name="hn"):
    hn_t = P.sb_raw([128, KC, T], BF16, name)
    hn = [Tile(hn_t, f"{name}{n}") for n in range(NB)]
    for n in range(NB):
        rstd = emit_rstd(P, lambda c, n=n: hf_t[:, c, blk(n)], [hf[n]])
        for c in range(KC):
            P.B.op("dve", lambda e, c=c, n=n, rstd=rstd: e.scalar_tensor_tensor(
                out=hn_t[:, c, blk(n)], in0=hf_t[:, c, blk(n)], scalar=gcol[:, gofs + c:gofs + c + 1],
                in1=rstd[:, 0:TB], op0=ALU.mult, op1=ALU.mult),
                reads=[hf[n], gcol, rstd], writes=[hn[n]])
    return hn_t, hn


def emit_norm_from_dram(P, h_d, gcol, gofs, name="hn"):
    hn_t = P.sb_raw([128, KC, T], BF16, name)
    hn = [Tile(hn_t, f"{name}{n}") for n in range(NB)]
    hb = [P.sb([128, KC, TB], F32, f"{name}_hb{i}") for i in range(2)]
    for n in range(NB):
        hbt = hb[n % 2]
        for c in range(KC):
            P.B.dma("sp" if c % 2 == 0 else "act", hbt[:, c, :], h_d[c, :, blk(n)], writes=[hbt])
        rstd = emit_rstd(P, lambda c, hbt=hbt: hbt[:, c, :], [hbt])
        for c in range(KC):
            P.B.op("dve", lambda e, c=c, n=n, rstd=rstd, hbt=hbt: e.scalar_tensor_tensor(
                out=hn_t[:, c, blk(n)], in0=hbt[:, c, :], scalar=gcol[:, gofs + c:gofs + c + 1],
                in1=rstd[:, 0:TB], op0=ALU.mult, op1=ALU.mult),
                reads=[hbt, gcol, rstd], writes=[hn[n]])
    return hn_t, hn


def load_weight_resident(P, ws, w_d, ncols, name, group=512, kchunks=KC):
    wv = w_d.rearrange("(kc p) n -> p kc n", p=128)
    tiles = []
    per = max(1, ws.stage_elems // group)
    for g0 in range(0, ncols, group):
        gw = min(group, ncols - g0)
        wt = P.sb([128, kchunks, gw], BF16, f"{name}{g0}")
        for k0 in range(0, kchunks, per):
            k1 = min(kchunks, k0 + per)
            ws.load(wt, lambda wt=wt, k0=k0, k1=k1: wt[:, k0:k1, :], wv[:, k0:k1, g0:g0 + gw],
                    [128, k1 - k0, gw])
        tiles.append(wt)
    return tiles


def emit_linear(P, wtiles, group, m, x_t, x_tile, n, ps, kchunks=KC):
    wt = wtiles[(m * 128) // group]
    mo = (m * 128) % group
    for kc in range(kchunks):
        P.B.op("pe", lambda e, kc=kc: e.matmul(ps[:, 0:TB], wt[:, kc, mo:mo + 128], x_t[:, kc, blk(n)],
                                               start=(kc == 0), stop=(kc == kchunks - 1)),
               reads=[wt, x_tile], writes=[ps])


def build_oa():
    P = Prog()
    B = P.B
    h_d = P.dram_in("h", [KC, 128, T])
    w_d = P.dram_in("w", [D, 3 * D])
    g_d = P.dram_in("g", [128, KC])
    cs_d = P.dram_in("cs", [2, 128, T])
    pm_d = P.dram_in("perm", [128, 128])
    q_d = P.dram_out("q", [KC, 128, T], BF16)
    k_d = P.dram_out("k", [KC, 128, T], BF16)
    v_d = P.dram_out("v", [KC, 128, T], BF16)
    emit_consts(P)
    gcol = load_gcols(P, g_d, KC, "gcol")
    perm = P.sb([128, 128], F32, "perm")
    B.dma("sp", perm[:], pm_d, writes=[perm])
    cos = P.sb([128, T], F32, "cos")
    sin = P.sb([128, T], F32, "sin")
    B.dma("sp", cos[:], cs_d[0], writes=[cos])
    B.dma("act", sin[:], cs_d[1], writes=[sin])
    ws = WStream(P)
    wt = load_weight_resident(P, ws, w_d, 3 * D, "win")
    hn_t, hn = emit_norm_from_dram(P, h_d, gcol, 0)
    psa = [P.ps(name=f"pa{i}") for i in range(3)]
    psr = [P.ps(name=f"pr{i}") for i in range(2)]
    qf = [P.sb([128, TB], F32, f"qf{i}") for i in range(2)]
    t1 = [P.sb([128, TB], F32, f"t1{i}") for i in range(2)]
    t2 = [P.sb([128, TB], F32, f"t2{i}") for i in range(2)]
    ob = [P.sb([128, TB], BF16, f"ob{i}") for i in range(3)]
    ia = 0
    io = 0
    for n in range(NB):
        for m in range(3 * KC):
            ps = psa[ia % 3]
            emit_linear(P, wt, 512, m, hn_t, hn[n], n, ps)
            o = ob[io % 3]
            io += 1
            if m < 2 * KC:
                f, a, b, pr = qf[ia % 2], t1[ia % 2], t2[ia % 2], psr[ia % 2]
                B.op("act", lambda e, f=f, ps=ps: e.activation(out=f[:], in_=ps[:, 0:TB], func=AF.Copy),
                     reads=[ps], writes=[f])
                B.op("pe", lambda e, f=f, pr=pr: e.matmul(pr[:, 0:TB], perm[:], f[:], start=True, stop=True),
                     reads=[perm, f], writes=[pr])
                B.op("dve", lambda e, f=f, a=a, n=n: e.tensor_tensor(out=a[:], in0=f[:], in1=cos[:, blk(n)], op=ALU.mult),
                     reads=[f, cos], writes=[a])
                B.op("dve", lambda e, b=b, pr=pr, n=n: e.tensor_tensor(out=b[:], in0=pr[:, 0:TB], in1=sin[:, blk(n)], op=ALU.mult),
                     reads=[pr, sin], writes=[b])
                B.op("pool", lambda e, a=a, b=b, o=o: e.tensor_tensor(out=o[:], in0=a[:], in1=b[:], op=ALU.add),
                     reads=[a, b], writes=[o])
                dst = (q_d if m < KC else k_d)[m % KC, :, blk(n)]
            else:
                B.op("act", lambda e, o=o, ps=ps: e.activation(out=o[:], in_=ps[:, 0:TB], func=AF.Copy),
                     reads=[ps], writes=[o])
                dst = v_d[m % KC, :, blk(n)]
            ia += 1
            B.dma("sp", dst, o[:], reads=[o])
    return P.finish()


def rope_consts(pos0, npos):
    dh = 64
    inv = (np.float32(10000.0) ** (-np.arange(0, dh, 2, dtype=np.float32) / np.float32(dh))).astype(np.float32)
    pos = np.arange(pos0, pos0 + npos, dtype=np.float32)
    ang = (pos[:, None] * inv[None, :]).astype(np.float32)
    ang = np.concatenate([ang, ang], axis=-1)
    c = np.cos(ang).astype(np.float32).T
    s = np.sin(ang).astype(np.float32).T
    sgn = np.concatenate([-np.ones(32, np.float32), np.ones(32, np.float32)])[:, None]
    ss = s * sgn
    return np.ascontiguousarray(np.stack([np.concatenate([c, c], 0), np.concatenate([ss, ss], 0)], 0))


def rope_perm():
    pm = np.zeros((128, 128), np.float32)
    for b in (0, 64):
        for d in range(64):
            src = d + 32 if d < 32 else d - 32
            pm[b + src, b + d] = 1.0
    return pm


LP = 16512
NKB = LP // 128
SUBLN_EPS = 1e-5


def build_ob(lam_init):
    P = Prog()
    B = P.B
    q_d = P.dram_in("q", [128, LP], BF16)
    k_d = P.dram_in("k", [128, LP], BF16)
    v_d = P.dram_in("v", [128, LP], BF16)
    lv_d = P.dram_in("lv", [128, 256])
    sw_d = P.dram_in("sw", [128, 1])
    id_d = P.dram_in("ident", [128, 128], BF16)
    tri_d = P.dram_in("tri", [128, 128], BF16)
    o_d = P.dram_out("o", [128, LP], BF16)
    emit_consts(P, norm=False)
    ones = P.ones
    qT = P.sb([128, LP], BF16, "qT")
    kT = P.sb([128, LP], BF16, "kT")
    vT = P.sb([128, LP], BF16, "vT")
    for i, (t, d) in enumerate(((qT, q_d), (kT, k_d), (vT, v_d))):
        B.dma(("sp", "act", "pool")[i], t[:], d, writes=[t])
    ident = P.sb([128, 128], BF16, "ident")
    tri = P.sb([128, 128], BF16, "tri")
    B.dma("sp", ident[:], id_d, writes=[ident])
    B.dma("sp", tri[:], tri_d, writes=[tri])
    lv = P.sb([128, 256], F32, "lv")
    sw = P.sb([128, 1], F32, "sw")
    B.dma("sp", lv[:], lv_d, writes=[lv])
    B.dma("sp", sw[:], sw_d, writes=[sw])
    lp = P.sb([128, 128], F32, "lp")
    ls = P.sb([128, 2], F32, "ls")
    nlam = P.sb([128, 1], F32, "nlam")
    swc = P.sb([128, 1], F32, "swc")
    B.op("dve", lambda e: e.tensor_tensor(out=lp[:, 0:64], in0=lv[:, 0:64], in1=lv[:, 64:128], op=ALU.mult),
         reads=[lv], writes=[lp])
    B.op("dve", lambda e: e.tensor_tensor(out=lp[:, 64:128], in0=lv[:, 128:192], in1=lv[:, 192:256], op=ALU.mult),
         reads=[lv, lp], writes=[lp])
    B.op("dve", lambda e: e.reduce_sum(out=ls[:, 0:1], in_=lp[:, 0:64], axis=AX.X), reads=[lp], writes=[ls])
    B.op("dve", lambda e: e.reduce_sum(out=ls[:, 1:2], in_=lp[:, 64:128], axis=AX.X), reads=[lp, ls], writes=[ls])
    B.op("act", lambda e: e.activation(out=ls[:], in_=ls[:], func=AF.Exp), reads=[ls], writes=[ls])
    B.op("dve", lambda e: e.scalar_tensor_tensor(out=nlam[:], in0=ls[:, 1:2], scalar=-float(lam_init),
                                                 in1=ls[:, 0:1], op0=ALU.add, op1=ALU.subtract),
         reads=[ls], writes=[nlam])
    B.op("dve", lambda e: e.tensor_scalar(out=swc[:], in0=sw[:], scalar1=float(1.0 - lam_init), scalar2=None,
                                          op0=ALU.mult), reads=[sw], writes=[swc])
    V = P.sb([128, NKB, 128], BF16, "V")
    ptr = P.ps([128, 1024], BF16, name="ptr")
    for j0 in range(0, NKB, 8):
        j1 = min(NKB, j0 + 8)
        for j in range(j0, j1):
            B.op("pe", lambda e, j=j, j0=j0: e.transpose(ptr[:, (j - j0) * 128:(j - j0 + 1) * 128],
                                                          vT[:, j * 128:(j + 1) * 128], ident[:]),
                 reads=[vT, ident], writes=[ptr])
        B.op("dve", lambda e, j0=j0, j1=j1: e.tensor_copy(
            out=V[:, j0:j1, :], in_=ptr[:, 0:(j1 - j0) * 128].rearrange("p (a b) -> p a b", b=128)),
            reads=[ptr], writes=[V])
    acc = [[P.ps(name=f"O{c}"), P.ps(name=f"s{c}")] for c in range(2)]
    psS = [P.ps(name=f"S{i}") for i in range(3)]
    pts = [P.sb([128, 512], BF16, f"pt{i}") for i in range(4)]
    rr = [P.sb([128, 512], F32, f"rr{i}") for i in range(2)]
    oo = [P.sb([128, 512], F32, f"oo{i}") for i in range(2)]
    osq = P.sb([128, 512], BF16, "osq")
    rs = P.sb([128, 512], F32, "rs")
    outb = [P.sb([128, 512], BF16, f"outb{i}") for i in range(2)]
    epsc = P.eps_col(SUBLN_EPS)
    _iS = [0]
    nqt = (LP + 511) // 512
    def finalize(i, q0, qw):
        for c in range(2):
            O, s = acc[c]
            B.op("dve", lambda e, s=s, c=c, qw=qw: e.reciprocal(out=rr[c][:, 0:qw], in_=s[:, 0:qw]),
                 reads=[s], writes=[rr[c]])
            B.op("dve", lambda e, O=O, c=c, qw=qw: e.tensor_tensor(
                out=oo[c][:, 0:qw], in0=O[:, 0:qw], in1=rr[c][:, 0:qw], op=ALU.mult),
                reads=[O, rr[c]], writes=[oo[c]])
        B.op("dve", lambda e, qw=qw: e.scalar_tensor_tensor(
            out=oo[0][:, 0:qw], in0=oo[1][:, 0:qw], scalar=nlam[:, 0:1], in1=oo[0][:, 0:qw],
            op0=ALU.mult, op1=ALU.add), reads=[oo[0], oo[1], nlam], writes=[oo[0]])
        B.op("act", lambda e, qw=qw: e.activation(out=osq[:, 0:qw], in_=oo[0][:, 0:qw], func=AF.Square),
             reads=[oo[0]], writes=[osq])
        ps = psS[_iS[0] % 3]
        _iS[0] += 1
        B.op("pe", lambda e, ps=ps, qw=qw: e.matmul(ps[:, 0:qw], ones[:], osq[:, 0:qw], start=True, stop=True),
             reads=[ones, osq], writes=[ps])
        B.op("act", lambda e, ps=ps, qw=qw: e.activation(out=rs[:, 0:qw], in_=ps[:, 0:qw], func=AF.Sqrt,
                                                         scale=1.0 / 128, bias=epsc[:]),
             reads=[ps, epsc], writes=[rs])
        B.op("dve", lambda e, qw=qw: e.reciprocal(out=rs[:, 0:qw], in_=rs[:, 0:qw]), reads=[rs], writes=[rs])
        ot = outb[i % 2]
        B.op("dve", lambda e, ot=ot, qw=qw: e.scalar_tensor_tensor(
            out=ot[:, 0:qw], in0=oo[0][:, 0:qw], scalar=swc[:, 0:1], in1=rs[:, 0:qw],
            op0=ALU.mult, op1=ALU.mult), reads=[oo[0], swc, rs], writes=[ot])
        B.dma("sp", o_d[:, q0:q0 + qw], ot[:, 0:qw], reads=[ot])

    units = []
    for i in range(nqt):
        q0 = i * 512
        qw = min(512, LP - q0)
        nqb = qw // 128
        nk = 4 * i + nqb
        for j in range(nk):
            a = j - 4 * i
            c0 = max(0, a) * 128
            for c in range(2):
                units.append(dict(i=i, j=j, c=c, a=a, c0=c0, q0=q0, qw=qw, nk=nk, last=(j == nk - 1 and c == 1)))

    def stage1(u, idx):
        ps = psS[_iS[0] % 3]
        _iS[0] += 1
        pt = pts[idx % 4]
        u["pt"] = pt
        c, j, c0, q0, qw, a = u["c"], u["j"], u["c0"], u["q0"], u["qw"], u["a"]
        rows = slice(c * 64, (c + 1) * 64)
        B.op("pe", lambda e: e.matmul(ps[:, c0:qw], kT[rows, j * 128:(j + 1) * 128], qT[rows, q0 + c0:q0 + qw],
                                      start=True, stop=True), reads=[kT, qT], writes=[ps])
        B.op("act", lambda e: e.activation(out=pt[:, c0:qw], in_=ps[:, c0:qw], func=AF.Exp, scale=0.125),
             reads=[ps], writes=[pt])
        if a >= 0:
            B.op("pool", lambda e: e.tensor_tensor(out=pt[:, c0:c0 + 128], in0=pt[:, c0:c0 + 128], in1=tri[:], op=ALU.mult),
                 reads=[pt, tri], writes=[pt])

    def stage2(u):
        pt = u["pt"]
        c, j, c0, qw, nk = u["c"], u["j"], u["c0"], u["qw"], u["nk"]
        O, s = acc[c]
        B.op("pe", lambda e: e.matmul(O[:, c0:qw], V[:, j, :], pt[:, c0:qw], start=(j == 0), stop=(j == nk - 1)),
             reads=[V, pt], writes=[O])
        B.op("pe", lambda e: e.matmul(s[:, c0:qw], ones[:], pt[:, c0:qw], start=(j == 0), stop=(j == nk - 1)),
             reads=[ones, pt], writes=[s])
        if u["last"]:
            finalize(u["i"], u["q0"], u["qw"])

    LA = 2
    for idx in range(len(units) + LA):
        if idx < len(units):
            stage1(units[idx], idx)
        if idx >= LA:
            stage2(units[idx - LA])
    return P.finish()


def build_oc():
    P = Prog()
    B = P.B
    h_d = P.dram_in("h", [KC, 128, T])
    mo_d = P.dram_in("mo", [KC, 128, T], BF16)
    w_d = P.dram_in("w", [D, D])
    g_d = P.dram_in("g", [128, KC])
    o_d = P.dram_out("ho", [KC, 128, T])
    emit_consts(P)
    gcol = load_gcols(P, g_d, KC, "gcol")
    ws = WStream(P)
    wt = load_weight_resident(P, ws, w_d, D, "wout")
    hf_t, hf = load_h(P, h_d)
    mo_t = P.sb_raw([128, KC, T], BF16, "mo")
    mo = [Tile(mo_t, f"mo{n}") for n in range(NB)]
    for n in range(NB):
        for c in range(KC):
            B.dma("pool", mo_t[:, c, blk(n)], mo_d[c, :, blk(n)], writes=[mo[n]])
    emit_outproj_residual(P, wt, mo_t, mo, hf_t, hf, gcol, 0)
    for n in range(NB):
        for c in range(KC):
            B.dma("sp", o_d[c, :, blk(n)], hf_t[:, c, blk(n)], reads=[hf[n]])
    return P.finish()


def emit_outproj_residual(P, wt, mo_t, mo, hf_t, hf, gcol, gofs):
    B = P.B
    psa = [P.ps(name=f"po{i}") for i in range(3)]
    mb = [P.sb([128, KC, TB], F32, f"mb{i}") for i in range(2)]
    ia = 0
    for n in range(NB):
        m_t = mb[n % 2]
        for m in range(KC):
            ps = psa[ia % 3]
            ia += 1
            emit_linear(P, wt, 512, m, mo_t, mo[n], n, ps)
            B.op("act", lambda e, ps=ps, m=m, m_t=m_t: e.activation(out=m_t[:, m, :], in_=ps[:, 0:TB], func=AF.Copy),
                 reads=[ps], writes=[m_t])
        rstd = emit_rstd(P, lambda c, m_t=m_t: m_t[:, c, :], [m_t])
        for c in range(KC):
            B.op("dve", lambda e, c=c, m_t=m_t, rstd=rstd: e.scalar_tensor_tensor(
                out=m_t[:, c, :], in0=m_t[:, c, :], scalar=gcol[:, gofs + c:gofs + c + 1],
                in1=rstd[:, 0:TB], op0=ALU.mult, op1=ALU.mult),
                reads=[m_t, gcol, rstd], writes=[m_t])
            B.op("pool", lambda e, c=c, n=n, m_t=m_t: e.tensor_tensor(
                out=hf_t[:, c, blk(n)], in0=hf_t[:, c, blk(n)], in1=m_t[:, c, :], op=ALU.add),
                reads=[m_t, hf[n]], writes=[hf[n]])


def _run(name, builder, in_maps):
    nc = _get(name, builder)
    res = run_bass_kernel_spmd(nc, in_maps, core_ids=list(range(NCORES)))
    return res.results


def run_odd_mixer(h_shards, w_in, lam_vecs, subln_w, w_out, g0, g1, layer):
    perm = rope_perm()
    g = gcols(g0)
    in_maps = [{"h": h_shards[i], "w": w_in, "g": g, "cs": rope_consts(i * T, T), "perm": perm}
               for i in range(NCORES)]
    ra = _run("oa", build_oa, in_maps)
    bf = ra[0]["q"].dtype

    def gather(key, hd):
        full = np.zeros((128, LP), bf)
        for i in range(NCORES):
            full[:, i * T:(i + 1) * T] = ra[i][key][hd]
        return full
    lam_init = 0.8 - 0.6 * math.exp(-0.3 * layer)
    lv = np.ascontiguousarray(np.broadcast_to(lam_vecs.reshape(1, 256), (128, 256)))
    ident = np.eye(128, dtype=np.float32).astype(bf)
    tri = np.triu(np.ones((128, 128), np.float32)).astype(bf)
    sw = np.ascontiguousarray(subln_w.reshape(128, 1))
    in_maps = [{"q": gather("q", hd), "k": gather("k", hd), "v": gather("v", hd), "lv": lv, "sw": sw,
                "ident": ident, "tri": tri} for hd in range(NCORES)]
    rb = _run(f"ob{layer}", lambda: build_ob(lam_init), in_maps)
    g = gcols(g1)
    in_maps = []
    for i in range(NCORES):
        mo = np.stack([rb[hd]["o"][:, i * T:(i + 1) * T] for hd in range(NCORES)], 0)
        in_maps.append({"h": h_shards[i], "mo": np.ascontiguousarray(mo), "w": w_out, "g": g})
    rc = _run("oc", build_oc, in_maps)
    return [r["ho"] for r in rc]


CH = 82
NCH = T // CH
HALO = 16
CW = CH + HALO
NH = 8
RW = 512
EVEN_IN = 2304
GN_EPS = 64 * 1e-5
DEBUG = False
NLEV = 7
DECAY_SCALE = -math.exp(-0.5)


def build_even(aug):
    P = Prog()
    B = P.B
    h_d = P.dram_in("h", [KC, 128, T])
    hh_d = P.dram_in("hh", [KC, 128, HALO])
    w_d = P.dram_in("w", [D, EVEN_IN])
    g_d = P.dram_in("g", [128, 2 * KC])
    c64_d = P.dram_in("c64", [64, 8 * 8])
    c64b_d = P.dram_in("c64b", [64, 2])
    c128_d = P.dram_in("c128", [128, 1 + 4 + 4])
    wup_d = P.dram_in("wup", [64, RW])
    aup_d = P.dram_in("aup", [64, RW])
    gup_d = P.dram_in("gup", [128, RW])
    pw_d = P.dram_in("pw", [4, 128, 128])
    ln_d = P.dram_in("ln", [2, 128, RW])
    msk_d = P.dram_in("msk", [CH, 5 * CH])
    id_d = P.dram_in("ident", [128, 128])
    corr_d = P.dram_in("corr", [128, 4, HALO])
    if aug:
        sum_d = P.dram_out("summ", [64, NH, 128])
    else:
        wo_d = P.dram_in("wo", [D, D])
        sums_d = P.dram_in("sums", [NCORES, 64, NH, 128])
        oh_d = P.dram_in("onehot", [64, NCORES])
        o_d = P.dram_out("ho", [KC, 128, T])
        dbg_d = P.dram_out("dbg", [KC, 128, T], BF16) if DEBUG else None
    emit_consts(P, nbuf=1, sqw=CW)
    ones = P.ones
    gcol = load_gcols(P, g_d, 2 * KC, "gcol")

    def cload(ap, shape, name, q="sp"):
        t = P.sb(shape, F32, name)
        B.dma(q, t[:], ap, writes=[t])
        return t
    c64 = cload(c64_d, [64, 64], "c64")
    c64b = cload(c64b_d, [64, 2], "c64b")
    c128 = cload(c128_d, [128, 9], "c128")
    lnw = cload(ln_d[0], [128, RW], "lnw")
    lnb = cload(ln_d[1], [128, RW], "lnb", "act")
    msk = cload(msk_d, [CH, 5 * CH], "msk")
    ident = cload(id_d, [128, 128], "identf")
    corr = cload(corr_d, [128, 4, HALO], "corr")
    ones32 = P.sb([64, 64], F32, "ones32")
    B.op("pool", lambda e: e.memset(ones32[:], 1.0), writes=[ones32])
    tinyc = P.eps_col(1e-24)
    gnc = P.eps_col(GN_EPS)
    c1 = P.sb([64, 8], F32, "c1")
    B.op("dve", lambda e: e.tensor_scalar(out=c1[:], in0=c64[:, 32:40], scalar1=-1.0, scalar2=1.0,
                                          op0=ALU.mult, op1=ALU.add), reads=[c64], writes=[c1])

    def cc(i):
        return c64[:, i * 8:(i + 1) * 8]

    def bc(ap2, w):
        return ap2.unsqueeze(2).to_broadcast([64, NH, w])

    scanmask = P.sb([64, NH, CH], F32, "scanmask")
    B.op("pool", lambda e: e.memset(scanmask[:], 1.0), writes=[scanmask])
    B.op("pool", lambda e: e.memset(scanmask[:, :, 0:1], 0.0), reads=[scanmask], writes=[scanmask])

    ws = WStream(P, nstage=2, stage_elems=1024)
    wupb = P.sb([64, RW], BF16, "wupb")
    aupb = P.sb([64, RW], BF16, "aupb")
    gupb = P.sb([128, RW], BF16, "gupb")
    pwb = P.sb([128, 4, 128], BF16, "pwb")
    ws.load(wupb, lambda: wupb[:, :], wup_d, [64, RW])
    ws.load(aupb, lambda: aupb[:, :], aup_d, [64, RW])
    ws.load(gupb, lambda: gupb[:, :], gup_d, [128, RW])
    ws.load(pwb, lambda: pwb[:, :, :], pw_d.rearrange("g c d -> c g d"), [128, 4, 128])
    wt = load_weight_resident(P, ws, w_d, EVEN_IN, "win", group=256)
    if not aug:
        wot = load_weight_resident(P, ws, wo_d, D, "wout", group=512)

    hx = P.sb([128, KC, CW], F32, "hx")
    hnc = P.sb([128, KC, CW], BF16, "hnc")

    pj = [P.ps(name=f"pj{i}") for i in range(2)]
    sc = [P.ps(name=f"sc{i}") for i in range(4)]
    pst = P.ps(name="pstate")
    _isc = [0]
    _ipj = [0]

    def nsc():
        _isc[0] += 1
        return sc[_isc[0] % 4]

    def npj():
        _ipj[0] += 1
        return pj[_ipj[0] % 2]

    SW = 128 if aug else 64
    S = [P.sb([64, NH, SW], F32, f"S{i}") for i in range(2)]
    if aug:
        B.op("pool", lambda e: e.memset(S[0][:], 0.0), writes=[S[0]])
        for h in range(NH):
            B.op("pool", lambda e, h=h: e.tensor_copy(out=S[0][:, h, 0:64], in_=ident[0:64, 0:64]),
                 reads=[ident, S[0]], writes=[S[0]])
    else:
        sums = P.sb([64, NH, 128], F32, "sums")
        oh = cload(oh_d, [64, NCORES], "oh")
        X = [P.sb([64, NH, 64], F32, f"X{i}") for i in range(2)]
        PT = P.sb([64, NH, 64], F32, "PmT")
        B.op("pool", lambda e: e.memset(X[0][:], 0.0), writes=[X[0]])
        B.op("pool", lambda e: e.memset(S[0][:], 0.0), writes=[S[0]])
        for j in range(NCORES - 1):
            xo, xn = X[j % 2], X[(j + 1) % 2]
            B.dma("sp", sums[:], sums_d[j], writes=[sums])
            ps = nsc()
            for h in range(NH):
                B.op("pe", lambda e, ps=ps, j=j, h=h: e.transpose(ps[0:64, h * 64:(h + 1) * 64],
                                                                    sums[:, h, 0:64], ident[0:64, 0:64]),
                     reads=[sums, ident], writes=[ps])
            B.op("act", lambda e, ps=ps: e.activation(out=PT[:].rearrange("k h x -> k (h x)"), in_=ps[0:64, 0:512],
                                                      func=AF.Copy), reads=[ps], writes=[PT])
            ps2 = nsc()
            for h in range(NH):
                B.op("pe", lambda e, ps2=ps2, h=h, xo=xo: e.matmul(ps2[0:64, h * 64:(h + 1) * 64], PT[:, h, :],
                                                                    xo[:, h, :], start=True, stop=True),
                     reads=[PT, xo], writes=[ps2])
            B.op("dve", lambda e, ps2=ps2, xn=xn, j=j: e.tensor_tensor(
                out=xn[:], in0=ps2[0:64, 0:512].rearrange("k (h x) -> k h x", h=NH), in1=sums[:, :, 64:128],
                op=ALU.add), reads=[ps2, sums], writes=[xn])
            B.op("dve", lambda e, xn=xn, j=j: e.scalar_tensor_tensor(
                out=S[0][:], in0=xn[:], scalar=oh[:, j + 1:j + 2], in1=S[0][:], op0=ALU.mult, op1=ALU.add),
                reads=[xn, oh, S[0]], writes=[S[0]])

    def t64(name, w=CW, nb=1):
        return [P.sb([64, NH, w], F32, f"{name}{i}") for i in range(nb)]
    yr, yk, yv, dtm = t64("yr"), t64("yk"), t64("yv"), t64("dtm", CW, 1)
    ywa = [P.sb([64, 2, CW], F32, f"ywa{i}") for i in range(1)]
    ygd = [P.sb([128, CW], F32, f"ygd{i}") for i in range(1)]
    yu = [P.sb([128, 4, CW], F32, f"yu{i}") for i in range(1)]
    wab = [P.sb([64, 2, CH], BF16, f"wab{i}") for i in range(1)]
    gdb = [P.sb([128, CH], BF16, f"gdb{i}") for i in range(1)]
    bufA, bufB, bufC = t64("bA", CH), t64("bB", CH), t64("bC", CH)
    a_t, kk_t, kp_t, b_t, rk_t = t64("a", CH), t64("kk", CH), t64("kp", CH), t64("b", CH), t64("rk", CH)
    tmp_t = t64("tmp", CH, 1)
    AR = [P.sb([64, NH, 2, CH], F32, f"AR{i}") for i in range(1)]
    Bt, Kt, BW, KW = t64("Bt", CH), t64("Kt", CH), t64("BW", CH), t64("KW", CH)
    TM = [P.sb([CH, NH, 256], F32, f"TM{i}") for i in range(1)]
    SCm = [P.sb([CH, NH, 5 * CH], F32, f"SC{i}") for i in range(1)]
    Y = [P.sb([CH, NH, 128], F32, f"Y{i}") for i in range(1)]
    Pp = [P.sb([CH, NH, 2, CH], F32, f"Pp{i}") for i in range(2)]
    AF_t = t64("AhF", CH)
    Gp = [P.sb([64, NH, 64], F32, f"Gp{i}") for i in range(1)]
    if not aug:
        UT = [P.sb([CH, NH, 64], F32, f"UT{i}") for i in range(1)]
        OT = [P.sb([CH, NH, 64], F32, f"OT{i}") for i in range(1)]
        osq = P.sb([CH, NH, 64], F32, "osq")
        st = [P.sb([CH, 4 * NH], F32, f"st{i}") for i in range(1)]
        sT = [P.sb([CH, NH], F32, f"sT{i}") for i in range(1)]
        gg = [P.sb([128, 4, CH], F32, f"gg{i}") for i in range(1)]
        mo = [P.sb([128, KC, CH], BF16, f"mo{i}") for i in range(1)]
        ws1 = [P.sb([128, 4, CW], F32, f"ws1{i}") for i in range(1)]
        ws2 = [P.sb([128, 4, CW], F32, f"ws2{i}") for i in range(1)]
        dpb = [P.sb([128, 4, CH], BF16, f"dpb{i}") for i in range(1)]
        mblk = [P.sb([128, KC, CH], F32, f"mblk{i}") for i in range(1)]
        hck = [P.sb([128, KC, CH], F32, f"hck{i}") for i in range(1)]

    v3 = lambda ap, h: ap.rearrange("k (h x) -> k h x", h=h)

    for ci in range(NCH):
        p2 = 0
        e0 = ci * CH
        e1 = e0 + CW
        if ci == 0:
            B.dma("sp", hx[:, :, 0:HALO], hh_d.rearrange("c p t -> p c t"), writes=[hx])
            B.dma("act", hx[:, :, HALO:CW], h_d[:, :, 0:CH].rearrange("c p t -> p c t"), reads=[hx], writes=[hx])
        else:
            B.dma("sp", hx[:], h_d[:, :, ci * CH - HALO:(ci + 1) * CH].rearrange("c p t -> p c t"), writes=[hx])
        rstd = emit_rstd(P, lambda c: hx[:, c, :], [hx], width=CW)
        for c in range(KC):
            B.op("dve", lambda e, c=c, rstd=rstd: e.scalar_tensor_tensor(
                out=hnc[:, c, :], in0=hx[:, c, :], scalar=gcol[:, c:c + 1], in1=rstd[:, 0:CW],
                op0=ALU.mult, op1=ALU.mult), reads=[hx, gcol, rstd], writes=[hnc])
        hts = [hnc]

        def proj(col0, m, dst_fn, dst_tile, part=None):
            ps = npj()
            wtile = wt[col0 // 256]
            co = col0 % 256
            for kc in range(KC):
                B.op("pe", lambda e, kc=kc, ps=ps, wtile=wtile, co=co, m=m: e.matmul(
                    ps[0:m, 0:CW], wtile[:, kc, co:co + m], hnc[:, kc, :], start=(kc == 0), stop=(kc == KC - 1)),
                    reads=[wtile] + hts, writes=[ps])
            B.op("act", lambda e, ps=ps, m=m: e.activation(out=dst_fn(), in_=ps[0:m, 0:CW], func=AF.Copy),
                 reads=[ps], writes=[dst_tile])
        for arr, yt in enumerate((yr, yk, yv)):
            for h in range(NH):
                proj(arr * RW + h * 64, 64, lambda yt=yt, h=h: yt[p2][:, h, :], yt[p2])
        proj(3 * RW, 64, lambda: ywa[p2][:, 0, :], ywa[p2])
        proj(3 * RW + 64, 64, lambda: ywa[p2][:, 1, :], ywa[p2])
        proj(3 * RW + 128, 128, lambda: ygd[p2][:, :], ygd[p2])
        for gi in range(4):
            proj(1792 + gi * 128, 128, lambda gi=gi: yu[p2][:, gi, :], yu[p2])

        def shift(yt, nparts, lead, mu_b):
            d = dtm[0]
            dv = d[0:nparts, 0:lead, 0:CW - 1] if lead else None
            B.op("dve", lambda e: e.tensor_tensor(out=d[0:nparts, 0:lead, 0:CW - 1], in0=yt[0:nparts, :, 0:CW - 1],
                                                  in1=yt[0:nparts, :, 1:CW], op=ALU.subtract),
                 reads=[yt], writes=[d])
            B.op("dve", lambda e: e.tensor_tensor(out=d[0:nparts, 0:lead, 0:CW - 1], in0=d[0:nparts, 0:lead, 0:CW - 1],
                                                  in1=mu_b, op=ALU.mult), reads=[d, c64, c64b], writes=[d])
            B.op("dve", lambda e: e.tensor_tensor(out=yt[0:nparts, :, 1:CW], in0=yt[0:nparts, :, 1:CW],
                                                  in1=d[0:nparts, 0:lead, 0:CW - 1], op=ALU.add),
                 reads=[d, yt], writes=[yt])
        for arr, yt in enumerate((yr, yk, yv)):
            shift(yt[p2], 64, NH, bc(cc(arr), CW - 1))
        shift(ywa[p2], 64, 2, c64b[:, 0:2].unsqueeze(2).to_broadcast([64, 2, CW - 1]))
        B.op("pool", lambda e: e.tensor_tensor(out=ws_g[p2][:, 0:CW - 1], in0=ygd[p2][:, 0:CW - 1],
                                               in1=ygd[p2][:, 1:CW], op=ALU.subtract),
             reads=[ygd[p2]], writes=[ws_g[p2]]) if False else None
        R = yr[p2]
        Kx = yk[p2]
        Vx = yv[p2]
        own = slice(HALO, CW)
        B.op("act", lambda e: e.activation(out=wab[p2][:, 0, :], in_=ywa[p2][:, 0, own], func=AF.Tanh),
             reads=[ywa[p2]], writes=[wab[p2]])
        B.op("act", lambda e: e.activation(out=wab[p2][:, 1, :], in_=ywa[p2][:, 1, own], func=AF.Copy),
             reads=[ywa[p2], wab[p2]], writes=[wab[p2]])
        A_, B_, C_ = bufA[p2], bufB[p2], bufC[p2]
        for (srcrow, upw, bias_i, dst) in ((0, wupb, 6, A_), (1, aupb, 7, a_t[p2])):
            for h0 in (0, 4):
                ps = nsc()
                for h in range(h0, h0 + 4):
                    B.op("pe", lambda e, ps=ps, h=h, h0=h0, upw=upw, srcrow=srcrow: e.matmul(
                        ps[0:64, (h - h0) * CH:(h - h0 + 1) * CH], upw[:, h * 64:(h + 1) * 64], wab[p2][:, srcrow, :],
                        start=True, stop=True), reads=[upw, wab[p2]], writes=[ps])
                for h in range(h0, h0 + 4):
                    B.op("act", lambda e, ps=ps, h=h, h0=h0, dst=dst, bias_i=bias_i: e.activation(
                        out=dst[:, h, :], in_=ps[0:64, (h - h0) * CH:(h - h0 + 1) * CH], func=AF.Sigmoid,
                        bias=c64[:, bias_i * 8 + h:bias_i * 8 + h + 1]), reads=[ps, c64], writes=[dst])
        B.op("dve", lambda e: e.tensor_scalar(out=A_[:], in0=A_[:], scalar1=DECAY_SCALE, scalar2=None, op0=ALU.mult),
             reads=[A_], writes=[A_])
        B.op("dve", lambda e: e.tensor_tensor_scan(
            out=B_[:].rearrange("k h x -> k (h x)"), data0=scanmask[:].rearrange("k h x -> k (h x)"),
            data1=A_[:].rearrange("k h x -> k (h x)"), initial=0.0, op0=ALU.mult, op1=ALU.add),
            reads=[scanmask, A_], writes=[B_])
        B.op("dve", lambda e: e.tensor_tensor(out=A_[:], in0=B_[:], in1=A_[:], op=ALU.subtract),
             reads=[B_, A_], writes=[A_])
        B.op("act", lambda e: e.activation(out=A_[:], in_=A_[:], func=AF.Exp), reads=[A_], writes=[A_])
        B.op("act", lambda e: e.activation(out=C_[:], in_=B_[:], func=AF.Exp), reads=[B_], writes=[C_])
        B.op("act", lambda e: e.activation(out=B_[:], in_=B_[:], func=AF.Exp, scale=-1.0), reads=[B_], writes=[B_])
        kkx, tmp = kk_t[p2], tmp_t[0]
        B.op("dve", lambda e: e.tensor_tensor(out=kkx[:], in0=Kx[:, :, own], in1=bc(cc(3), CH), op=ALU.mult),
             reads=[Kx, c64], writes=[kkx])
        B.op("act", lambda e: e.activation(out=tmp[:], in_=kkx[:], func=AF.Square), reads=[kkx], writes=[tmp])
        for h0 in (0, 4):
            ps = nsc()
            B.op("pe", lambda e, ps=ps, h0=h0: e.matmul(
                ps[0:64, 0:4 * CH], ones32[:], tmp[:, h0:h0 + 4, :].rearrange("k h x -> k (h x)"),
                start=True, stop=True), reads=[ones32, tmp], writes=[ps])
            B.op("act", lambda e, ps=ps, h0=h0: e.activation(out=tmp[:, h0:h0 + 4, :].rearrange("k h x -> k (h x)"),
                                                             in_=ps[0:64, 0:4 * CH], func=AF.Sqrt, bias=tinyc[0:64, :]),
                 reads=[ps, tinyc, tmp], writes=[tmp])
        B.op("dve", lambda e: e.reciprocal(out=tmp[:], in_=tmp[:]), reads=[tmp], writes=[tmp])
        B.op("dve", lambda e: e.tensor_tensor(out=kkx[:], in0=kkx[:], in1=tmp[:], op=ALU.mult),
             reads=[kkx, tmp], writes=[kkx])
        kp, ax, bx, rk = kp_t[p2], a_t[p2], b_t[p2], rk_t[p2]
        B.op("dve", lambda e: e.tensor_tensor(out=kp[:], in0=ax[:], in1=bc(cc(4), CH), op=ALU.mult),
             reads=[ax, c64], writes=[kp])
        B.op("dve", lambda e: e.tensor_tensor(out=kp[:], in0=kp[:], in1=bc(c1[:, :], CH), op=ALU.add),
             reads=[kp, c1], writes=[kp])
        B.op("dve", lambda e: e.tensor_tensor(out=kp[:], in0=kp[:], in1=Kx[:, :, own], op=ALU.mult),
             reads=[kp, Kx], writes=[kp])
        B.op("dve", lambda e: e.tensor_tensor(out=bx[:], in0=kkx[:], in1=ax[:], op=ALU.mult),
             reads=[kkx, ax], writes=[bx])
        B.op("dve", lambda e: e.tensor_tensor(out=rk[:], in0=R[:, :, own], in1=bc(cc(5), CH), op=ALU.mult),
             reads=[R, c64], writes=[rk])
        B.op("dve", lambda e: e.tensor_tensor(out=rk[:], in0=rk[:], in1=kp[:], op=ALU.mult),
             reads=[rk, kp], writes=[rk])
        ar, bt, kt, bw, kw = AR[p2], Bt[p2], Kt[p2], BW[p2], KW[p2]
        B.op("dve", lambda e: e.scalar_tensor_tensor(out=ar[:, :, 0, :], in0=kkx[:], scalar=-1.0, in1=A_[:],
                                                     op0=ALU.mult, op1=ALU.mult), reads=[kkx, A_], writes=[ar])
        B.op("dve", lambda e: e.tensor_tensor(out=ar[:, :, 1, :], in0=R[:, :, own], in1=C_[:], op=ALU.mult),
             reads=[R, C_, ar], writes=[ar])
        B.op("dve", lambda e: e.tensor_tensor(out=bt[:], in0=bx[:], in1=B_[:], op=ALU.mult), reads=[bx, B_], writes=[bt])
        B.op("dve", lambda e: e.tensor_tensor(out=kt[:], in0=kp[:], in1=B_[:], op=ALU.mult), reads=[kp, B_], writes=[kt])
        wc_b = C_[:, :, CH - 1:CH].to_broadcast([64, NH, CH])
        B.op("dve", lambda e: e.tensor_tensor(out=bw[:], in0=bt[:], in1=wc_b, op=ALU.mult), reads=[bt, C_], writes=[bw])
        B.op("dve", lambda e: e.tensor_tensor(out=kw[:], in0=kt[:], in1=wc_b, op=ALU.mult), reads=[kt, C_], writes=[kw])

        tm = TM[p2]
        for hp in range(NH // 2):
            ps = nsc()
            for hh in range(2):
                h = hp * 2 + hh
                srcs = (ar[:, h, 0, :], Vx[:, h, own], bw[:, h, :], kw[:, h, :])
                for si, s_ap in enumerate(srcs):
                    c0 = hh * 256 + si * 64
                    B.op("pe", lambda e, ps=ps, s_ap=s_ap, c0=c0: e.transpose(ps[0:CH, c0:c0 + 64], s_ap, ident[0:64, 0:64]),
                         reads=[ar, Vx, bw, kw, ident], writes=[ps])
            B.op("act", lambda e, ps=ps, hp=hp: e.activation(
                out=tm[:, 2 * hp:2 * hp + 2, :], in_=ps[0:CH, 0:512].rearrange("c (h x) -> c h x", h=2), func=AF.Copy),
                reads=[ps], writes=[tm])
        scm = SCm[p2]
        for h in range(NH):
            ps = nsc()
            B.op("pe", lambda e, ps=ps, h=h: e.matmul(ps[0:CH, 0:2 * CH], bt[:, h, :], ar[:, h, :, :].rearrange("k a x -> k (a x)"),
                                                      start=True, stop=True), reads=[bt, ar], writes=[ps])
            B.op("pe", lambda e, ps=ps, h=h: e.matmul(ps[0:CH, 2 * CH:4 * CH], kt[:, h, :], ar[:, h, :, :].rearrange("k a x -> k (a x)"),
                                                      start=True, stop=True), reads=[kt, ar], writes=[ps])
            B.op("pe", lambda e, ps=ps, h=h: e.matmul(ps[0:CH, 4 * CH:5 * CH], ar[:, h, 0, :], bt[:, h, :],
                                                      start=True, stop=True), reads=[bt, ar], writes=[ps])
            B.op("dve", lambda e, ps=ps, h=h: e.tensor_tensor(out=scm[:, h, :], in0=ps[0:CH, 0:5 * CH], in1=msk[:],
                                                              op=ALU.mult), reads=[ps, msk], writes=[scm])
        y = Y[p2]
        B.op("pool", lambda e: e.tensor_copy(out=y[:, :, 0:64], in_=tm[:, :, 0:64]), reads=[tm], writes=[y])
        ps = nsc()
        for h in range(NH):
            B.op("pe", lambda e, ps=ps, h=h: e.matmul(ps[0:CH, h * 64:(h + 1) * 64], scm[:, h, 2 * CH:3 * CH], tm[:, h, 64:128],
                                                      start=True, stop=True), reads=[scm, tm], writes=[ps])
        B.op("act", lambda e, ps=ps: e.activation(out=y[:, :, 64:128], in_=ps[0:CH, 0:512].rearrange("c (h x) -> c h x", h=NH),
                                                  func=AF.Copy), reads=[ps, y], writes=[y])
        pcur = None
        for lev in range(NLEV):
            def Pm_(h, lev=lev, pcur=pcur):
                return scm[:, h, 0:CH] if lev == 0 else pcur[:, h, 0, :]

            def PmT_(h, lev=lev, pcur=pcur):
                return scm[:, h, 4 * CH:5 * CH] if lev == 0 else pcur[:, h, 1, :]
            ptile = scm if lev == 0 else pcur
            for h0 in (0, 4):
                ps = nsc()
                for h in range(h0, h0 + 4):
                    B.op("pe", lambda e, ps=ps, h=h, h0=h0, Pm_=Pm_: e.matmul(
                        ps[0:CH, (h - h0) * 128:(h - h0 + 1) * 128], Pm_(h), y[:, h, :], start=True, stop=True),
                        reads=[ptile, y], writes=[ps])
                B.op("dve", lambda e, ps=ps, h0=h0: e.tensor_tensor(
                    out=y[:, h0:h0 + 4, :], in0=y[:, h0:h0 + 4, :],
                    in1=ps[0:CH, 0:512].rearrange("c (h x) -> c h x", h=4), op=ALU.add),
                    reads=[ps, y], writes=[y])
            if lev < NLEV - 1:
                pn = Pp[lev % 2]
                for h0 in (0, 2, 4, 6):
                    ps = nsc()
                    for h in range(h0, h0 + 2):
                        o0 = (h - h0) * 2 * CH
                        B.op("pe", lambda e, ps=ps, h=h, o0=o0, Pm_=Pm_, PmT_=PmT_: e.matmul(
                            ps[0:CH, o0:o0 + CH], PmT_(h), Pm_(h), start=True, stop=True),
                            reads=[ptile], writes=[ps])
                        B.op("pe", lambda e, ps=ps, h=h, o0=o0, Pm_=Pm_, PmT_=PmT_: e.matmul(
                            ps[0:CH, o0 + CH:o0 + 2 * CH], Pm_(h), PmT_(h), start=True, stop=True),
                            reads=[ptile], writes=[ps])
                    B.op("act", lambda e, ps=ps, h0=h0, pn=pn: e.activation(
                        out=pn[:, h0:h0 + 2, :, :].rearrange("c h a x -> c h (a x)"),
                        in_=ps[0:CH, 0:4 * CH].rearrange("c (h x) -> c h x", h=2), func=AF.Copy),
                        reads=[ps], writes=[pn])
                pcur = pn
        ahf = AF_t[p2]
        for h0 in (0, 4):
            ps = nsc()
            for h in range(h0, h0 + 4):
                B.op("pe", lambda e, ps=ps, h=h, h0=h0: e.transpose(ps[0:64, (h - h0) * CH:(h - h0 + 1) * CH],
                                                                     y[:, h, 0:64], ident[0:CH, 0:CH]),
                     reads=[y, ident], writes=[ps])
            B.op("act", lambda e, ps=ps, h0=h0: e.activation(
                out=ahf[:, h0:h0 + 4, :], in_=ps[0:64, 0:4 * CH].rearrange("k (h x) -> k h x", h=4), func=AF.Copy),
                reads=[ps], writes=[ahf])
        gp = Gp[p2]
        ps = nsc()
        for h in range(NH):
            B.op("pe", lambda e, ps=ps, h=h: e.matmul(ps[0:64, h * 64:(h + 1) * 64], y[:, h, 0:64], tm[:, h, 128:192],
                                                      start=True, stop=True), reads=[y, tm], writes=[ps])
        B.op("act", lambda e, ps=ps: e.activation(out=gp[:].rearrange("k h x -> k (h x)"), in_=ps[0:64, 0:512], func=AF.Copy),
             reads=[ps], writes=[gp])
        So, Sn = S[ci % 2], S[(ci + 1) % 2]
        if not aug:
            ut, ot = UT[p2], OT[p2]
            ps = nsc()
            for h in range(NH):
                B.op("pe", lambda e, ps=ps, h=h, So=So: e.matmul(ps[0:CH, h * 64:(h + 1) * 64], ahf[:, h, :], So[:, h, :],
                                                          start=True, stop=True), reads=[ahf, So], writes=[ps])
            B.op("dve", lambda e, ps=ps: e.tensor_tensor(out=ut[:], in0=ps[0:CH, 0:512].rearrange("c (h x) -> c h x", h=NH),
                                                         in1=y[:, :, 64:128], op=ALU.add), reads=[ps, y], writes=[ut])
            ps = nsc()
            for h in range(NH):
                o = ps[0:CH, h * 64:(h + 1) * 64]
                B.op("pe", lambda e, o=o, h=h, So=So: e.matmul(o, ar[:, h, 1, :], So[:, h, :], start=True, stop=False),
                     reads=[ar, So], writes=[ps])
                B.op("pe", lambda e, o=o, h=h: e.matmul(o, scm[:, h, CH:2 * CH], ut[:, h, :], start=False, stop=False),
                     reads=[scm, ut], writes=[ps])
                B.op("pe", lambda e, o=o, h=h: e.matmul(o, scm[:, h, 3 * CH:4 * CH], tm[:, h, 64:128], start=False, stop=True),
                     reads=[scm, tm], writes=[ps])
            psO = ps
            ps = nsc()
            for h in range(NH):
                B.op("pe", lambda e, ps=ps, h=h: e.matmul(ps[0:CH, h:h + 1], rk[:, h, :], ones32[:, 0:1], start=True, stop=True),
                     reads=[rk, ones32], writes=[ps])
            B.op("act", lambda e, ps=ps: e.activation(out=sT[p2][:], in_=ps[0:CH, 0:NH], func=AF.Copy), reads=[ps], writes=[sT[p2]])
            s_ = st[p2]
            B.op("act", lambda e, psO=psO: e.activation(out=ot[:].rearrange("c h x -> c (h x)"), in_=psO[0:CH, 0:512], func=AF.Copy),
                 reads=[psO], writes=[ot])
            B.op("act", lambda e: e.activation(out=osq[:], in_=ot[:], func=AF.Square), reads=[ot], writes=[osq])
            B.op("dve", lambda e: e.reduce_sum(out=s_[:, 0:8], in_=ot[:], axis=AX.X), reads=[ot], writes=[s_])
            B.op("dve", lambda e: e.reduce_sum(out=s_[:, 8:16], in_=osq[:], axis=AX.X), reads=[osq, s_], writes=[s_])
            B.op("dve", lambda e: e.tensor_scalar(out=s_[:, 16:24], in0=s_[:, 0:8], scalar1=1.0 / 64, scalar2=None, op0=ALU.mult),
                 reads=[s_], writes=[s_])
            B.op("dve", lambda e: e.tensor_tensor(out=s_[:, 0:8], in0=s_[:, 16:24], in1=s_[:, 16:24], op=ALU.mult),
                 reads=[s_], writes=[s_])
            B.op("dve", lambda e: e.scalar_tensor_tensor(out=s_[:, 8:16], in0=s_[:, 8:16], scalar=1.0 / 64, in1=s_[:, 0:8],
                                                         op0=ALU.mult, op1=ALU.subtract), reads=[s_], writes=[s_])
            B.op("act", lambda e: e.activation(out=s_[:, 24:32], in_=s_[:, 8:16], func=AF.Sqrt, bias=gnc[0:CH, :]),
                 reads=[s_, gnc], writes=[s_])
            B.op("dve", lambda e: e.reciprocal(out=s_[:, 24:32], in_=s_[:, 24:32]), reads=[s_], writes=[s_])
            B.op("dve", lambda e: e.tensor_tensor(out=ot[:], in0=ot[:], in1=s_[:, 16:24].unsqueeze(2).to_broadcast([CH, NH, 64]),
                                                  op=ALU.subtract), reads=[ot, s_], writes=[ot])
            B.op("dve", lambda e: e.tensor_tensor(out=ot[:], in0=ot[:], in1=s_[:, 24:32].unsqueeze(2).to_broadcast([CH, NH, 64]),
                                                  op=ALU.mult), reads=[ot, s_], writes=[ot])
            otf = lambda: ot[:].rearrange("c h x -> c (h x)")
            B.op("dve", lambda e: e.tensor_tensor(out=otf(), in0=otf(), in1=lnw[0:CH, :], op=ALU.mult), reads=[ot, lnw], writes=[ot])
            B.op("dve", lambda e: e.tensor_tensor(out=otf(), in0=otf(), in1=lnb[0:CH, :], op=ALU.add), reads=[ot, lnb], writes=[ot])
            B.op("dve", lambda e: e.tensor_tensor(out=osq[:], in0=tm[:, :, 64:128],
                                                  in1=sT[p2][:, :].unsqueeze(2).to_broadcast([CH, NH, 64]), op=ALU.mult),
                 reads=[tm, sT[p2], osq], writes=[osq])
            B.op("dve", lambda e: e.tensor_tensor(out=ot[:], in0=ot[:], in1=osq[:], op=ALU.add), reads=[ot, osq], writes=[ot])
            gdt = ygd[p2]
            B.op("pool", lambda e: e.tensor_tensor(out=ws1[p2][:, 0, 0:CW - 1], in0=gdt[:, 0:CW - 1], in1=gdt[:, 1:CW], op=ALU.subtract),
                 reads=[gdt], writes=[ws1[p2]])
            B.op("dve", lambda e: e.scalar_tensor_tensor(out=gdt[:, 1:CW], in0=ws1[p2][:, 0, 0:CW - 1], scalar=c128[:, 0:1],
                                                         in1=gdt[:, 1:CW], op0=ALU.mult, op1=ALU.add),
                 reads=[ws1[p2], c128, gdt], writes=[gdt])
            B.op("act", lambda e: e.activation(out=gdb[p2][:], in_=gdt[:, own], func=AF.Sigmoid), reads=[gdt], writes=[gdb[p2]])
            ps = nsc()
            for pr in range(4):
                B.op("pe", lambda e, ps=ps, pr=pr: e.matmul(ps[:, pr * CH:(pr + 1) * CH], gupb[:, pr * 128:(pr + 1) * 128], gdb[p2][:],
                                                            start=True, stop=True), reads=[gupb, gdb[p2]], writes=[ps])
            B.op("act", lambda e, ps=ps: e.activation(out=gg[p2][:].rearrange("p a x -> p (a x)"), in_=ps[:, 0:4 * CH], func=AF.Copy),
                 reads=[ps], writes=[gg[p2]])
            ps = nsc()
            for pr in range(4):
                B.op("pe", lambda e, ps=ps, pr=pr: e.transpose(ps[:, pr * CH:(pr + 1) * CH],
                                                                ot[:, 2 * pr:2 * pr + 2, :].rearrange("c h x -> c (h x)"), ident[0:CH, 0:CH]),
                     reads=[ot, ident], writes=[ps])
            mot = mo[p2]
            B.op("dve", lambda e, ps=ps: e.tensor_tensor(out=mot[:, 0:4, :], in0=ps[:, 0:4 * CH].rearrange("p (a x) -> p a x", a=4),
                                                         in1=gg[p2][:], op=ALU.mult), reads=[ps, gg[p2]], writes=[mot])
            u = yu[p2]
            w1_, w2_ = ws1[p2], ws2[p2]
            B.op("pool", lambda e: e.tensor_tensor(out=w1_[:, :, 1:CW], in0=u[:, :, 1:CW], in1=u[:, :, 0:CW - 1], op=ALU.add),
                 reads=[u, w1_], writes=[w1_])
            B.op("pool", lambda e: e.tensor_tensor(out=w2_[:, 1:4, 3:CW], in0=w1_[:, 1:4, 3:CW], in1=w1_[:, 1:4, 1:CW - 2], op=ALU.add),
                 reads=[w1_], writes=[w2_])
            B.op("pool", lambda e: e.tensor_copy(out=w2_[:, 0, 1:CW], in_=w1_[:, 0, 1:CW]), reads=[w1_, w2_], writes=[w2_])
            B.op("pool", lambda e: e.tensor_tensor(out=w1_[:, 2:4, 7:CW], in0=w2_[:, 2:4, 7:CW], in1=w2_[:, 2:4, 3:CW - 4], op=ALU.add),
                 reads=[w2_, w1_], writes=[w1_])
            B.op("pool", lambda e: e.tensor_copy(out=w2_[:, 2, 7:CW], in_=w1_[:, 2, 7:CW]), reads=[w1_, w2_], writes=[w2_])
            B.op("pool", lambda e: e.tensor_tensor(out=w2_[:, 3, 15:CW], in0=w1_[:, 3, 15:CW], in1=w1_[:, 3, 7:CW - 8], op=ALU.add),
                 reads=[w1_, w2_], writes=[w2_])
            for gi in range(4):
                B.op("dve", lambda e, gi=gi: e.scalar_tensor_tensor(
                    out=w2_[:, gi, own], in0=w2_[:, gi, own], scalar=1.0 / (2 << gi), in1=u[:, gi, own],
                    op0=ALU.mult, op1=ALU.subtract), reads=[w2_, u], writes=[w2_])
            if ci == 0:
                B.op("pool", lambda e: e.tensor_tensor(out=w1_[:, :, 0:HALO], in0=w2_[:, :, HALO:2 * HALO], in1=u[:, :, HALO:2 * HALO],
                                                       op=ALU.add), reads=[w2_, u, w1_], writes=[w1_])
                B.op("pool", lambda e: e.tensor_tensor(out=w1_[:, :, 0:HALO], in0=w1_[:, :, 0:HALO], in1=corr[:], op=ALU.mult),
                     reads=[w1_, corr], writes=[w1_])
                B.op("pool", lambda e: e.tensor_tensor(out=w2_[:, :, HALO:2 * HALO], in0=w1_[:, :, 0:HALO], in1=u[:, :, HALO:2 * HALO],
                                                       op=ALU.subtract), reads=[w1_, u, w2_], writes=[w2_])
            B.op("pool", lambda e: e.tensor_copy(out=dpb[p2][:], in_=w2_[:, :, own]), reads=[w2_], writes=[dpb[p2]])
            ps = nsc()
            for gi in range(4):
                B.op("pe", lambda e, ps=ps, gi=gi: e.matmul(ps[:, gi * CH:(gi + 1) * CH], pwb[:, gi, :], dpb[p2][:, gi, :],
                                                            start=True, stop=True), reads=[pwb, dpb[p2]], writes=[ps])
            for gi in range(4):
                B.op("act", lambda e, ps=ps, gi=gi: e.activation(out=mot[:, 4 + gi, :], in_=ps[:, gi * CH:(gi + 1) * CH], func=AF.Identity,
                                                                 scale=c128[:, 5 + gi:6 + gi]), reads=[ps, c128, mot], writes=[mot])
            if DEBUG:
                B.dma("sp", dbg_d[:, :, ci * CH:(ci + 1) * CH].rearrange("c p t -> p c t"), mot[:], reads=[mot])
            mb = mblk[p2]
            hc = hck[0]
            for m in range(KC):
                ps = npj()
                wtile = wot[(m * 128) // 512]
                mo_ = (m * 128) % 512
                for kc in range(KC):
                    B.op("pe", lambda e, ps=ps, kc=kc, wtile=wtile, mo_=mo_: e.matmul(
                        ps[:, 0:CH], wtile[:, kc, mo_:mo_ + 128], mot[:, kc, :], start=(kc == 0), stop=(kc == KC - 1)),
                        reads=[wtile, mot], writes=[ps])
                B.op("act", lambda e, ps=ps, m=m: e.activation(out=mb[:, m, :], in_=ps[:, 0:CH], func=AF.Copy),
                     reads=[ps], writes=[mb])
            rstd = emit_rstd(P, lambda c: mb[:, c, :], [mb], width=CH)
            for c in range(KC):
                B.op("dve", lambda e, c=c, rstd=rstd: e.scalar_tensor_tensor(
                    out=mb[:, c, :], in0=mb[:, c, :], scalar=gcol[:, KC + c:KC + c + 1], in1=rstd[:, 0:CH],
                    op0=ALU.mult, op1=ALU.mult), reads=[mb, gcol, rstd], writes=[mb])
            B.op("pool", lambda e: e.tensor_tensor(out=hc[:], in0=hx[:, :, HALO:CW], in1=mb[:], op=ALU.add), reads=[hx, mb, hc], writes=[hc])
            B.dma("sp", o_d[:, :, ci * CH:(ci + 1) * CH].rearrange("c p t -> p c t"), hc[:], reads=[hc])
        halves = (0, 64) if aug else (0,)
        for h in range(NH):
            o = pst[0:64, h * 64:(h + 1) * 64]
            B.op("pe", lambda e, o=o, h=h: e.matmul(o, tm[:, h, 128:192], y[:, h, 64:128], start=True, stop=False),
                 reads=[tm, y], writes=[pst])
            B.op("pe", lambda e, o=o, h=h: e.matmul(o, tm[:, h, 192:256], tm[:, h, 64:128], start=False, stop=False),
                 reads=[tm], writes=[pst])
            B.op("pe", lambda e, o=o, h=h, So=So: e.matmul(o, gp[:, h, :], So[:, h, SW - 64:SW], start=False, stop=True),
                 reads=[gp, So], writes=[pst])
        wcb = C_[:, :, CH - 1:CH].to_broadcast([64, NH, 64])
        B.op("dve", lambda e, So=So, Sn=Sn: e.tensor_tensor(out=Sn[:, :, SW - 64:SW], in0=So[:, :, SW - 64:SW], in1=wcb, op=ALU.mult),
             reads=[So, C_, Sn], writes=[Sn])
        B.op("dve", lambda e, Sn=Sn: e.tensor_tensor(out=Sn[:, :, SW - 64:SW], in0=Sn[:, :, SW - 64:SW],
                                              in1=pst[0:64, 0:512].rearrange("k (h x) -> k h x", h=NH), op=ALU.add),
             reads=[Sn, pst], writes=[Sn])
        if aug:
            ps = nsc()
            for h in range(NH):
                B.op("pe", lambda e, ps=ps, h=h, So=So: e.matmul(ps[0:64, h * 64:(h + 1) * 64], gp[:, h, :], So[:, h, 0:64],
                                                          start=True, stop=True), reads=[gp, So], writes=[ps])
            B.op("dve", lambda e, So=So, Sn=Sn: e.tensor_tensor(out=Sn[:, :, 0:64], in0=So[:, :, 0:64], in1=wcb, op=ALU.mult),
                 reads=[So, C_, Sn], writes=[Sn])
            B.op("dve", lambda e, ps=ps, Sn=Sn: e.tensor_tensor(out=Sn[:, :, 0:64], in0=Sn[:, :, 0:64],
                                                         in1=ps[0:64, 0:512].rearrange("k (h x) -> k h x", h=NH), op=ALU.add),
                 reads=[Sn, ps], writes=[Sn])
    if aug:
        B.dma("sp", sum_d, S[NCH % 2][:], reads=[S[NCH % 2]])
    return P.finish()


def even_consts(p):
    def h64(v):
        return v.reshape(NH, 64).T
    mu = p["mu"]
    c64 = np.concatenate([h64(mu[0:512]), h64(mu[512:1024]), h64(mu[1024:1536]), h64(p["k_k"]), h64(p["k_a"]),
                          h64(p["r_k"].reshape(-1)), h64(p["w0"]), h64(p["a0"])], axis=1)
    c64b = np.stack([mu[1536:1600], mu[1600:1664]], axis=1)
    c128 = np.zeros((128, 9), np.float32)
    c128[:, 0] = mu[1664:1792]
    c128[:, 5:9] = p["pool_scale"].reshape(4, 128).T
    ln = np.stack([np.broadcast_to(p["ln_w"][None], (128, RW)), np.broadcast_to(p["ln_b"][None], (128, RW))], 0)
    r = np.arange(CH)
    su = (r[None, :] > r[:, None]).astype(np.float32)
    ui = (r[None, :] >= r[:, None]).astype(np.float32)
    sl = (r[None, :] < r[:, None]).astype(np.float32)
    msk = np.concatenate([su, ui, su, ui, sl], axis=1)
    out = dict(c64=c64, c64b=c64b, c128=c128, wup=p["w_up"], aup=p["a_up"], gup=p["g_up"], pw=p["pool_w"],
               ln=ln, msk=msk, ident=np.eye(128, dtype=np.float32))
    return {k: np.ascontiguousarray(v, dtype=np.float32) for k, v in out.items()}


def pool_corr(core):
    corr = np.ones((128, 4, HALO), np.float32)
    if core == 0:
        for gi in range(4):
            win = 2 << gi
            t = np.arange(HALO)
            corr[:, gi, :] = (win / np.minimum(t + 1, win)).astype(np.float32)[None]
    return corr


def run_even_mixer(h_shards, p, w_in, w_out, g0, g1):
    cst = even_consts(p)
    g = gcols(g0, g1)
    base = []
    for i in range(NCORES):
        hh = np.zeros((KC, 128, HALO), np.float32) if i == 0 else np.ascontiguousarray(h_shards[i - 1][:, :, T - HALO:])
        m = {"h": h_shards[i], "hh": hh, "w": w_in, "g": g, "corr": pool_corr(i)}
        m.update(cst)
        base.append(m)
    r1 = _run("e1", lambda: build_even(True), base)
    sums = np.ascontiguousarray(np.stack([r1[i]["summ"] for i in range(NCORES)], 0))
    in2 = []
    for i in range(NCORES):
        m = dict(base[i])
        oh = np.zeros((64, NCORES), np.float32)
        oh[:, i] = 1.0
        m.update({"wo": w_out, "sums": sums, "onehot": oh})
        in2.append(m)
    r2 = _run("e2", lambda: build_even(False), in2)
    if DEBUG:
        global _DBG
        _DBG = [r["dbg"] for r in r2]
    return [r["ho"] for r in r2]


def kernel(x, meta, norm_g, mlp_w1, mlp_w2, ev_w_in, ev_mu, ev_w0, ev_w_up, ev_a0, ev_a_up, ev_g_up,
           ev_k_k, ev_k_a, ev_r_k, ev_ln_w, ev_ln_b, ev_pool_w, ev_pool_scale, ev_w_out,
           od_w_in, od_lambda, od_subln_w, od_w_out):
    f = lambda a: np.ascontiguousarray(np.asarray(a), dtype=np.float32)
    x = f(x)
    h = np.concatenate([f(meta), x[0]], axis=0)
    shards = [to_fm(h[c * T:(c + 1) * T]) for c in range(NCORES)]
    norm_g = f(norm_g)
    for i in range(4):
        j = i // 2
        g = norm_g[i]
        if i % 2 == 0:
            p = dict(mu=f(ev_mu[j]), w0=f(ev_w0[j]), w_up=f(ev_w_up[j]), a0=f(ev_a0[j]), a_up=f(ev_a_up[j]),
                     g_up=f(ev_g_up[j]), k_k=f(ev_k_k[j]), k_a=f(ev_k_a[j]), r_k=f(ev_r_k[j]), ln_w=f(ev_ln_w[j]),
                     ln_b=f(ev_ln_b[j]), pool_w=f(ev_pool_w[j]), pool_scale=f(ev_pool_scale[j]))
            shards = run_even_mixer(shards, p, f(ev_w_in[j]), f(ev_w_out[j]), g[0], g[1])
        else:
            shards = run_odd_mixer(shards, f(od_w_in[j]), f(od_lambda[j]), f(od_subln_w[j]), f(od_w_out[j]),
                                   g[0], g[1], i)
        shards = run_mlp(shards, f(mlp_w1[i]), f(mlp_w2[i]), g[2], g[3])
    hfull = np.concatenate([from_fm(s) for s in shards], axis=0)
    return np.ascontiguousarray(hfull[16:][None]).astype(np.float32)
```

```python
import math
from contextlib import ExitStack

import numpy as np
import concourse.bass as bass
import concourse.mybir as mybir
from concourse.bass_utils import run_bass_kernel_spmd

F32 = mybir.dt.float32
BF16 = mybir.dt.bfloat16
AF = mybir.ActivationFunctionType
ALU = mybir.AluOpType
AX = mybir.AxisListType

NCORES = 8
D = 1024
KC = 8
L = 16400
T = L // NCORES
TB = 410
NB = T // TB
DFF = 4096
RMS_EPS = 1e-6

N_DMA_SLOTS = 6


class Tile:
    __slots__ = ("t", "w", "r", "name")

    def __init__(self, t, name=""):
        self.t = t
        self.w = None
        self.r = {}
        self.name = name

    def __getitem__(self, idx):
        return self.t[idx]


class Builder:
    ENGS = ("pe", "dve", "act", "pool", "sp")

    def __init__(self, nc):
        self.nc = nc
        self.cnt = {e: 0 for e in self.ENGS}
        self.sem = {e: nc.alloc_semaphore(name=f"c_{e}") for e in self.ENGS}
        self.seen = {e: {f: 0 for f in self.ENGS} for e in self.ENGS}
        self.rec = {e: [] for e in self.ENGS}
        self.dslots = {}
        for e in ("sp", "act", "pool"):
            self.dslots[e] = [dict(sem=nc.alloc_semaphore(name=f"d_{e}{i}"), val=0)
                              for i in range(N_DMA_SLOTS)]
        self.dnext = {e: 0 for e in ("sp", "act", "pool")}
        self.dseen = {e: {} for e in self.ENGS}

    def _need(self, e, dep, waits):
        if dep is None:
            return
        if dep[0] == "c":
            _, f, n = dep
            if f == "pe" and e == "pe":
                return
            if self.seen[e][f] >= n:
                return
            self.seen[e][f] = n
            waits[("c", f)] = max(waits.get(("c", f), 0), n)
        else:
            _, f, s, v = dep
            key = (f, s)
            if self.dseen[e].get(key, 0) >= v:
                return
            self.dseen[e][key] = v
            waits[("d", f, s)] = max(waits.get(("d", f, s), 0), v)

    def _deps(self, e, reads, writes, waits):
        for t in reads:
            self._need(e, t.w, waits)
        for t in writes:
            self._need(e, t.w, waits)
            for d in t.r.values():
                self._need(e, d, waits)

    def op(self, e, fn, reads=(), writes=()):
        waits = {}
        self._deps(e, reads, writes, waits)
        self.cnt[e] += 1
        me = ("c", e, self.cnt[e])
        for t in reads:
            t.r[e] = me
        for t in writes:
            t.w = me
            t.r = {}
        self.rec[e].append((waits, fn, ("c", e)))
        return me

    def dma(self, e, out_ap, in_ap, reads=(), writes=(), **kw):
        waits = {}
        self._deps(e, reads, writes, waits)
        si = self.dnext[e]
        self.dnext[e] = (si + 1) % N_DMA_SLOTS
        slot = self.dslots[e][si]
        if slot["val"] > 0:
            self._need(e, ("d", e, si, slot["val"]), waits)
        slot["val"] += 16
        me = ("d", e, si, slot["val"])
        for t in reads:
            t.r[("dma", e, si)] = me
        for t in writes:
            t.w = me
            t.r = {}

        def fn(eng, out_ap=out_ap, in_ap=in_ap, kw=kw):
            return eng.dma_start(out=out_ap, in_=in_ap, **kw)
        self.rec[e].append((waits, fn, ("d", e, si)))
        return me

    def wait_all(self, e):
        waits = {}
        for f in self.ENGS:
            if f != e and self.cnt[f] > 0:
                self._need(e, ("c", f, self.cnt[f]), waits)
        for f in self.dslots:
            for si, s in enumerate(self.dslots[f]):
                if s["val"] > 0:
                    self._need(e, ("d", f, si, s["val"]), waits)
        self.rec[e].append((waits, None, None))

    def emit(self):
        nc = self.nc
        with nc.Block() as block:
            for e in self.ENGS:
                recs = self.rec[e]
                if not recs:
                    continue
                deco = {"pe": block.tensor, "dve": block.vector, "act": block.scalar,
                        "pool": block.gpsimd, "sp": block.sync}[e]

                def body(eng, recs=recs):
                    for waits, fn, inc in recs:
                        for k, v in waits.items():
                            if k[0] == "c":
                                eng.wait_ge(self.sem[k[1]], v)
                            else:
                                eng.wait_ge(self.dslots[k[1]][k[2]]["sem"], v)
                        if fn is None:
                            continue
                        ins = fn(eng)
                        if inc[0] == "c":
                            ins.then_inc(self.sem[inc[1]], 1)
                        else:
                            ins.then_inc(self.dslots[inc[1]][inc[2]]["sem"], 16)
                deco(body)


class Prog:
    def __init__(self):
        self.nc = bass.Bass("TRN2", target_bir_lowering=False)
        self.es = ExitStack()
        self.es.__enter__()
        self.B = Builder(self.nc)
        self.n = 0
        self._rr = 0

    def dram_in(self, name, shape, dt=F32):
        return self.nc.dram_tensor(name, list(shape), dt, kind="ExternalInput").ap()

    def dram_out(self, name, shape, dt=F32):
        return self.nc.dram_tensor(name, list(shape), dt, kind="ExternalOutput").ap()

    def sb_raw(self, shape, dt, name=None):
        self.n += 1
        name = "s_" + (name or f"sb{self.n}")
        return self.es.enter_context(self.nc.sbuf_tensor(name, list(shape), dt))

    def sb(self, shape, dt, name=None):
        t = self.sb_raw(shape, dt, name)
        return Tile(t, name or "")

    def ps(self, shape=(128, 512), dt=F32, name=None):
        self.n += 1
        name = "p_" + (name or f"ps{self.n}")
        return Tile(self.es.enter_context(self.nc.psum_tensor(name, list(shape), dt)), name)

    def finish(self):
        self.B.wait_all("sp")
        self.B.emit()
        self.es.__exit__(None, None, None)
        return self.nc


def emit_consts(P, norm=True, nbuf=2, sqw=TB):
    B = P.B
    ones = P.sb([128, 128], BF16, "ones_bf")
    B.op("pool", lambda e: e.memset(ones[:], 1.0), writes=[ones])
    P.ones = ones
    if not norm:
        return
    P.ps_ss = P.ps(name="ps_ss")
    P.sqt = [P.sb([128, KC, sqw], BF16, f"sq{i}") for i in range(nbuf)]
    P.rstd = [P.sb([128, sqw], F32, f"rstd{i}") for i in range(nbuf)]
    P._nbuf = nbuf
    P._sqi = 0


def emit_rstd(P, src_fn, src_tiles, nchunk=KC, width=TB, dim=D, eps=RMS_EPS):
    B = P.B
    i = P._sqi
    P._sqi = (P._sqi + 1) % P._nbuf
    sq, rstd, pss, ones = P.sqt[i], P.rstd[i], P.ps_ss, P.ones
    epsc = P.eps_col(eps)
    for c in range(nchunk):
        B.op("act", lambda e, c=c: e.activation(out=sq[:, c, 0:width], in_=src_fn(c), func=AF.Square),
             reads=src_tiles, writes=[sq])
    for c in range(nchunk):
        B.op("pe", lambda e, c=c: e.matmul(pss[:, 0:width], ones[:], sq[:, c, 0:width],
                                           start=(c == 0), stop=(c == nchunk - 1)),
             reads=[ones, sq], writes=[pss])
    B.op("act", lambda e: e.activation(out=rstd[:, 0:width], in_=pss[:, 0:width], func=AF.Sqrt,
                                       scale=1.0 / dim, bias=epsc[:]),
         reads=[pss, epsc], writes=[rstd])
    B.op("dve", lambda e: e.reciprocal(out=rstd[:, 0:width], in_=rstd[:, 0:width]),
         reads=[rstd], writes=[rstd])
    return rstd


def _eps_col(P, eps):
    if not hasattr(P, "_eps"):
        P._eps = {}
    if eps not in P._eps:
        t = P.sb([128, 1], F32, f"eps{len(P._eps)}")
        P.B.op("pool", lambda e: e.memset(t[:], float(eps)), writes=[t])
        P._eps[eps] = t
    return P._eps[eps]


Prog.eps_col = _eps_col


def load_gcols(P, ap, ncol, name):
    t = P.sb([128, ncol], F32, name)
    P.B.dma("sp", t[:], ap, writes=[t])
    return t


class WStream:
    def __init__(self, P, nstage=3, stage_elems=2048):
        self.P = P
        self.stage = [P.sb([128, stage_elems], F32, f"wst{i}") for i in range(nstage)]
        self.i = 0
        self.stage_elems = stage_elems
        self.cast_rr = 0

    def load(self, dst_tile, dst_ap_fn, src_ap, shape, scale_col=None, queue="sp"):
        P, B = self.P, self.P.B
        st = self.stage[self.i]
        self.i = (self.i + 1) % len(self.stage)
        n = int(np.prod(shape[1:]))
        assert n <= self.stage_elems
        if len(shape) == 3:
            sview = st.t[0:shape[0], 0:n].rearrange("p (a b) -> p a b", a=shape[1])
        else:
            sview = st.t[0:shape[0], 0:n]
        B.dma(queue, sview, src_ap, writes=[st])
        eng = ("pool", "dve")[self.cast_rr % 2] if False else "pool"
        self.cast_rr += 1
        B.op(eng, lambda e: e.tensor_copy(out=dst_ap_fn(), in_=sview), reads=[st], writes=[dst_tile])


def build_mlp():
    P = Prog()
    B = P.B
    h_d = P.dram_in("h", [KC, 128, T])
    w1_d = P.dram_in("w1", [D, DFF])
    w2_d = P.dram_in("w2", [DFF, D])
    g_d = P.dram_in("g", [128, 2 * KC])
    o_d = P.dram_out("ho", [KC, 128, T])
    emit_consts(P)
    gcol = load_gcols(P, g_d, 2 * KC, "gcol")

    hf_t = P.sb_raw([128, KC, T], F32, "hf")
    hf = [Tile(hf_t, f"hf{n}") for n in range(NB)]
    hn_t = P.sb_raw([128, KC, T], BF16, "hn")
    hn = [Tile(hn_t, f"hn{n}") for n in range(NB)]

    def blk(n):
        return slice(n * TB, (n + 1) * TB)

    for n in range(NB):
        for c in range(KC):
            B.dma("sp" if c % 2 == 0 else "act", hf_t[:, c, blk(n)], h_d[c, :, blk(n)], writes=[hf[n]])
    for n in range(NB):
        rstd = emit_rstd(P, lambda c, n=n: hf_t[:, c, blk(n)], [hf[n]])
        for c in range(KC):
            B.op("dve", lambda e, c=c, n=n, rstd=rstd: e.scalar_tensor_tensor(
                out=hn_t[:, c, blk(n)], in0=hf_t[:, c, blk(n)], scalar=gcol[:, c:c + 1],
                in1=rstd[:, 0:TB], op0=ALU.mult, op1=ALU.mult),
                reads=[hf[n], gcol, rstd], writes=[hn[n]])

    FG = 512
    NG = DFF // FG
    ws = WStream(P)
    w1b = [P.sb([128, KC, FG], BF16, f"w1b{i}") for i in range(2)]
    w2b = [P.sb([128, FG // 128, D], BF16, f"w2b{i}") for i in range(2)]
    h1 = [P.sb([128, FG // 128, TB], BF16, f"h1_{i}") for i in range(2)]
    rl = [P.sb([128, TB], F32, f"rl{i}") for i in range(2)]
    pha = [P.ps(name=f"ph{i}") for i in range(3)]
    pfa = [P.ps(name=f"pf{i}") for i in range(3)]
    w1v = w1_d.rearrange("(kc p) n -> p kc n", p=128)
    w2v = w2_d.rearrange("(fc p) n -> p fc n", p=128)
    ih = 0
    ipf = 0
    it = 0
    for j in range(NG):
        wb1, wb2 = w1b[j % 2], w2b[j % 2]
        for half in range(2):
            ks = slice(half * 4, half * 4 + 4)
            ws.load(wb1, lambda wb1=wb1, ks=ks: wb1[:, ks, :], w1v[:, ks, j * FG:(j + 1) * FG], [128, 4, FG])
        for half in range(2):
            fs = slice(half * 2, half * 2 + 2)
            ws.load(wb2, lambda wb2=wb2, fs=fs: wb2[:, fs, :],
                    w2v[:, j * 4 + half * 2: j * 4 + half * 2 + 2, :], [128, 2, D])
        for n in range(NB):
            h1t = h1[it % 2]
            it += 1
            for fc in range(FG // 128):
                ph = pha[ih % 3]
                r = rl[ih % 2]
                ih += 1
                for kc in range(KC):
                    B.op("pe", lambda e, ph=ph, wb1=wb1, kc=kc, fc=fc, n=n: e.matmul(
                        ph[:, 0:TB], wb1[:, kc, fc * 128:(fc + 1) * 128], hn_t[:, kc, blk(n)],
                        start=(kc == 0), stop=(kc == KC - 1)),
                        reads=[wb1, hn[n]], writes=[ph])
                B.op("act", lambda e, ph=ph, r=r: e.activation(out=r[:, :], in_=ph[:, 0:TB], func=AF.Relu),
                     reads=[ph], writes=[r])
                B.op("pool", lambda e, r=r, h1t=h1t, fc=fc: e.tensor_tensor(
                    out=h1t[:, fc, :], in0=r[:, :], in1=r[:, :], op=ALU.mult),
                    reads=[r], writes=[h1t])
            for m in range(KC):
                pf = pfa[ipf % 3]
                ipf += 1
                nfc = FG // 128
                for fc in range(nfc):
                    B.op("pe", lambda e, pf=pf, wb2=wb2, fc=fc, m=m, h1t=h1t: e.matmul(
                        pf[:, 0:TB], wb2[:, fc, m * 128:(m + 1) * 128], h1t[:, fc, :],
                        start=(fc == 0), stop=(fc == nfc - 1)),
                        reads=[wb2, h1t], writes=[pf])
                if j == 0:
                    B.op("act", lambda e, pf=pf, m=m, n=n: e.activation(
                        out=hf_t[:, m, blk(n)], in_=pf[:, 0:TB], func=AF.Copy),
                        reads=[pf], writes=[hf[n]])
                else:
                    B.op("dve", lambda e, pf=pf, m=m, n=n: e.tensor_tensor(
                        out=hf_t[:, m, blk(n)], in0=hf_t[:, m, blk(n)], in1=pf[:, 0:TB], op=ALU.add),
                        reads=[pf, hf[n]], writes=[hf[n]])
    hb = [P.sb([128, KC, TB], F32, f"hb{i}") for i in range(2)]
    for n in range(NB):
        hbt = hb[n % 2]
        for c in range(KC):
            B.dma("act", hbt[:, c, :], h_d[c, :, blk(n)], writes=[hbt])
        rstd = emit_rstd(P, lambda c, n=n: hf_t[:, c, blk(n)], [hf[n]])
        for c in range(KC):
            B.op("dve", lambda e, c=c, n=n, rstd=rstd: e.scalar_tensor_tensor(
                out=hf_t[:, c, blk(n)], in0=hf_t[:, c, blk(n)], scalar=gcol[:, KC + c:KC + c + 1],
                in1=rstd[:, 0:TB], op0=ALU.mult, op1=ALU.mult),
                reads=[hf[n], gcol, rstd], writes=[hf[n]])
            B.op("pool", lambda e, c=c, n=n, hbt=hbt: e.tensor_tensor(
                out=hbt[:, c, :], in0=hbt[:, c, :], in1=hf_t[:, c, blk(n)], op=ALU.add),
                reads=[hf[n], hbt], writes=[hbt])
        for c in range(KC):
            B.dma("sp", o_d[c, :, blk(n)], hbt[:, c, :], reads=[hbt])
    return P.finish()


def to_fm(h_tok):
    return np.ascontiguousarray(h_tok.T.reshape(KC, 128, h_tok.shape[0]))


def from_fm(h_fm):
    return np.ascontiguousarray(h_fm.reshape(D, h_fm.shape[-1]).T)


def gcols(*vecs):
    return np.ascontiguousarray(np.concatenate([v.reshape(KC, 128).T for v in vecs], axis=1))


_NC_CACHE = {}


def _get(name, builder):
    if name not in _NC_CACHE:
        _NC_CACHE[name] = builder()
    return _NC_CACHE[name]


def run_mlp(h_shards, w1, w2, g2, g3):
    nc = _get("mlp", build_mlp)
    g = gcols(g2, g3)
    in_maps = [{"h": h_shards[i], "w1": w1, "w2": w2, "g": g} for i in range(NCORES)]
    res = run_bass_kernel_spmd(nc, in_maps, core_ids=list(range(NCORES)))
    return [r["ho"] for r in res.results]


def blk(n):
    return slice(n * TB, (n + 1) * TB)


def load_h(P, h_d, name="hf"):
    hf_t = P.sb_raw([128, KC, T], F32, name)
    hf = [Tile(hf_t, f"{name}{n}") for n in range(NB)]
    for n in range(NB):
        for c in range(KC):
            P.B.dma("sp" if c % 2 == 0 else "act", hf_t[:, c, blk(n)], h_d[c, :, blk(n)], writes=[hf[n]])
    return hf_t, hf


def emit_norm_bf16(P, hf_t, hf, gcol, gofs, name="hn"):
    hn_t = P.sb_raw([128, KC, T], BF16, name)
    hn = [Tile(hn_t, f"{name}{n}") for n in range(NB)]
    for n in range(NB):
        rstd = emit_rstd(P, lambda c, n=n: hf_t[:, c, blk(n)], [hf[n]])
        for c in range(KC):
            P.B.op("dve", lambda e, c=c, n=n, rstd=rstd: e.scalar_tensor_tensor(
                out=hn_t[:, c, blk(n)], in0=hf_t[:, c, blk(n)], scalar=gcol[:, gofs + c:gofs + c + 1],
                in1=rstd[:, 0:TB], op0=ALU.mult, op1=ALU.mult),
                reads=[hf[n], gcol, rstd], writes=[hn[n]])
    return hn_t, hn


def emit_norm_from_dram(P, h_d, gcol, gofs, name="hn"):
    hn_t = P.sb_raw([128, KC, T], BF16, name)
    hn = [Tile(hn_t, f"{name}{n}") for n in range(NB)]
    hb = [P.sb([128, KC, TB], F32, f"{name}_hb{i}") for i in range(2)]
    for n in range(NB):
        hbt = hb[n % 2]
        for c in range(KC):
            P.B.dma("sp" if c % 2 == 0 else "act", hbt[:, c, :], h_d[c, :, blk(n)], writes=[hbt])
        rstd = emit_rstd(P, lambda c, hbt=hbt: hbt[:, c, :], [hbt])
        for c in range(KC):
            P.B.op("dve", lambda e, c=c, n=n, rstd=rstd, hbt=hbt: e.scalar_tensor_tensor(
                out=hn_t[:, c, blk(n)], in0=hbt[:, c, :], scalar=gcol[:, gofs + c:gofs + c + 1],
                in1=rstd[:, 0:TB], op0=ALU.mult, op1=ALU.mult),
                reads=[hbt, gcol, rstd], writes=[hn[n]])
    return hn_t, hn


def load_weight_resident(P, ws, w_d, ncols, name, group=512, kchunks=KC):
    wv = w_d.rearrange("(kc p) n -> p kc n", p=128)
    tiles = []
    per = max(1, ws.stage_elems // group)
    for g0 in range(0, ncols, group):
        gw = min(group, ncols - g0)
        wt = P.sb([128, kchunks, gw], BF16, f"{name}{g0}")
        for k0 in range(0, kchunks, per):
            k1 = min(kchunks, k0 + per)
            ws.load(wt, lambda wt=wt, k0=k0, k1=k1: wt[:, k0:k1, :], wv[:, k0:k1, g0:g0 + gw],
                    [128, k1 - k0, gw])
        tiles.append(wt)
    return tiles


def emit_linear(P, wtiles, group, m, x_t, x_tile, n, ps, kchunks=KC):
    wt = wtiles[(m * 128) // group]
    mo = (m * 128) % group
    for kc in range(kchunks):
        P.B.op("pe", lambda e, kc=kc: e.matmul(ps[:, 0:TB], wt[:, kc, mo:mo + 128], x_t[:, kc, blk(n)],
                                               start=(kc == 0), stop=(kc == kchunks - 1)),
               reads=[wt, x_tile], writes=[ps])


def build_oa():
    P = Prog()
    B = P.B
    h_d = P.dram_in("h", [KC, 128, T])
    w_d = P.dram_in("w", [D, 3 * D])
    g_d = P.dram_in("g", [128, KC])
    cs_d = P.dram_in("cs", [2, 128, T])
    pm_d = P.dram_in("perm", [128, 128])
    q_d = P.dram_out("q", [KC, 128, T], BF16)
    k_d = P.dram_out("k", [KC, 128, T], BF16)
    v_d = P.dram_out("v", [KC, 128, T], BF16)
    emit_consts(P)
    gcol = load_gcols(P, g_d, KC, "gcol")
    perm = P.sb([128, 128], F32, "perm")
    B.dma("sp", perm[:], pm_d, writes=[perm])
    cos = P.sb([128, T], F32, "cos")
    sin = P.sb([128, T], F32, "sin")
    B.dma("sp", cos[:], cs_d[0], writes=[cos])
    B.dma("act", sin[:], cs_d[1], writes=[sin])
    ws = WStream(P)
    wt = load_weight_resident(P, ws, w_d, 3 * D, "win")
    hn_t, hn = emit_norm_from_dram(P, h_d, gcol, 0)
    psa = [P.ps(name=f"pa{i}") for i in range(3)]
    psr = [P.ps(name=f"pr{i}") for i in range(2)]
    qf = [P.sb([128, TB], F32, f"qf{i}") for i in range(2)]
    t1 = [P.sb([128, TB], F32, f"t1{i}") for i in range(2)]
    t2 = [P.sb([128, TB], F32, f"t2{i}") for i in range(2)]
    ob = [P.sb([128, TB], BF16, f"ob{i}") for i in range(3)]
    ia = 0
    io = 0
    for n in range(NB):
        for m in range(3 * KC):
            ps = psa[ia % 3]
            emit_linear(P, wt, 512, m, hn_t, hn[n], n, ps)
            o = ob[io % 3]
            io += 1
            if m < 2 * KC:
                f, a, b, pr = qf[ia % 2], t1[ia % 2], t2[ia % 2], psr[ia % 2]
                B.op("act", lambda e, f=f, ps=ps: e.activation(out=f[:], in_=ps[:, 0:TB], func=AF.Copy),
                     reads=[ps], writes=[f])
                B.op("pe", lambda e, f=f, pr=pr: e.matmul(pr[:, 0:TB], perm[:], f[:], start=True, stop=True),
                     reads=[perm, f], writes=[pr])
                B.op("dve", lambda e, f=f, a=a, n=n: e.tensor_tensor(out=a[:], in0=f[:], in1=cos[:, blk(n)], op=ALU.mult),
                     reads=[f, cos], writes=[a])
                B.op("dve", lambda e, b=b, pr=pr, n=n: e.tensor_tensor(out=b[:], in0=pr[:, 0:TB], in1=sin[:, blk(n)], op=ALU.mult),
                     reads=[pr, sin], writes=[b])
                B.op("pool", lambda e, a=a, b=b, o=o: e.tensor_tensor(out=o[:], in0=a[:], in1=b[:], op=ALU.add),
                     reads=[a, b], writes=[o])
                dst = (q_d if m < KC else k_d)[m % KC, :, blk(n)]
            else:
                B.op("act", lambda e, o=o, ps=ps: e.activation(out=o[:], in_=ps[:, 0:TB], func=AF.Copy),
                     reads=[ps], writes=[o])
                dst = v_d[m % KC, :, blk(n)]
            ia += 1
            B.dma("sp", dst, o[:], reads=[o])
    return P.finish()


def rope_consts(pos0, npos):
    dh = 64
    inv = (np.float32(10000.0) ** (-np.arange(0, dh, 2, dtype=np.float32) / np.float32(dh))).astype(np.float32)
    pos = np.arange(pos0, pos0 + npos, dtype=np.float32)
    ang = (pos[:, None] * inv[None, :]).astype(np.float32)
    ang = np.concatenate([ang, ang], axis=-1)
    c = np.cos(ang).astype(np.float32).T
    s = np.sin(ang).astype(np.float32).T
    sgn = np.concatenate([-np.ones(32, np.float32), np.ones(32, np.float32)])[:, None]
    ss = s * sgn
    return np.ascontiguousarray(np.stack([np.concatenate([c, c], 0), np.concatenate([ss, ss], 0)], 0))


def rope_perm():
    pm = np.zeros((128, 128), np.float32)
    for b in (0, 64):
        for d in range(64):
            src = d + 32 if d < 32 else d - 32
            pm[b + src, b + d] = 1.0
    return pm


LP = 16512
NKB = LP // 128
SUBLN_EPS = 1e-5


def build_ob(lam_init):
    P = Prog()
    B = P.B
    q_d = P.dram_in("q", [128, LP], BF16)
    k_d = P.dram_in("k", [128, LP], BF16)
    v_d = P.dram_in("v", [128, LP], BF16)
    lv_d = P.dram_in("lv", [128, 256])
    sw_d = P.dram_in("sw", [128, 1])
    id_d = P.dram_in("ident", [128, 128], BF16)
    tri_d = P.dram_in("tri", [128, 128], BF16)
    o_d = P.dram_out("o", [128, LP], BF16)
    emit_consts(P, norm=False)
    ones = P.ones
    qT = P.sb([128, LP], BF16, "qT")
    kT = P.sb([128, LP], BF16, "kT")
    vT = P.sb([128, LP], BF16, "vT")
    for i, (t, d) in enumerate(((qT, q_d), (kT, k_d), (vT, v_d))):
        B.dma(("sp", "act", "pool")[i], t[:], d, writes=[t])
    ident = P.sb([128, 128], BF16, "ident")
    tri = P.sb([128, 128], BF16, "tri")
    B.dma("sp", ident[:], id_d, writes=[ident])
    B.dma("sp", tri[:], tri_d, writes=[tri])
    lv = P.sb([128, 256], F32, "lv")
    sw = P.sb([128, 1], F32, "sw")
    B.dma("sp", lv[:], lv_d, writes=[lv])
    B.dma("sp", sw[:], sw_d, writes=[sw])
    lp = P.sb([128, 128], F32, "lp")
    ls = P.sb([128, 2], F32, "ls")
    nlam = P.sb([128, 1], F32, "nlam")
    swc = P.sb([128, 1], F32, "swc")
    B.op("dve", lambda e: e.tensor_tensor(out=lp[:, 0:64], in0=lv[:, 0:64], in1=lv[:, 64:128], op=ALU.mult),
         reads=[lv], writes=[lp])
    B.op("dve", lambda e: e.tensor_tensor(out=lp[:, 64:128], in0=lv[:, 128:192], in1=lv[:, 192:256], op=ALU.mult),
         reads=[lv, lp], writes=[lp])
    B.op("dve", lambda e: e.reduce_sum(out=ls[:, 0:1], in_=lp[:, 0:64], axis=AX.X), reads=[lp], writes=[ls])
    B.op("dve", lambda e: e.reduce_sum(out=ls[:, 1:2], in_=lp[:, 64:128], axis=AX.X), reads=[lp, ls], writes=[ls])
    B.op("act", lambda e: e.activation(out=ls[:], in_=ls[:], func=AF.Exp), reads=[ls], writes=[ls])
    B.op("dve", lambda e: e.scalar_tensor_tensor(out=nlam[:], in0=ls[:, 1:2], scalar=-float(lam_init),
                                                 in1=ls[:, 0:1], op0=ALU.add, op1=ALU.subtract),
         reads=[ls], writes=[nlam])
    B.op("dve", lambda e: e.tensor_scalar(out=swc[:], in0=sw[:], scalar1=float(1.0 - lam_init), scalar2=None,
                                          op0=ALU.mult), reads=[sw], writes=[swc])
    V = P.sb([128, NKB, 128], BF16, "V")
    psS = [P.ps(name=f"S{i}") for i in range(4)]
    ptr = psS[3]
    ptrv = ptr.t[:, :].bitcast(BF16)
    for j0 in range(0, NKB, 8):
        j1 = min(NKB, j0 + 8)
        for j in range(j0, j1):
            B.op("pe", lambda e, j=j, j0=j0: e.transpose(ptrv[:, (j - j0) * 128:(j - j0 + 1) * 128],
                                                          vT[:, j * 128:(j + 1) * 128], ident[:]),
                 reads=[vT, ident], writes=[ptr])
        B.op("dve", lambda e, j0=j0, j1=j1: e.tensor_copy(
            out=V[:, j0:j1, :], in_=ptrv[:, 0:(j1 - j0) * 128].rearrange("p (a b) -> p a b", b=128)),
            reads=[ptr], writes=[V])
    acc = [[P.ps(name=f"O{c}"), P.ps(name=f"s{c}")] for c in range(2)]
    pts = [P.sb([128, 512], BF16, f"pt{i}") for i in range(6)]
    rr = [P.sb([128, 512], F32, f"rr{i}") for i in range(2)]
    oo = [P.sb([128, 512], F32, f"oo{i}") for i in range(2)]
    osq = P.sb([128, 512], BF16, "osq")
    rs = P.sb([128, 512], F32, "rs")
    outb = [P.sb([128, 512], BF16, f"outb{i}") for i in range(2)]
    epsc = P.eps_col(SUBLN_EPS)
    _iS = [0]
    nqt = (LP + 511) // 512
    def finalize(i, q0, qw):
        for c in range(2):
            O, s = acc[c]
            B.op("dve", lambda e, s=s, c=c, qw=qw: e.reciprocal(out=rr[c][:, 0:qw], in_=s[:, 0:qw]),
                 reads=[s], writes=[rr[c]])
            B.op("dve", lambda e, O=O, c=c, qw=qw: e.tensor_tensor(
                out=oo[c][:, 0:qw], in0=O[:, 0:qw], in1=rr[c][:, 0:qw], op=ALU.mult),
                reads=[O, rr[c]], writes=[oo[c]])
        B.op("dve", lambda e, qw=qw: e.scalar_tensor_tensor(
            out=oo[0][:, 0:qw], in0=oo[1][:, 0:qw], scalar=nlam[:, 0:1], in1=oo[0][:, 0:qw],
            op0=ALU.mult, op1=ALU.add), reads=[oo[0], oo[1], nlam], writes=[oo[0]])
        B.op("act", lambda e, qw=qw: e.activation(out=osq[:, 0:qw], in_=oo[0][:, 0:qw], func=AF.Square),
             reads=[oo[0]], writes=[osq])
        ps = psS[_iS[0] % 4]
        _iS[0] += 1
        B.op("pe", lambda e, ps=ps, qw=qw: e.matmul(ps[:, 0:qw], ones[:], osq[:, 0:qw], start=True, stop=True),
             reads=[ones, osq], writes=[ps])
        B.op("act", lambda e, ps=ps, qw=qw: e.activation(out=rs[:, 0:qw], in_=ps[:, 0:qw], func=AF.Sqrt,
                                                         scale=1.0 / 128, bias=epsc[:]),
             reads=[ps, epsc], writes=[rs])
        B.op("dve", lambda e, qw=qw: e.reciprocal(out=rs[:, 0:qw], in_=rs[:, 0:qw]), reads=[rs], writes=[rs])
        ot = outb[i % 2]
        B.op("dve", lambda e, ot=ot, qw=qw: e.scalar_tensor_tensor(
            out=ot[:, 0:qw], in0=oo[0][:, 0:qw], scalar=swc[:, 0:1], in1=rs[:, 0:qw],
            op0=ALU.mult, op1=ALU.mult), reads=[oo[0], swc, rs], writes=[ot])
        B.dma("sp", o_d[:, q0:q0 + qw], ot[:, 0:qw], reads=[ot])

    units = []
    for i in range(nqt):
        q0 = i * 512
        qw = min(512, LP - q0)
        nqb = qw // 128
        nk = 4 * i + nqb
        for j in range(nk):
            a = j - 4 * i
            c0 = max(0, a) * 128
            for c in range(2):
                units.append(dict(i=i, j=j, c=c, a=a, c0=c0, q0=q0, qw=qw, nk=nk, last=(j == nk - 1 and c == 1)))

    def stage1(u, idx):
        ps = psS[_iS[0] % 4]
        _iS[0] += 1
        pt = pts[idx % 6]
        u["pt"] = pt
        c, j, c0, q0, qw, a = u["c"], u["j"], u["c0"], u["q0"], u["qw"], u["a"]
        rows = slice(c * 64, (c + 1) * 64)
        B.op("pe", lambda e: e.matmul(ps[:, c0:qw], kT[rows, j * 128:(j + 1) * 128], qT[rows, q0 + c0:q0 + qw],
                                      start=True, stop=True), reads=[kT, qT], writes=[ps])
        B.op("act", lambda e: e.activation(out=pt[:, c0:qw], in_=ps[:, c0:qw], func=AF.Exp, scale=0.125),
             reads=[ps], writes=[pt])
        if a >= 0:
            B.op("pool", lambda e: e.tensor_tensor(out=pt[:, c0:c0 + 128], in0=pt[:, c0:c0 + 128], in1=tri[:], op=ALU.mult),
                 reads=[pt, tri], writes=[pt])

    def stage2(u):
        pt = u["pt"]
        c, j, c0, qw, nk = u["c"], u["j"], u["c0"], u["qw"], u["nk"]
        O, s = acc[c]
        B.op("pe", lambda e: e.matmul(O[:, c0:qw], V[:, j, :], pt[:, c0:qw], start=(j == 0), stop=(j == nk - 1)),
             reads=[V, pt], writes=[O])
        B.op("pe", lambda e: e.matmul(s[:, c0:qw], ones[:], pt[:, c0:qw], start=(j == 0), stop=(j == nk - 1)),
             reads=[ones, pt], writes=[s])
        if u["last"]:
            finalize(u["i"], u["q0"], u["qw"])

    LA = 3
    for idx in range(len(units) + LA):
        if idx < len(units):
            stage1(units[idx], idx)
        if idx >= LA:
            stage2(units[idx - LA])
    return P.finish()


def build_oc():
    P = Prog()
    B = P.B
    h_d = P.dram_in("h", [KC, 128, T])
    mo_d = P.dram_in("mo", [KC, 128, T], BF16)
    w_d = P.dram_in("w", [D, D])
    g_d = P.dram_in("g", [128, KC])
    o_d = P.dram_out("ho", [KC, 128, T])
    emit_consts(P)
    gcol = load_gcols(P, g_d, KC, "gcol")
    ws = WStream(P)
    wt = load_weight_resident(P, ws, w_d, D, "wout")
    hf_t, hf = load_h(P, h_d)
    mo_t = P.sb_raw([128, KC, T], BF16, "mo")
    mo = [Tile(mo_t, f"mo{n}") for n in range(NB)]
    for n in range(NB):
        for c in range(KC):
            B.dma("pool", mo_t[:, c, blk(n)], mo_d[c, :, blk(n)], writes=[mo[n]])
    emit_outproj_residual(P, wt, mo_t, mo, hf_t, hf, gcol, 0)
    for n in range(NB):
        for c in range(KC):
            B.dma("sp", o_d[c, :, blk(n)], hf_t[:, c, blk(n)], reads=[hf[n]])
    return P.finish()


def emit_outproj_residual(P, wt, mo_t, mo, hf_t, hf, gcol, gofs):
    B = P.B
    psa = [P.ps(name=f"po{i}") for i in range(3)]
    mb = [P.sb([128, KC, TB], F32, f"mb{i}") for i in range(2)]
    ia = 0
    for n in range(NB):
        m_t = mb[n % 2]
        for m in range(KC):
            ps = psa[ia % 3]
            ia += 1
            emit_linear(P, wt, 512, m, mo_t, mo[n], n, ps)
            B.op("act", lambda e, ps=ps, m=m, m_t=m_t: e.activation(out=m_t[:, m, :], in_=ps[:, 0:TB], func=AF.Copy),
                 reads=[ps], writes=[m_t])
        rstd = emit_rstd(P, lambda c, m_t=m_t: m_t[:, c, :], [m_t])
        for c in range(KC):
            B.op("dve", lambda e, c=c, m_t=m_t, rstd=rstd: e.scalar_tensor_tensor(
                out=m_t[:, c, :], in0=m_t[:, c, :], scalar=gcol[:, gofs + c:gofs + c + 1],
                in1=rstd[:, 0:TB], op0=ALU.mult, op1=ALU.mult),
                reads=[m_t, gcol, rstd], writes=[m_t])
            B.op("pool", lambda e, c=c, n=n, m_t=m_t: e.tensor_tensor(
                out=hf_t[:, c, blk(n)], in0=hf_t[:, c, blk(n)], in1=m_t[:, c, :], op=ALU.add),
                reads=[m_t, hf[n]], writes=[hf[n]])


def _run(name, builder, in_maps):
    nc = _get(name, builder)
    res = run_bass_kernel_spmd(nc, in_maps, core_ids=list(range(NCORES)))
    return res.results


def run_odd_mixer(h_shards, w_in, lam_vecs, subln_w, w_out, g0, g1, layer):
    perm = rope_perm()
    g = gcols(g0)
    in_maps = [{"h": h_shards[i], "w": w_in, "g": g, "cs": rope_consts(i * T, T), "perm": perm}
               for i in range(NCORES)]
    ra = _run("oa", build_oa, in_maps)
    bf = ra[0]["q"].dtype

    def gather(key, hd):
        full = np.zeros((128, LP), bf)
        for i in range(NCORES):
            full[:, i * T:(i + 1) * T] = ra[i][key][hd]
        return full
    lam_init = 0.8 - 0.6 * math.exp(-0.3 * layer)
    lv = np.ascontiguousarray(np.broadcast_to(lam_vecs.reshape(1, 256), (128, 256)))
    ident = np.eye(128, dtype=np.float32).astype(bf)
    tri = np.triu(np.ones((128, 128), np.float32)).astype(bf)
    sw = np.ascontiguousarray(subln_w.reshape(128, 1))
    in_maps = [{"q": gather("q", hd), "k": gather("k", hd), "v": gather("v", hd), "lv": lv, "sw": sw,
                "ident": ident, "tri": tri} for hd in range(NCORES)]
    rb = _run(f"ob{layer}", lambda: build_ob(lam_init), in_maps)
    g = gcols(g1)
    in_maps = []
    for i in range(NCORES):
        mo = np.stack([rb[hd]["o"][:, i * T:(i + 1) * T] for hd in range(NCORES)], 0)
        in_maps.append({"h": h_shards[i], "mo": np.ascontiguousarray(mo), "w": w_out, "g": g})
    rc = _run("oc", build_oc, in_maps)
    return [r["ho"] for r in rc]


CH = 82
NCH = T // CH
HALO = 16
CW = CH + HALO
NH = 8
RW = 512
EVEN_IN = 2304
GN_EPS = 64 * 1e-5
DEBUG = False
NLEV = 7
DECAY_SCALE = -math.exp(-0.5)


def build_even(aug):
    P = Prog()
    B = P.B
    h_d = P.dram_in("h", [KC, 128, T])
    hh_d = P.dram_in("hh", [KC, 128, HALO])
    w_d = P.dram_in("w", [D, EVEN_IN])
    g_d = P.dram_in("g", [128, 2 * KC])
    c64_d = P.dram_in("c64", [64, 8 * 8])
    c64b_d = P.dram_in("c64b", [64, 2])
    c128_d = P.dram_in("c128", [128, 1 + 4 + 4])
    wup_d = P.dram_in("wup", [64, RW])
    aup_d = P.dram_in("aup", [64, RW])
    gup_d = P.dram_in("gup", [128, RW])
    pw_d = P.dram_in("pw", [4, 128, 128])
    ln_d = P.dram_in("ln", [2, 128, RW])
    msk_d = P.dram_in("msk", [CH, 5 * CH])
    id_d = P.dram_in("ident", [128, 128])
    corr_d = P.dram_in("corr", [128, 4, HALO])
    if aug:
        sum_d = P.dram_out("summ", [64, NH, 128])
    else:
        wo_d = P.dram_in("wo", [D, D])
        sums_d = P.dram_in("sums", [NCORES, 64, NH, 128])
        oh_d = P.dram_in("onehot", [64, NCORES])
        o_d = P.dram_out("ho", [KC, 128, T])
        dbg_d = P.dram_out("dbg", [KC, 128, T], BF16) if DEBUG else None
    emit_consts(P, nbuf=1, sqw=CW)
    ones = P.ones
    gcol = load_gcols(P, g_d, 2 * KC, "gcol")

    def cload(ap, shape, name, q="sp"):
        t = P.sb(shape, F32, name)
        B.dma(q, t[:], ap, writes=[t])
        return t
    c64 = cload(c64_d, [64, 64], "c64")
    c64b = cload(c64b_d, [64, 2], "c64b")
    c128 = cload(c128_d, [128, 9], "c128")
    lnw = cload(ln_d[0], [128, RW], "lnw")
    lnb = cload(ln_d[1], [128, RW], "lnb", "act")
    msk = cload(msk_d, [CH, 5 * CH], "msk")
    ident = cload(id_d, [128, 128], "identf")
    corr = cload(corr_d, [128, 4, HALO], "corr")
    ones32 = P.sb([64, 64], F32, "ones32")
    B.op("pool", lambda e: e.memset(ones32[:], 1.0), writes=[ones32])
    tinyc = P.eps_col(1e-24)
    gnc = P.eps_col(GN_EPS)
    c1 = P.sb([64, 8], F32, "c1")
    B.op("dve", lambda e: e.tensor_scalar(out=c1[:], in0=c64[:, 32:40], scalar1=-1.0, scalar2=1.0,
                                          op0=ALU.mult, op1=ALU.add), reads=[c64], writes=[c1])

    def cc(i):
        return c64[:, i * 8:(i + 1) * 8]

    def bc(ap2, w):
        return ap2.unsqueeze(2).to_broadcast([64, NH, w])

    scanmask = P.sb([64, NH, CH], F32, "scanmask")
    B.op("pool", lambda e: e.memset(scanmask[:], 1.0), writes=[scanmask])
    B.op("pool", lambda e: e.memset(scanmask[:, :, 0:1], 0.0), reads=[scanmask], writes=[scanmask])

    ws = WStream(P, nstage=2, stage_elems=1024)
    wupb = P.sb([64, RW], BF16, "wupb")
    aupb = P.sb([64, RW], BF16, "aupb")
    gupb = P.sb([128, RW], BF16, "gupb")
    pwb = P.sb([128, 4, 128], BF16, "pwb")
    ws.load(wupb, lambda: wupb[:, :], wup_d, [64, RW])
    ws.load(aupb, lambda: aupb[:, :], aup_d, [64, RW])
    ws.load(gupb, lambda: gupb[:, :], gup_d, [128, RW])
    ws.load(pwb, lambda: pwb[:, :, :], pw_d.rearrange("g c d -> c g d"), [128, 4, 128])
    wt = load_weight_resident(P, ws, w_d, EVEN_IN, "win", group=256)
    if not aug:
        wot = load_weight_resident(P, ws, wo_d, D, "wout", group=512)

    hx = P.sb([128, KC, CW], F32, "hx")
    hnc = P.sb([128, KC, CW], BF16, "hnc")

    pj = [P.ps(name=f"pj{i}") for i in range(2)]
    sc = [P.ps(name=f"sc{i}") for i in range(4)]
    pst = P.ps(name="pstate")
    _isc = [0]
    _ipj = [0]

    def nsc():
        _isc[0] += 1
        return sc[_isc[0] % 4]

    def npj():
        _ipj[0] += 1
        return pj[_ipj[0] % 2]

    SW = 128 if aug else 64
    S = [P.sb([64, NH, SW], F32, f"S{i}") for i in range(2)]
    if aug:
        B.op("pool", lambda e: e.memset(S[0][:], 0.0), writes=[S[0]])
        for h in range(NH):
            B.op("pool", lambda e, h=h: e.tensor_copy(out=S[0][:, h, 0:64], in_=ident[0:64, 0:64]),
                 reads=[ident, S[0]], writes=[S[0]])
    else:
        sums = P.sb([64, NH, 128], F32, "sums")
        oh = cload(oh_d, [64, NCORES], "oh")
        X = [P.sb([64, NH, 64], F32, f"X{i}") for i in range(2)]
        PT = P.sb([64, NH, 64], F32, "PmT")
        B.op("pool", lambda e: e.memset(X[0][:], 0.0), writes=[X[0]])
        B.op("pool", lambda e: e.memset(S[0][:], 0.0), writes=[S[0]])
        for j in range(NCORES - 1):
            xo, xn = X[j % 2], X[(j + 1) % 2]
            B.dma("sp", sums[:], sums_d[j], writes=[sums])
            ps = nsc()
            for h in range(NH):
                B.op("pe", lambda e, ps=ps, j=j, h=h: e.transpose(ps[0:64, h * 64:(h + 1) * 64],
                                                                    sums[:, h, 0:64], ident[0:64, 0:64]),
                     reads=[sums, ident], writes=[ps])
            B.op("act", lambda e, ps=ps: e.activation(out=PT[:].rearrange("k h x -> k (h x)"), in_=ps[0:64, 0:512],
                                                      func=AF.Copy), reads=[ps], writes=[PT])
            ps2 = nsc()
            for h in range(NH):
                B.op("pe", lambda e, ps2=ps2, h=h, xo=xo: e.matmul(ps2[0:64, h * 64:(h + 1) * 64], PT[:, h, :],
                                                                    xo[:, h, :], start=True, stop=True),
                     reads=[PT, xo], writes=[ps2])
            B.op("dve", lambda e, ps2=ps2, xn=xn, j=j: e.tensor_tensor(
                out=xn[:], in0=ps2[0:64, 0:512].rearrange("k (h x) -> k h x", h=NH), in1=sums[:, :, 64:128],
                op=ALU.add), reads=[ps2, sums], writes=[xn])
            B.op("dve", lambda e, xn=xn, j=j: e.scalar_tensor_tensor(
                out=S[0][:], in0=xn[:], scalar=oh[:, j + 1:j + 2], in1=S[0][:], op0=ALU.mult, op1=ALU.add),
                reads=[xn, oh, S[0]], writes=[S[0]])

    def t64(name, w=CW, nb=1):
        return [P.sb([64, NH, w], F32, f"{name}{i}") for i in range(nb)]
    yr, yk, yv, dtm = t64("yr"), t64("yk"), t64("yv"), t64("dtm", CW, 1)
    ywa = [P.sb([64, 2, CW], F32, f"ywa{i}") for i in range(1)]
    ygd = [P.sb([128, CW], F32, f"ygd{i}") for i in range(1)]
    yu = [P.sb([128, 4, CW], F32, f"yu{i}") for i in range(1)]
    wab = [P.sb([64, 2, CH], BF16, f"wab{i}") for i in range(1)]
    gdb = [P.sb([128, CH], BF16, f"gdb{i}") for i in range(1)]
    bufA, bufB, bufC = t64("bA", CH), t64("bB", CH), t64("bC", CH)
    a_t, kk_t, kp_t, b_t, rk_t = t64("a", CH), t64("kk", CH), t64("kp", CH), t64("b", CH), t64("rk", CH)
    tmp_t = t64("tmp", CH, 1)
    AR = [P.sb([64, NH, 2, CH], F32, f"AR{i}") for i in range(1)]
    Bt, Kt, BW, KW = t64("Bt", CH), t64("Kt", CH), t64("BW", CH), t64("KW", CH)
    TM = [P.sb([CH, NH, 256], F32, f"TM{i}") for i in range(1)]
    SCm = [P.sb([CH, NH, 5 * CH], F32, f"SC{i}") for i in range(1)]
    Y = [P.sb([CH, NH, 128], F32, f"Y{i}") for i in range(1)]
    Pp = [P.sb([CH, NH, 2, CH], F32, f"Pp{i}") for i in range(2)]
    AF_t = t64("AhF", CH)
    Gp = [P.sb([64, NH, 64], F32, f"Gp{i}") for i in range(1)]
    if not aug:
        UT = [P.sb([CH, NH, 64], F32, f"UT{i}") for i in range(1)]
        OT = [P.sb([CH, NH, 64], F32, f"OT{i}") for i in range(1)]
        osq = P.sb([CH, NH, 64], F32, "osq")
        st = [P.sb([CH, 4 * NH], F32, f"st{i}") for i in range(1)]
        sT = [P.sb([CH, NH], F32, f"sT{i}") for i in range(1)]
        gg = [P.sb([128, 4, CH], F32, f"gg{i}") for i in range(1)]
        mo = [P.sb([128, KC, CH], BF16, f"mo{i}") for i in range(1)]
        ws1 = [P.sb([128, 4, CW], F32, f"ws1{i}") for i in range(1)]
        ws2 = [P.sb([128, 4, CW], F32, f"ws2{i}") for i in range(1)]
        dpb = [P.sb([128, 4, CH], BF16, f"dpb{i}") for i in range(1)]
        mblk = [P.sb([128, KC, CH], F32, f"mblk{i}") for i in range(1)]
        hck = [P.sb([128, KC, CH], F32, f"hck{i}") for i in range(1)]

    v3 = lambda ap, h: ap.rearrange("k (h x) -> k h x", h=h)

    for ci in range(NCH):
        p2 = 0
        e0 = ci * CH
        e1 = e0 + CW
        if ci == 0:
            B.dma("sp", hx[:, :, 0:HALO], hh_d.rearrange("c p t -> p c t"), writes=[hx])
            B.dma("act", hx[:, :, HALO:CW], h_d[:, :, 0:CH].rearrange("c p t -> p c t"), reads=[hx], writes=[hx])
        else:
            B.dma("sp", hx[:], h_d[:, :, ci * CH - HALO:(ci + 1) * CH].rearrange("c p t -> p c t"), writes=[hx])
        rstd = emit_rstd(P, lambda c: hx[:, c, :], [hx], width=CW)
        for c in range(KC):
            B.op("dve", lambda e, c=c, rstd=rstd: e.scalar_tensor_tensor(
                out=hnc[:, c, :], in0=hx[:, c, :], scalar=gcol[:, c:c + 1], in1=rstd[:, 0:CW],
                op0=ALU.mult, op1=ALU.mult), reads=[hx, gcol, rstd], writes=[hnc])
        hts = [hnc]

        def proj(col0, m, dst_fn, dst_tile, part=None):
            ps = npj()
            wtile = wt[col0 // 256]
            co = col0 % 256
            for kc in range(KC):
                B.op("pe", lambda e, kc=kc, ps=ps, wtile=wtile, co=co, m=m: e.matmul(
                    ps[0:m, 0:CW], wtile[:, kc, co:co + m], hnc[:, kc, :], start=(kc == 0), stop=(kc == KC - 1)),
                    reads=[wtile] + hts, writes=[ps])
            B.op("act", lambda e, ps=ps, m=m: e.activation(out=dst_fn(), in_=ps[0:m, 0:CW], func=AF.Copy),
                 reads=[ps], writes=[dst_tile])
        for arr, yt in enumerate((yr, yk, yv)):
            for h in range(NH):
                proj(arr * RW + h * 64, 64, lambda yt=yt, h=h: yt[p2][:, h, :], yt[p2])
        proj(3 * RW, 64, lambda: ywa[p2][:, 0, :], ywa[p2])
        proj(3 * RW + 64, 64, lambda: ywa[p2][:, 1, :], ywa[p2])
        proj(3 * RW + 128, 128, lambda: ygd[p2][:, :], ygd[p2])
        for gi in range(4):
            proj(1792 + gi * 128, 128, lambda gi=gi: yu[p2][:, gi, :], yu[p2])

        def shift(yt, nparts, lead, mu_b):
            d = dtm[0]
            dv = d[0:nparts, 0:lead, 0:CW - 1] if lead else None
            B.op("dve", lambda e: e.tensor_tensor(out=d[0:nparts, 0:lead, 0:CW - 1], in0=yt[0:nparts, :, 0:CW - 1],
                                                  in1=yt[0:nparts, :, 1:CW], op=ALU.subtract),
                 reads=[yt], writes=[d])
            B.op("dve", lambda e: e.tensor_tensor(out=d[0:nparts, 0:lead, 0:CW - 1], in0=d[0:nparts, 0:lead, 0:CW - 1],
                                                  in1=mu_b, op=ALU.mult), reads=[d, c64, c64b], writes=[d])
            B.op("dve", lambda e: e.tensor_tensor(out=yt[0:nparts, :, 1:CW], in0=yt[0:nparts, :, 1:CW],
                                                  in1=d[0:nparts, 0:lead, 0:CW - 1], op=ALU.add),
                 reads=[d, yt], writes=[yt])
        for arr, yt in enumerate((yr, yk, yv)):
            shift(yt[p2], 64, NH, bc(cc(arr), CW - 1))
        shift(ywa[p2], 64, 2, c64b[:, 0:2].unsqueeze(2).to_broadcast([64, 2, CW - 1]))
        B.op("pool", lambda e: e.tensor_tensor(out=ws_g[p2][:, 0:CW - 1], in0=ygd[p2][:, 0:CW - 1],
                                               in1=ygd[p2][:, 1:CW], op=ALU.subtract),
             reads=[ygd[p2]], writes=[ws_g[p2]]) if False else None
        R = yr[p2]
        Kx = yk[p2]
        Vx = yv[p2]
        own = slice(HALO, CW)
        B.op("act", lambda e: e.activation(out=wab[p2][:, 0, :], in_=ywa[p2][:, 0, own], func=AF.Tanh),
             reads=[ywa[p2]], writes=[wab[p2]])
        B.op("act", lambda e: e.activation(out=wab[p2][:, 1, :], in_=ywa[p2][:, 1, own], func=AF.Copy),
             reads=[ywa[p2], wab[p2]], writes=[wab[p2]])
        A_, B_, C_ = bufA[p2], bufB[p2], bufC[p2]
        for (srcrow, upw, bias_i, dst) in ((0, wupb, 6, A_), (1, aupb, 7, a_t[p2])):
            for h0 in (0, 4):
                ps = nsc()
                for h in range(h0, h0 + 4):
                    B.op("pe", lambda e, ps=ps, h=h, h0=h0, upw=upw, srcrow=srcrow: e.matmul(
                        ps[0:64, (h - h0) * CH:(h - h0 + 1) * CH], upw[:, h * 64:(h + 1) * 64], wab[p2][:, srcrow, :],
                        start=True, stop=True), reads=[upw, wab[p2]], writes=[ps])
                for h in range(h0, h0 + 4):
                    B.op("act", lambda e, ps=ps, h=h, h0=h0, dst=dst, bias_i=bias_i: e.activation(
                        out=dst[:, h, :], in_=ps[0:64, (h - h0) * CH:(h - h0 + 1) * CH], func=AF.Sigmoid,
                        bias=c64[:, bias_i * 8 + h:bias_i * 8 + h + 1]), reads=[ps, c64], writes=[dst])
        B.op("dve", lambda e: e.tensor_scalar(out=A_[:], in0=A_[:], scalar1=DECAY_SCALE, scalar2=None, op0=ALU.mult),
             reads=[A_], writes=[A_])
        B.op("dve", lambda e: e.tensor_tensor_scan(
            out=B_[:].rearrange("k h x -> k (h x)"), data0=scanmask[:].rearrange("k h x -> k (h x)"),
            data1=A_[:].rearrange("k h x -> k (h x)"), initial=0.0, op0=ALU.mult, op1=ALU.add),
            reads=[scanmask, A_], writes=[B_])
        B.op("dve", lambda e: e.tensor_tensor(out=A_[:], in0=B_[:], in1=A_[:], op=ALU.subtract),
             reads=[B_, A_], writes=[A_])
        B.op("act", lambda e: e.activation(out=A_[:], in_=A_[:], func=AF.Exp), reads=[A_], writes=[A_])
        B.op("act", lambda e: e.activation(out=C_[:], in_=B_[:], func=AF.Exp), reads=[B_], writes=[C_])
        B.op("act", lambda e: e.activation(out=B_[:], in_=B_[:], func=AF.Exp, scale=-1.0), reads=[B_], writes=[B_])
        kkx, tmp = kk_t[p2], tmp_t[0]
        B.op("dve", lambda e: e.tensor_tensor(out=kkx[:], in0=Kx[:, :, own], in1=bc(cc(3), CH), op=ALU.mult),
             reads=[Kx, c64], writes=[kkx])
        B.op("act", lambda e: e.activation(out=tmp[:], in_=kkx[:], func=AF.Square), reads=[kkx], writes=[tmp])
        for h0 in (0, 4):
            ps = nsc()
            B.op("pe", lambda e, ps=ps, h0=h0: e.matmul(
                ps[0:64, 0:4 * CH], ones32[:], tmp[:, h0:h0 + 4, :].rearrange("k h x -> k (h x)"),
                start=True, stop=True), reads=[ones32, tmp], writes=[ps])
            B.op("act", lambda e, ps=ps, h0=h0: e.activation(out=tmp[:, h0:h0 + 4, :].rearrange("k h x -> k (h x)"),
                                                             in_=ps[0:64, 0:4 * CH], func=AF.Sqrt, bias=tinyc[0:64, :]),
                 reads=[ps, tinyc, tmp], writes=[tmp])
        B.op("dve", lambda e: e.reciprocal(out=tmp[:], in_=tmp[:]), reads=[tmp], writes=[tmp])
        B.op("dve", lambda e: e.tensor_tensor(out=kkx[:], in0=kkx[:], in1=tmp[:], op=ALU.mult),
             reads=[kkx, tmp], writes=[kkx])
        kp, ax, bx, rk = kp_t[p2], a_t[p2], b_t[p2], rk_t[p2]
        B.op("dve", lambda e: e.tensor_tensor(out=kp[:], in0=ax[:], in1=bc(cc(4), CH), op=ALU.mult),
             reads=[ax, c64], writes=[kp])
        B.op("dve", lambda e: e.tensor_tensor(out=kp[:], in0=kp[:], in1=bc(c1[:, :], CH), op=ALU.add),
             reads=[kp, c1], writes=[kp])
        B.op("dve", lambda e: e.tensor_tensor(out=kp[:], in0=kp[:], in1=Kx[:, :, own], op=ALU.mult),
             reads=[kp, Kx], writes=[kp])
        B.op("dve", lambda e: e.tensor_tensor(out=bx[:], in0=kkx[:], in1=ax[:], op=ALU.mult),
             reads=[kkx, ax], writes=[bx])
        B.op("dve", lambda e: e.tensor_tensor(out=rk[:], in0=R[:, :, own], in1=bc(cc(5), CH), op=ALU.mult),
             reads=[R, c64], writes=[rk])
        B.op("dve", lambda e: e.tensor_tensor(out=rk[:], in0=rk[:], in1=kp[:], op=ALU.mult),
             reads=[rk, kp], writes=[rk])
        ar, bt, kt, bw, kw = AR[p2], Bt[p2], Kt[p2], BW[p2], KW[p2]
        B.op("dve", lambda e: e.scalar_tensor_tensor(out=ar[:, :, 0, :], in0=kkx[:], scalar=-1.0, in1=A_[:],
                                                     op0=ALU.mult, op1=ALU.mult), reads=[kkx, A_], writes=[ar])
        B.op("dve", lambda e: e.tensor_tensor(out=ar[:, :, 1, :], in0=R[:, :, own], in1=C_[:], op=ALU.mult),
             reads=[R, C_, ar], writes=[ar])
        B.op("dve", lambda e: e.tensor_tensor(out=bt[:], in0=bx[:], in1=B_[:], op=ALU.mult), reads=[bx, B_], writes=[bt])
        B.op("dve", lambda e: e.tensor_tensor(out=kt[:], in0=kp[:], in1=B_[:], op=ALU.mult), reads=[kp, B_], writes=[kt])
        wc_b = C_[:, :, CH - 1:CH].to_broadcast([64, NH, CH])
        B.op("dve", lambda e: e.tensor_tensor(out=bw[:], in0=bt[:], in1=wc_b, op=ALU.mult), reads=[bt, C_], writes=[bw])
        B.op("dve", lambda e: e.tensor_tensor(out=kw[:], in0=kt[:], in1=wc_b, op=ALU.mult), reads=[kt, C_], writes=[kw])

        tm = TM[p2]
        for hp in range(NH // 2):
            ps = nsc()
            for hh in range(2):
                h = hp * 2 + hh
                srcs = (ar[:, h, 0, :], Vx[:, h, own], bw[:, h, :], kw[:, h, :])
                for si, s_ap in enumerate(srcs):
                    c0 = hh * 256 + si * 64
                    B.op("pe", lambda e, ps=ps, s_ap=s_ap, c0=c0: e.transpose(ps[0:CH, c0:c0 + 64], s_ap, ident[0:64, 0:64]),
                         reads=[ar, Vx, bw, kw, ident], writes=[ps])
            B.op("act", lambda e, ps=ps, hp=hp: e.activation(
                out=tm[:, 2 * hp:2 * hp + 2, :], in_=ps[0:CH, 0:512].rearrange("c (h x) -> c h x", h=2), func=AF.Copy),
                reads=[ps], writes=[tm])
        scm = SCm[p2]
        for h in range(NH):
            ps = nsc()
            B.op("pe", lambda e, ps=ps, h=h: e.matmul(ps[0:CH, 0:2 * CH], bt[:, h, :], ar[:, h, :, :].rearrange("k a x -> k (a x)"),
                                                      start=True, stop=True), reads=[bt, ar], writes=[ps])
            B.op("pe", lambda e, ps=ps, h=h: e.matmul(ps[0:CH, 2 * CH:4 * CH], kt[:, h, :], ar[:, h, :, :].rearrange("k a x -> k (a x)"),
                                                      start=True, stop=True), reads=[kt, ar], writes=[ps])
            B.op("pe", lambda e, ps=ps, h=h: e.matmul(ps[0:CH, 4 * CH:5 * CH], ar[:, h, 0, :], bt[:, h, :],
                                                      start=True, stop=True), reads=[bt, ar], writes=[ps])
            B.op("dve", lambda e, ps=ps, h=h: e.tensor_tensor(out=scm[:, h, :], in0=ps[0:CH, 0:5 * CH], in1=msk[:],
                                                              op=ALU.mult), reads=[ps, msk], writes=[scm])
        y = Y[p2]
        B.op("pool", lambda e: e.tensor_copy(out=y[:, :, 0:64], in_=tm[:, :, 0:64]), reads=[tm], writes=[y])
        ps = nsc()
        for h in range(NH):
            B.op("pe", lambda e, ps=ps, h=h: e.matmul(ps[0:CH, h * 64:(h + 1) * 64], scm[:, h, 2 * CH:3 * CH], tm[:, h, 64:128],
                                                      start=True, stop=True), reads=[scm, tm], writes=[ps])
        B.op("act", lambda e, ps=ps: e.activation(out=y[:, :, 64:128], in_=ps[0:CH, 0:512].rearrange("c (h x) -> c h x", h=NH),
                                                  func=AF.Copy), reads=[ps, y], writes=[y])
        pcur = None
        for lev in range(NLEV):
            def Pm_(h, lev=lev, pcur=pcur):
                return scm[:, h, 0:CH] if lev == 0 else pcur[:, h, 0, :]

            def PmT_(h, lev=lev, pcur=pcur):
                return scm[:, h, 4 * CH:5 * CH] if lev == 0 else pcur[:, h, 1, :]
            ptile = scm if lev == 0 else pcur
            for h0 in (0, 4):
                ps = nsc()
                for h in range(h0, h0 + 4):
                    B.op("pe", lambda e, ps=ps, h=h, h0=h0, Pm_=Pm_: e.matmul(
                        ps[0:CH, (h - h0) * 128:(h - h0 + 1) * 128], Pm_(h), y[:, h, :], start=True, stop=True),
                        reads=[ptile, y], writes=[ps])
                B.op("dve", lambda e, ps=ps, h0=h0: e.tensor_tensor(
                    out=y[:, h0:h0 + 4, :], in0=y[:, h0:h0 + 4, :],
                    in1=ps[0:CH, 0:512].rearrange("c (h x) -> c h x", h=4), op=ALU.add),
                    reads=[ps, y], writes=[y])
            if lev < NLEV - 1:
                pn = Pp[lev % 2]
                for h0 in (0, 2, 4, 6):
                    ps = nsc()
                    for h in range(h0, h0 + 2):
                        o0 = (h - h0) * 2 * CH
                        B.op("pe", lambda e, ps=ps, h=h, o0=o0, Pm_=Pm_, PmT_=PmT_: e.matmul(
                            ps[0:CH, o0:o0 + CH], PmT_(h), Pm_(h), start=True, stop=True),
                            reads=[ptile], writes=[ps])
                        B.op("pe", lambda e, ps=ps, h=h, o0=o0, Pm_=Pm_, PmT_=PmT_: e.matmul(
                            ps[0:CH, o0 + CH:o0 + 2 * CH], Pm_(h), PmT_(h), start=True, stop=True),
                            reads=[ptile], writes=[ps])
                    B.op("act", lambda e, ps=ps, h0=h0, pn=pn: e.activation(
                        out=pn[:, h0:h0 + 2, :, :].rearrange("c h a x -> c h (a x)"),
                        in_=ps[0:CH, 0:4 * CH].rearrange("c (h x) -> c h x", h=2), func=AF.Copy),
                        reads=[ps], writes=[pn])
                pcur = pn
        ahf = AF_t[p2]
        for h0 in (0, 4):
            ps = nsc()
            for h in range(h0, h0 + 4):
                B.op("pe", lambda e, ps=ps, h=h, h0=h0: e.transpose(ps[0:64, (h - h0) * CH:(h - h0 + 1) * CH],
                                                                     y[:, h, 0:64], ident[0:CH, 0:CH]),
                     reads=[y, ident], writes=[ps])
            B.op("act", lambda e, ps=ps, h0=h0: e.activation(
                out=ahf[:, h0:h0 + 4, :], in_=ps[0:64, 0:4 * CH].rearrange("k (h x) -> k h x", h=4), func=AF.Copy),
                reads=[ps], writes=[ahf])
        gp = Gp[p2]
        ps = nsc()
        for h in range(NH):
            B.op("pe", lambda e, ps=ps, h=h: e.matmul(ps[0:64, h * 64:(h + 1) * 64], y[:, h, 0:64], tm[:, h, 128:192],
                                                      start=True, stop=True), reads=[y, tm], writes=[ps])
        B.op("act", lambda e, ps=ps: e.activation(out=gp[:].rearrange("k h x -> k (h x)"), in_=ps[0:64, 0:512], func=AF.Copy),
             reads=[ps], writes=[gp])
        So, Sn = S[ci % 2], S[(ci + 1) % 2]
        if not aug:
            ut, ot = UT[p2], OT[p2]
            ps = nsc()
            for h in range(NH):
                B.op("pe", lambda e, ps=ps, h=h, So=So: e.matmul(ps[0:CH, h * 64:(h + 1) * 64], ahf[:, h, :], So[:, h, :],
                                                          start=True, stop=True), reads=[ahf, So], writes=[ps])
            B.op("dve", lambda e, ps=ps: e.tensor_tensor(out=ut[:], in0=ps[0:CH, 0:512].rearrange("c (h x) -> c h x", h=NH),
                                                         in1=y[:, :, 64:128], op=ALU.add), reads=[ps, y], writes=[ut])
            ps = nsc()
            for h in range(NH):
                o = ps[0:CH, h * 64:(h + 1) * 64]
                B.op("pe", lambda e, o=o, h=h, So=So: e.matmul(o, ar[:, h, 1, :], So[:, h, :], start=True, stop=False),
                     reads=[ar, So], writes=[ps])
                B.op("pe", lambda e, o=o, h=h: e.matmul(o, scm[:, h, CH:2 * CH], ut[:, h, :], start=False, stop=False),
                     reads=[scm, ut], writes=[ps])
                B.op("pe", lambda e, o=o, h=h: e.matmul(o, scm[:, h, 3 * CH:4 * CH], tm[:, h, 64:128], start=False, stop=True),
                     reads=[scm, tm], writes=[ps])
            psO = ps
            ps = nsc()
            for h in range(NH):
                B.op("pe", lambda e, ps=ps, h=h: e.matmul(ps[0:CH, h:h + 1], rk[:, h, :], ones32[:, 0:1], start=True, stop=True),
                     reads=[rk, ones32], writes=[ps])
            B.op("act", lambda e, ps=ps: e.activation(out=sT[p2][:], in_=ps[0:CH, 0:NH], func=AF.Copy), reads=[ps], writes=[sT[p2]])
            s_ = st[p2]
            B.op("act", lambda e, psO=psO: e.activation(out=ot[:].rearrange("c h x -> c (h x)"), in_=psO[0:CH, 0:512], func=AF.Copy),
                 reads=[psO], writes=[ot])
            B.op("act", lambda e: e.activation(out=osq[:], in_=ot[:], func=AF.Square), reads=[ot], writes=[osq])
            B.op("dve", lambda e: e.reduce_sum(out=s_[:, 0:8], in_=ot[:], axis=AX.X), reads=[ot], writes=[s_])
            B.op("dve", lambda e: e.reduce_sum(out=s_[:, 8:16], in_=osq[:], axis=AX.X), reads=[osq, s_], writes=[s_])
            B.op("dve", lambda e: e.tensor_scalar(out=s_[:, 16:24], in0=s_[:, 0:8], scalar1=1.0 / 64, scalar2=None, op0=ALU.mult),
                 reads=[s_], writes=[s_])
            B.op("dve", lambda e: e.tensor_tensor(out=s_[:, 0:8], in0=s_[:, 16:24], in1=s_[:, 16:24], op=ALU.mult),
                 reads=[s_], writes=[s_])
            B.op("dve", lambda e: e.scalar_tensor_tensor(out=s_[:, 8:16], in0=s_[:, 8:16], scalar=1.0 / 64, in1=s_[:, 0:8],
                                                         op0=ALU.mult, op1=ALU.subtract), reads=[s_], writes=[s_])
            B.op("act", lambda e: e.activation(out=s_[:, 24:32], in_=s_[:, 8:16], func=AF.Sqrt, bias=gnc[0:CH, :]),
                 reads=[s_, gnc], writes=[s_])
            B.op("dve", lambda e: e.reciprocal(out=s_[:, 24:32], in_=s_[:, 24:32]), reads=[s_], writes=[s_])
            B.op("dve", lambda e: e.tensor_tensor(out=ot[:], in0=ot[:], in1=s_[:, 16:24].unsqueeze(2).to_broadcast([CH, NH, 64]),
                                                  op=ALU.subtract), reads=[ot, s_], writes=[ot])
            B.op("dve", lambda e: e.tensor_tensor(out=ot[:], in0=ot[:], in1=s_[:, 24:32].unsqueeze(2).to_broadcast([CH, NH, 64]),
                                                  op=ALU.mult), reads=[ot, s_], writes=[ot])
            otf = lambda: ot[:].rearrange("c h x -> c (h x)")
            B.op("dve", lambda e: e.tensor_tensor(out=otf(), in0=otf(), in1=lnw[0:CH, :], op=ALU.mult), reads=[ot, lnw], writes=[ot])
            B.op("dve", lambda e: e.tensor_tensor(out=otf(), in0=otf(), in1=lnb[0:CH, :], op=ALU.add), reads=[ot, lnb], writes=[ot])
            B.op("dve", lambda e: e.tensor_tensor(out=osq[:], in0=tm[:, :, 64:128],
                                                  in1=sT[p2][:, :].unsqueeze(2).to_broadcast([CH, NH, 64]), op=ALU.mult),
                 reads=[tm, sT[p2], osq], writes=[osq])
            B.op("dve", lambda e: e.tensor_tensor(out=ot[:], in0=ot[:], in1=osq[:], op=ALU.add), reads=[ot, osq], writes=[ot])
            gdt = ygd[p2]
            B.op("pool", lambda e: e.tensor_tensor(out=ws1[p2][:, 0, 0:CW - 1], in0=gdt[:, 0:CW - 1], in1=gdt[:, 1:CW], op=ALU.subtract),
                 reads=[gdt], writes=[ws1[p2]])
            B.op("dve", lambda e: e.scalar_tensor_tensor(out=gdt[:, 1:CW], in0=ws1[p2][:, 0, 0:CW - 1], scalar=c128[:, 0:1],
                                                         in1=gdt[:, 1:CW], op0=ALU.mult, op1=ALU.add),
                 reads=[ws1[p2], c128, gdt], writes=[gdt])
            B.op("act", lambda e: e.activation(out=gdb[p2][:], in_=gdt[:, own], func=AF.Sigmoid), reads=[gdt], writes=[gdb[p2]])
            ps = nsc()
            for pr in range(4):
                B.op("pe", lambda e, ps=ps, pr=pr: e.matmul(ps[:, pr * CH:(pr + 1) * CH], gupb[:, pr * 128:(pr + 1) * 128], gdb[p2][:],
                                                            start=True, stop=True), reads=[gupb, gdb[p2]], writes=[ps])
            B.op("act", lambda e, ps=ps: e.activation(out=gg[p2][:].rearrange("p a x -> p (a x)"), in_=ps[:, 0:4 * CH], func=AF.Copy),
                 reads=[ps], writes=[gg[p2]])
            ps = nsc()
            for pr in range(4):
                B.op("pe", lambda e, ps=ps, pr=pr: e.transpose(ps[:, pr * CH:(pr + 1) * CH],
                                                                ot[:, 2 * pr:2 * pr + 2, :].rearrange("c h x -> c (h x)"), ident[0:CH, 0:CH]),
                     reads=[ot, ident], writes=[ps])
            mot = mo[p2]
            B.op("dve", lambda e, ps=ps: e.tensor_tensor(out=mot[:, 0:4, :], in0=ps[:, 0:4 * CH].rearrange("p (a x) -> p a x", a=4),
                                                         in1=gg[p2][:], op=ALU.mult), reads=[ps, gg[p2]], writes=[mot])
            u = yu[p2]
            w1_, w2_ = ws1[p2], ws2[p2]
            B.op("pool", lambda e: e.tensor_tensor(out=w1_[:, :, 1:CW], in0=u[:, :, 1:CW], in1=u[:, :, 0:CW - 1], op=ALU.add),
                 reads=[u, w1_], writes=[w1_])
            B.op("pool", lambda e: e.tensor_tensor(out=w2_[:, 1:4, 3:CW], in0=w1_[:, 1:4, 3:CW], in1=w1_[:, 1:4, 1:CW - 2], op=ALU.add),
                 reads=[w1_], writes=[w2_])
            B.op("pool", lambda e: e.tensor_copy(out=w2_[:, 0, 1:CW], in_=w1_[:, 0, 1:CW]), reads=[w1_, w2_], writes=[w2_])
            B.op("pool", lambda e: e.tensor_tensor(out=w1_[:, 2:4, 7:CW], in0=w2_[:, 2:4, 7:CW], in1=w2_[:, 2:4, 3:CW - 4], op=ALU.add),
                 reads=[w2_, w1_], writes=[w1_])
            B.op("pool", lambda e: e.tensor_copy(out=w2_[:, 2, 7:CW], in_=w1_[:, 2, 7:CW]), reads=[w1_, w2_], writes=[w2_])
            B.op("pool", lambda e: e.tensor_tensor(out=w2_[:, 3, 15:CW], in0=w1_[:, 3, 15:CW], in1=w1_[:, 3, 7:CW - 8], op=ALU.add),
                 reads=[w1_, w2_], writes=[w2_])
            for gi in range(4):
                B.op("dve", lambda e, gi=gi: e.scalar_tensor_tensor(
                    out=w2_[:, gi, own], in0=w2_[:, gi, own], scalar=1.0 / (2 << gi), in1=u[:, gi, own],
                    op0=ALU.mult, op1=ALU.subtract), reads=[w2_, u], writes=[w2_])
            if ci == 0:
                B.op("pool", lambda e: e.tensor_tensor(out=w1_[:, :, 0:HALO], in0=w2_[:, :, HALO:2 * HALO], in1=u[:, :, HALO:2 * HALO],
                                                       op=ALU.add), reads=[w2_, u, w1_], writes=[w1_])
                B.op("pool", lambda e: e.tensor_tensor(out=w1_[:, :, 0:HALO], in0=w1_[:, :, 0:HALO], in1=corr[:], op=ALU.mult),
                     reads=[w1_, corr], writes=[w1_])
                B.op("pool", lambda e: e.tensor_tensor(out=w2_[:, :, HALO:2 * HALO], in0=w1_[:, :, 0:HALO], in1=u[:, :, HALO:2 * HALO],
                                                       op=ALU.subtract), reads=[w1_, u, w2_], writes=[w2_])
            B.op("pool", lambda e: e.tensor_copy(out=dpb[p2][:], in_=w2_[:, :, own]), reads=[w2_], writes=[dpb[p2]])
            ps = nsc()
            for gi in range(4):
                B.op("pe", lambda e, ps=ps, gi=gi: e.matmul(ps[:, gi * CH:(gi + 1) * CH], pwb[:, gi, :], dpb[p2][:, gi, :],
                                                            start=True, stop=True), reads=[pwb, dpb[p2]], writes=[ps])
            for gi in range(4):
                B.op("act", lambda e, ps=ps, gi=gi: e.activation(out=mot[:, 4 + gi, :], in_=ps[:, gi * CH:(gi + 1) * CH], func=AF.Identity,
                                                                 scale=c128[:, 5 + gi:6 + gi]), reads=[ps, c128, mot], writes=[mot])
            if DEBUG:
                B.dma("sp", dbg_d[:, :, ci * CH:(ci + 1) * CH].rearrange("c p t -> p c t"), mot[:], reads=[mot])
            mb = mblk[p2]
            hc = hck[0]
            for m in range(KC):
                ps = npj()
                wtile = wot[(m * 128) // 512]
                mo_ = (m * 128) % 512
                for kc in range(KC):
                    B.op("pe", lambda e, ps=ps, kc=kc, wtile=wtile, mo_=mo_: e.matmul(
                        ps[:, 0:CH], wtile[:, kc, mo_:mo_ + 128], mot[:, kc, :], start=(kc == 0), stop=(kc == KC - 1)),
                        reads=[wtile, mot], writes=[ps])
                B.op("act", lambda e, ps=ps, m=m: e.activation(out=mb[:, m, :], in_=ps[:, 0:CH], func=AF.Copy),
                     reads=[ps], writes=[mb])
            rstd = emit_rstd(P, lambda c: mb[:, c, :], [mb], width=CH)
            for c in range(KC):
                B.op("dve", lambda e, c=c, rstd=rstd: e.scalar_tensor_tensor(
                    out=mb[:, c, :], in0=mb[:, c, :], scalar=gcol[:, KC + c:KC + c + 1], in1=rstd[:, 0:CH],
                    op0=ALU.mult, op1=ALU.mult), reads=[mb, gcol, rstd], writes=[mb])
            B.op("pool", lambda e: e.tensor_tensor(out=hc[:], in0=hx[:, :, HALO:CW], in1=mb[:], op=ALU.add), reads=[hx, mb, hc], writes=[hc])
            B.dma("sp", o_d[:, :, ci * CH:(ci + 1) * CH].rearrange("c p t -> p c t"), hc[:], reads=[hc])
        halves = (0, 64) if aug else (0,)
        for h in range(NH):
            o = pst[0:64, h * 64:(h + 1) * 64]
            B.op("pe", lambda e, o=o, h=h: e.matmul(o, tm[:, h, 128:192], y[:, h, 64:128], start=True, stop=False),
                 reads=[tm, y], writes=[pst])
            B.op("pe", lambda e, o=o, h=h: e.matmul(o, tm[:, h, 192:256], tm[:, h, 64:128], start=False, stop=False),
                 reads=[tm], writes=[pst])
            B.op("pe", lambda e, o=o, h=h, So=So: e.matmul(o, gp[:, h, :], So[:, h, SW - 64:SW], start=False, stop=True),
                 reads=[gp, So], writes=[pst])
        wcb = C_[:, :, CH - 1:CH].to_broadcast([64, NH, 64])
        B.op("dve", lambda e, So=So, Sn=Sn: e.tensor_tensor(out=Sn[:, :, SW - 64:SW], in0=So[:, :, SW - 64:SW], in1=wcb, op=ALU.mult),
             reads=[So, C_, Sn], writes=[Sn])
        B.op("dve", lambda e, Sn=Sn: e.tensor_tensor(out=Sn[:, :, SW - 64:SW], in0=Sn[:, :, SW - 64:SW],
                                              in1=pst[0:64, 0:512].rearrange("k (h x) -> k h x", h=NH), op=ALU.add),
             reads=[Sn, pst], writes=[Sn])
        if aug:
            ps = nsc()
            for h in range(NH):
                B.op("pe", lambda e, ps=ps, h=h, So=So: e.matmul(ps[0:64, h * 64:(h + 1) * 64], gp[:, h, :], So[:, h, 0:64],
                                                          start=True, stop=True), reads=[gp, So], writes=[ps])
            B.op("dve", lambda e, So=So, Sn=Sn: e.tensor_tensor(out=Sn[:, :, 0:64], in0=So[:, :, 0:64], in1=wcb, op=ALU.mult),
                 reads=[So, C_, Sn], writes=[Sn])
            B.op("dve", lambda e, ps=ps, Sn=Sn: e.tensor_tensor(out=Sn[:, :, 0:64], in0=Sn[:, :, 0:64],
                                                         in1=ps[0:64, 0:512].rearrange("k (h x) -> k h x", h=NH), op=ALU.add),
                 reads=[Sn, ps], writes=[Sn])
    if aug:
        B.dma("sp", sum_d, S[NCH % 2][:], reads=[S[NCH % 2]])
    return P.finish()


def even_consts(p):
    def h64(v):
        return v.reshape(NH, 64).T
    mu = p["mu"]
    c64 = np.concatenate([h64(mu[0:512]), h64(mu[512:1024]), h64(mu[1024:1536]), h64(p["k_k"]), h64(p["k_a"]),
                          h64(p["r_k"].reshape(-1)), h64(p["w0"]), h64(p["a0"])], axis=1)
    c64b = np.stack([mu[1536:1600], mu[1600:1664]], axis=1)
    c128 = np.zeros((128, 9), np.float32)
    c128[:, 0] = mu[1664:1792]
    c128[:, 5:9] = p["pool_scale"].reshape(4, 128).T
    ln = np.stack([np.broadcast_to(p["ln_w"][None], (128, RW)), np.broadcast_to(p["ln_b"][None], (128, RW))], 0)
    r = np.arange(CH)
    su = (r[None, :] > r[:, None]).astype(np.float32)
    ui = (r[None, :] >= r[:, None]).astype(np.float32)
    sl = (r[None, :] < r[:, None]).astype(np.float32)
    msk = np.concatenate([su, ui, su, ui, sl], axis=1)
    out = dict(c64=c64, c64b=c64b, c128=c128, wup=p["w_up"], aup=p["a_up"], gup=p["g_up"], pw=p["pool_w"],
               ln=ln, msk=msk, ident=np.eye(128, dtype=np.float32))
    return {k: np.ascontiguousarray(v, dtype=np.float32) for k, v in out.items()}


def pool_corr(core):
    corr = np.ones((128, 4, HALO), np.float32)
    if core == 0:
        for gi in range(4):
            win = 2 << gi
            t = np.arange(HALO)
            corr[:, gi, :] = (win / np.minimum(t + 1, win)).astype(np.float32)[None]
    return corr


def run_even_mixer(h_shards, p, w_in, w_out, g0, g1):
    cst = even_consts(p)
    g = gcols(g0, g1)
    base = []
    for i in range(NCORES):
        hh = np.zeros((KC, 128, HALO), np.float32) if i == 0 else np.ascontiguousarray(h_shards[i - 1][:, :, T - HALO:])
        m = {"h": h_shards[i], "hh": hh, "w": w_in, "g": g, "corr": pool_corr(i)}
        m.update(cst)
        base.append(m)
    r1 = _run("e1", lambda: build_even(True), base)
    sums = np.ascontiguousarray(np.stack([r1[i]["summ"] for i in range(NCORES)], 0))
    in2 = []
    for i in range(NCORES):
        m = dict(base[i])
        oh = np.zeros((64, NCORES), np.float32)
        oh[:, i] = 1.0
        m.update({"wo": w_out, "sums": sums, "onehot": oh})
        in2.append(m)
    r2 = _run("e2", lambda: build_even(False), in2)
    if DEBUG:
        global _DBG
        _DBG = [r["dbg"] for r in r2]
    return [r["ho"] for r in r2]


def kernel(x, meta, norm_g, mlp_w1, mlp_w2, ev_w_in, ev_mu, ev_w0, ev_w_up, ev_a0, ev_a_up, ev_g_up,
           ev_k_k, ev_k_a, ev_r_k, ev_ln_w, ev_ln_b, ev_pool_w, ev_pool_scale, ev_w_out,
           od_w_in, od_lambda, od_subln_w, od_w_out):
    f = lambda a: np.ascontiguousarray(np.asarray(a), dtype=np.float32)
    x = f(x)
    h = np.concatenate([f(meta), x[0]], axis=0)
    shards = [to_fm(h[c * T:(c + 1) * T]) for c in range(NCORES)]
    norm_g = f(norm_g)
    for i in range(4):
        j = i // 2
        g = norm_g[i]
        if i % 2 == 0:
            p = dict(mu=f(ev_mu[j]), w0=f(ev_w0[j]), w_up=f(ev_w_up[j]), a0=f(ev_a0[j]), a_up=f(ev_a_up[j]),
                     g_up=f(ev_g_up[j]), k_k=f(ev_k_k[j]), k_a=f(ev_k_a[j]), r_k=f(ev_r_k[j]), ln_w=f(ev_ln_w[j]),
                     ln_b=f(ev_ln_b[j]), pool_w=f(ev_pool_w[j]), pool_scale=f(ev_pool_scale[j]))
            shards = run_even_mixer(shards, p, f(ev_w_in[j]), f(ev_w_out[j]), g[0], g[1])
        else:
            shards = run_odd_mixer(shards, f(od_w_in[j]), f(od_lambda[j]), f(od_subln_w[j]), f(od_w_out[j]),
                                   g[0], g[1], i)
        shards = run_mlp(shards, f(mlp_w1[i]), f(mlp_w2[i]), g[2], g[3])
    hfull = np.concatenate([from_fm(s) for s in shards], axis=0)
    return np.ascontiguousarray(hfull[16:][None]).astype(np.float32)
```
